# Optimizing a Trainium2 kernel written in Bass

```python
import math
import jax, jax.numpy as jnp
from jax import lax
import numpy as np

D_MODEL = 4096
BATCH = 2
SEQ = 8192
DEPTH = 2

N_MIXERS = 2
N_META = 16
BLOCK_Q = 128
DA_HEADS = 16
DA_HEAD_DIM = D_MODEL // (2 * DA_HEADS)
ROPE_THETA = 10000.0
SSM_GROUP = 16
SSM_GROUPS = D_MODEL // SSM_GROUP
SSM_STATE = 64
DT_MIN = 0.001
DT_MAX = 0.1
D_FF = 4 * D_MODEL
EPS = 1e-6
N_ATTN_LAYERS = (DEPTH + N_MIXERS - 1) // N_MIXERS
N_SSM_LAYERS = DEPTH // N_MIXERS

kernel_name = "hybrid_diffattn_s5_sqrelu"


def lambda_init(layer_idx):
    return 0.8 - 0.6 * math.exp(-0.3 * layer_idx)


def rmsnorm(x, g):
    xf = x.astype(jnp.float32)
    y = xf * lax.rsqrt(jnp.mean(xf * xf, axis=-1, keepdims=True) + EPS)
    return (y * g.astype(jnp.float32)).astype(x.dtype)


def rope_tables(length):
    inv = ROPE_THETA ** (-jnp.arange(0, DA_HEAD_DIM, 2, dtype=jnp.float32) / DA_HEAD_DIM)
    ang = jnp.arange(length, dtype=jnp.float32)[:, None] * inv[None, :]
    return jnp.cos(ang), jnp.sin(ang)


def apply_rope(x, cos, sin):
    xf = x.astype(jnp.float32)
    x1, x2 = jnp.split(xf, 2, axis=-1)
    c = cos[None, :, None, :]
    s = sin[None, :, None, :]
    return jnp.concatenate([x1 * c - x2 * s, x2 * c + x1 * s], axis=-1).astype(x.dtype)


def diff_attention(u, w_qkv, q_g, k_g, lam_vecs, subln_g, w_o, cos, sin, lam_init):
    bsz, length, _ = u.shape
    H, dh = DA_HEADS, DA_HEAD_DIM
    nb = length // BLOCK_Q
    qkv = u @ w_qkv
    q, k, v = jnp.split(qkv, 3, axis=-1)
    q = q.reshape(bsz, length, 2 * H, dh)
    k = k.reshape(bsz, length, 2 * H, dh)
    v = v.reshape(bsz, length, H, 2 * dh)
    q = apply_rope(rmsnorm(q, q_g), cos, sin) * (dh ** -0.5)
    k = apply_rope(rmsnorm(k, k_g), cos, sin)
    q = q.reshape(bsz, nb, BLOCK_Q, H, 2, dh).transpose(1, 0, 2, 3, 4, 5)
    k = k.reshape(bsz, length, H, 2, dh)
    lv = lam_vecs.astype(jnp.float32)
    lam = jnp.exp(jnp.sum(lv[0] * lv[1])) - jnp.exp(jnp.sum(lv[2] * lv[3])) + lam_init
    kpos = jnp.arange(length)

    def block(args):
        qb, bi = args
        s = jnp.einsum('bqhcd,bkhcd->bhcqk', qb, k, preferred_element_type=jnp.float32)
        qpos = bi * BLOCK_Q + jnp.arange(BLOCK_Q)
        mask = kpos[None, :] <= qpos[:, None]
        p = jax.nn.softmax(jnp.where(mask, s, -jnp.inf), axis=-1)
        a = p[:, :, 0] - lam * p[:, :, 1]
        return jnp.einsum('bhqk,bkhe->bqhe', a.astype(v.dtype), v)

    o = lax.map(block, (q, jnp.arange(nb)))
    o = o.transpose(1, 0, 2, 3, 4).reshape(bsz, length, H, 2 * dh)
    o = rmsnorm(o, subln_g) * (1.0 - lam_init)
    return o.reshape(bsz, length, D_MODEL) @ w_o


def s5_mixer(u, a_re, a_im, log_dt, b_re, b_im, c_re, c_im, d_skip, w_glu, b_glu):
    bsz, length, _ = u.shape
    G, P = SSM_GROUPS, SSM_STATE
    uf = u.astype(jnp.float32).reshape(bsz, length, G, SSM_GROUP)
    dt = jnp.exp(log_dt.astype(jnp.float32))[:, None]
    ar = a_re.astype(jnp.float32)
    ai = a_im.astype(jnp.float32)
    mag = jnp.exp(dt * ar)
    ang = dt * ai
    abar_re = mag * jnp.cos(ang)
    abar_im = mag * jnp.sin(ang)
    nr = abar_re - 1.0
    ni = abar_im
    den = ar * ar + ai * ai
    f_re = (nr * ar + ni * ai) / den
    f_im = (ni * ar - nr * ai) / den
    br = b_re.astype(jnp.float32)
    bi = b_im.astype(jnp.float32)
    bb_re = f_re[..., None] * br - f_im[..., None] * bi
    bb_im = f_re[..., None] * bi + f_im[..., None] * br
    bu_re = jnp.einsum('blgh,gph->blgp', uf, bb_re)
    bu_im = jnp.einsum('blgh,gph->blgp', uf, bb_im)
    a_r = jnp.broadcast_to(abar_re, (1, length, G, P))
    a_i = jnp.broadcast_to(abar_im, (1, length, G, P))

    def combine(e1, e2):
        a1r, a1i, b1r, b1i = e1
        a2r, a2i, b2r, b2i = e2
        return (a1r * a2r - a1i * a2i,
                a1r * a2i + a1i * a2r,
                a2r * b1r - a2i * b1i + b2r,
                a2r * b1i + a2i * b1r + b2i)

    _, _, x_re, x_im = lax.associative_scan(combine, (a_r, a_i, bu_re, bu_im), axis=1)
    y = (jnp.einsum('blgp,ghp->blgh', x_re, c_re.astype(jnp.float32))
         - jnp.einsum('blgp,ghp->blgh', x_im, c_im.astype(jnp.float32)))
    y = y.reshape(bsz, length, D_MODEL) + d_skip.astype(jnp.float32) * uf.reshape(bsz, length, D_MODEL)
    z = jax.nn.gelu(y).astype(u.dtype)
    gl = z @ w_glu + b_glu
    val, gate = jnp.split(gl, 2, axis=-1)
    return val * jax.nn.sigmoid(gate)


def sqrelu_mlp(u, w_up, w_down):
    return jnp.square(jax.nn.relu(u @ w_up)) @ w_down


def setup_inputs(seed: int = 0) -> dict:
    key = jax.random.key(seed)
    ks = jax.random.split(key, 24)
    D, NA, NS = D_MODEL, N_ATTN_LAYERS, N_SSM_LAYERS
    G, P, GS, dh = SSM_GROUPS, SSM_STATE, SSM_GROUP, DA_HEAD_DIM
    f32 = jnp.float32
    nrm = lambda k, s: jax.random.normal(k, s, f32)
    return {
        "x": nrm(ks[0], (BATCH, SEQ, D)),
        "meta_tokens": nrm(ks[1], (N_META, D)),
        "norm_mix_g": 1.0 + 0.02 * nrm(ks[2], (DEPTH, D)),
        "norm_mlp_g": 1.0 + 0.02 * nrm(ks[3], (DEPTH, D)),
        "da_w_qkv": nrm(ks[4], (NA, D, 3 * D)) * D ** -0.5,
        "da_q_norm_g": 1.0 + 0.02 * nrm(ks[5], (NA, dh)),
        "da_k_norm_g": 1.0 + 0.02 * nrm(ks[6], (NA, dh)),
        "da_lambda": 0.1 * nrm(ks[7], (NA, 4, dh)),
        "da_subln_g": 1.0 + 0.02 * nrm(ks[8], (NA, 2 * dh)),
        "da_w_o": nrm(ks[9], (NA, D, D)) * D ** -0.5,
        "ssm_a_re": -0.5 + 0.01 * nrm(ks[10], (NS, G, P)),
        "ssm_a_im": math.pi * jnp.arange(P, dtype=f32)[None, None, :] + 0.01 * nrm(ks[11], (NS, G, P)),
        "ssm_log_dt": jax.random.uniform(ks[12], (NS, G), f32, math.log(DT_MIN), math.log(DT_MAX)),
        "ssm_b_re": nrm(ks[13], (NS, G, P, GS)) * (2 * GS) ** -0.5,
        "ssm_b_im": nrm(ks[14], (NS, G, P, GS)) * (2 * GS) ** -0.5,
        "ssm_c_re": nrm(ks[15], (NS, G, GS, P)) * P ** -0.5,
        "ssm_c_im": nrm(ks[16], (NS, G, GS, P)) * P ** -0.5,
        "ssm_d": nrm(ks[17], (NS, D)),
        "ssm_w_glu": nrm(ks[18], (NS, D, 2 * D)) * D ** -0.5,
        "ssm_b_glu": 0.01 * nrm(ks[19], (NS, 2 * D)),
        "mlp_w_up": nrm(ks[20], (DEPTH, D, D_FF)) * D ** -0.5,
        "mlp_w_down": nrm(ks[21], (DEPTH, D_FF, D)) * D_FF ** -0.5,
    }


def reference(x, meta_tokens, norm_mix_g, norm_mlp_g, da_w_qkv, da_q_norm_g, da_k_norm_g,
              da_lambda, da_subln_g, da_w_o, ssm_a_re, ssm_a_im, ssm_log_dt, ssm_b_re,
              ssm_b_im, ssm_c_re, ssm_c_im, ssm_d, ssm_w_glu, ssm_b_glu, mlp_w_up, mlp_w_down):
    bsz, seq, _ = x.shape
    length = N_META + seq
    padded = ((length + BLOCK_Q - 1) // BLOCK_Q) * BLOCK_Q
    meta = jnp.broadcast_to(meta_tokens.astype(x.dtype)[None], (bsz, N_META, D_MODEL))
    pad = jnp.zeros((bsz, padded - length, D_MODEL), x.dtype)
    h = jnp.concatenate([meta, x, pad], axis=1)
    cos, sin = rope_tables(padded)
    for i in range(DEPTH):
        j = i // N_MIXERS
        hn = rmsnorm(h, norm_mix_g[i])
        if i % N_MIXERS == 0:
            h = h + diff_attention(hn, da_w_qkv[j], da_q_norm_g[j], da_k_norm_g[j], da_lambda[j],
                                   da_subln_g[j], da_w_o[j], cos, sin, lambda_init(i))
        else:
            h = h + s5_mixer(hn, ssm_a_re[j], ssm_a_im[j], ssm_log_dt[j], ssm_b_re[j], ssm_b_im[j],
                             ssm_c_re[j], ssm_c_im[j], ssm_d[j], ssm_w_glu[j], ssm_b_glu[j])
        h = h + sqrelu_mlp(rmsnorm(h, norm_mlp_g[i]), mlp_w_up[i], mlp_w_down[i])
    return h[:, N_META:length]
```

```python
import math
from contextlib import ExitStack

import numpy as np
import ml_dtypes

import concourse.bass as bass
import concourse.mybir as mybir
from concourse.bass_utils import run_bass_kernel_spmd

F32 = mybir.dt.float32
BF16 = mybir.dt.bfloat16
AF = mybir.ActivationFunctionType
ALU = mybir.AluOpType


class Buf:
    def __init__(self, k, name, t, accumulate=False):
        self.k = k
        self.name = name
        self.t = t
        self.acc = accumulate
        self.writers = {}
        self.readers = {}
        self._in = None
        self._out = None

    def __getitem__(self, idx):
        return self.t[idx]

    def bump_in(self):
        if self._in is None:
            self._in = self.k.take_dsem()
        self._in[1] += 16
        return (self._in[0], self._in[1])

    def bump_out(self):
        if self._out is None:
            self._out = self.k.take_dsem()
        self._out[1] += 16
        return (self._out[0], self._out[1])


class KB:
    def __init__(self):
        self.nc = bass.Bass("TRN2", target_bir_lowering=False)
        self.es = ExitStack()
        self.nsem = 0
        nc = self.nc
        self.eng = {"pe": nc.tensor, "act": nc.scalar, "dve": nc.vector, "pool": nc.gpsimd, "sp": nc.sync}
        self.esem = {e: self.new_sem("e_" + e) for e in ("pe", "act", "dve", "pool")}
        self.ecnt = {e: 0 for e in self.esem}
        self.seen = {e: {} for e in self.eng}
        self.uid = 0

    def new_sem(self, name):
        self.nsem += 1
        return self.es.enter_context(self.nc.semaphore(name))

    def take_dsem(self):
        if not hasattr(self, "dpool"):
            self.dpool = []
            self.allsems = []
        if self.dpool:
            return self.dpool.pop()
        ent = [self.new_sem("d%d" % self.nsem), 0]
        self.allsems.append(ent)
        return ent

    def begin_stage(self):
        self.stage_es = ExitStack()
        self.stage_bufs = []

    def end_stage(self):
        for e in ("pe", "act", "dve", "pool", "sp"):
            for e2 in self.esem:
                if self.ecnt[e2] > 0:
                    self._wait(e, self.esem[e2], self.ecnt[e2])
            for ent in getattr(self, "allsems", []):
                if ent[1] > 0:
                    self._wait(e, ent[0], ent[1])
        for b in self.stage_bufs:
            for ent in (b._in, b._out):
                if ent is not None:
                    self.dpool.append(ent)
        self.stage_es.close()
        self.stage_es = None

    def dram(self, name, shape, dtype, kind="Internal"):
        t = self.nc.dram_tensor(name, list(shape), dtype, kind=kind)
        return Buf(self, name, t.ap(), accumulate=True)

    def sbuf(self, name, shape, dtype):
        st = getattr(self, "stage_es", None)
        self.uid += 1
        t = (st or self.es).enter_context(self.nc.sbuf_tensor("%s_%d" % (name, self.uid), list(shape), dtype))
        b = Buf(self, name, t)
        if st is not None:
            self.stage_bufs.append(b)
        return b

    def psum(self, name, shape, dtype=F32):
        st = getattr(self, "stage_es", None)
        self.uid += 1
        t = (st or self.es).enter_context(self.nc.psum_tensor("%s_%d" % (name, self.uid), list(shape), dtype))
        b = Buf(self, name, t)
        if st is not None:
            self.stage_bufs.append(b)
        return b

    def _wait(self, e, sem, val):
        sid = id(sem)
        if self.seen[e].get(sid, 0) >= val:
            return
        self.seen[e][sid] = val
        self.eng[e].wait_ge(sem, val)

    def _deps(self, e, reads, writes):
        for b in reads:
            for sem, val in b.writers.values():
                self._wait(e, sem, val)
        for b in writes:
            for sem, val in b.writers.values():
                if not b.acc:
                    self._wait(e, sem, val)
            for sem, val in b.readers.values():
                self._wait(e, sem, val)

    def _record(self, ev, reads, writes):
        sem, val = ev
        for b in reads:
            b.readers[id(sem)] = ev
        for b in writes:
            if b.acc:
                b.writers[id(sem)] = ev
            else:
                b.writers = {id(sem): ev}
                b.readers = {}

    def op(self, e, fn, reads=(), writes=()):
        self._deps(e, reads, writes)
        ins = fn(self.eng[e])
        self.ecnt[e] += 1
        ins.then_inc(self.esem[e], 1)
        self._record((self.esem[e], self.ecnt[e]), reads, writes)

    def dma(self, q, out_ap, in_ap, src, dst, **kw):
        self._deps(q, [src], [dst])
        ins = self.eng[q].dma_start(out=out_ap, in_=in_ap, **kw)
        if not dst.acc or src.acc:
            ev = dst.bump_in()
        else:
            ev = src.bump_out()
        ins.then_inc(ev[0], 16)
        self._record(ev, [src], [dst])

    def finish(self, outs):
        for b in outs:
            for sem, val in b.writers.values():
                self._wait("sp", sem, val)
        self.es.close()
        return self.nc


class Rot:
    def __init__(self, bufs):
        self.bufs = bufs
        self.i = 0

    def next(self):
        b = self.bufs[self.i % len(self.bufs)]
        self.i += 1
        return b


def token_blocks(T, TB):
    out = []
    t0 = 0
    while t0 < T:
        n = min(TB, T - t0)
        out.append((t0, n))
        t0 += n
    return out


def load_block_fm(k, dst, srcT, KC, t0, n):
    parts = srcT if isinstance(srcT, list) else [srcT]
    kpp = KC // len(parts)
    half = min(8, kpp)
    for pi, part in enumerate(parts):
        view = part.t.rearrange("(kc p) t -> p kc t", p=128)
        for a in range(0, kpp, half):
            b = min(kpp, a + half)
            k.dma("sp", dst.t[:, pi * kpp + a:pi * kpp + b, 0:n], view[:, a:b, t0:t0 + n], part, dst)


def rmsnorm_block(k, st, hblk, out_bf, KC, n, gcol, eps, want_rstd=None):
    D = KC * 128
    ps = st["ps_stat"]
    sq = st["sq"]

    for c in range(KC):
        s = sq.next()
        k.op("act", lambda e, s=s, c=c: e.activation(out=s.t[:, 0:n], in_=hblk.t[:, c, 0:n], func=AF.Square),
             reads=[hblk], writes=[s])
        last = (c == KC - 1)
        k.op("pe", lambda e, s=s, c=c: e.matmul(ps.t[:, 0:n], lhsT=st["ones"].t[:, :], rhs=s.t[:, 0:n],
                                                  start=(c == 0), stop=(c == KC - 1)),
             reads=[s, st["ones"]], writes=[ps] if c == 0 else [])
        if last:
            k._record((k.esem["pe"], k.ecnt["pe"]), [], [ps])
    rstd = st["rstd"]
    k.op("act", lambda e: e.activation(out=rstd.t[:, 0:n], in_=ps.t[:, 0:n], func=AF.Sqrt,
                                       bias=st["epsb"].t[:, 0:1], scale=1.0 / D),
         reads=[ps, st["epsb"]], writes=[rstd])
    k.op("dve", lambda e: e.reciprocal(out=rstd.t[:, 0:n], in_=rstd.t[:, 0:n]), reads=[rstd], writes=[rstd])
    for c in range(KC):
        k.op("dve", lambda e, c=c: e.scalar_tensor_tensor(out=out_bf.t[:, c, 0:n], in0=hblk.t[:, c, 0:n],
                                                          scalar=gcol.t[:, c:c + 1], in1=rstd.t[:, 0:n],
                                                          op0=ALU.mult, op1=ALU.mult),
             reads=[hblk, gcol, rstd], writes=[out_bf])
    if want_rstd is not None:
        k.op("act", lambda e: e.activation(out=want_rstd.t[:, 0:n], in_=rstd.t[:, 0:n], func=AF.Copy),
             reads=[rstd], writes=[want_rstd])


def fm_linear(k, xin, KC, n, w_dram, groups, wbufs, psums, epilogue):
    for gi, grp in enumerate(groups):
        pss = []
        for ch in grp:
            w = wbufs.next()
            k.dma("sp", w.t[:, :, :], w_dram.t[ch], w_dram, w)
            ps = psums.next()

            def mm(e, w=w, ps=ps):
                ins = None
                for kc in range(KC):
                    ins = e.matmul(ps.t[:, 0:n], lhsT=w.t[:, kc, :], rhs=xin.t[:, kc, 0:n],
                                   start=(kc == 0), stop=(kc == KC - 1))
                return ins
            k.op("pe", mm, reads=[w, xin], writes=[ps])
            pss.append(ps)
        epilogue(gi, pss)


class Cfg:
    def __init__(self, D=4096, SEQ=8192, H=16, NMETA=16):
        self.D, self.SEQ, self.H, self.NMETA = D, SEQ, H, NMETA
        self.T = SEQ + NMETA
        self.KC = D // 128
        self.NSH = 2 * H
        self.FF = 4 * D
        self.FC = self.FF // 128
        self.G = D // 16
        self.GP = self.G // 2
        self.EPS = 1e-6


def common_scratch(k, TB):
    ones = k.sbuf("ones", [128, 128], F32)
    epsb = k.sbuf("epsb", [128, 1], F32)
    k.op("dve", lambda e: e.memset(ones.t[:, :], 1.0), writes=[ones])
    k.op("dve", lambda e: e.memset(epsb.t[:, :], 1e-6), writes=[epsb])
    return dict(ones=ones, epsb=epsb, ps_stat=k.psum("ps_stat", [128, 512]),
                rstd=k.sbuf("rstd", [128, TB], F32),
                sq=Rot([k.sbuf("sq%d" % i, [128, TB], F32) for i in range(2)]))


def cast_weights(k, wf, wb, nchunks):
    for c in range(nchunks):
        k.dma("pool", wb.t[c], wf.t[c], wf, wb)


def stage_qkv(k, cfg, hT, gmix, wqk, wv, gqk, ropeC, ropeS, qk_dram, v_dram):
    TB = 512
    KC, T, NSH = cfg.KC, cfg.T, cfg.NSH
    k.begin_stage()
    st = common_scratch(k, TB)
    gcol = k.sbuf("gcol", [128, KC], F32)
    k.dma("sp", gcol.t[:, :], gmix.t[:, 0, :], gmix, gcol)
    gq = k.sbuf("gq", [128, 4], F32)
    k.dma("sp", gq.t[:, :], gqk.t[:, :], gqk, gq)
    bones = k.sbuf("bones", [128, 128], F32)
    k.op("dve", lambda e: e.memset(bones.t[:, :], 0.0), writes=[bones])
    k.op("dve", lambda e: e.memset(bones.t[0:64, 0:64], 1.0), reads=[bones], writes=[bones])
    k.op("dve", lambda e: e.memset(bones.t[64:128, 64:128], 1.0), reads=[bones], writes=[bones])
    hblk = k.sbuf("hblk", [128, KC, TB], F32)
    xin = k.sbuf("xin", [128, KC, TB], BF16)
    cs = k.sbuf("cs", [128, TB], F32)
    sn = k.sbuf("sn", [128, TB], F32)
    wbufs = Rot([k.sbuf("w%d" % i, [128, KC, 128], BF16) for i in range(3)])
    wvb = Rot([k.sbuf("wv%d" % i, [128, KC, 512], BF16) for i in range(1)])
    psums = Rot([k.psum("ps%d" % i, [128, 512]) for i in range(4)])
    psq = k.psum("psq", [128, 512])
    sqa = k.sbuf("sqa", [128, TB], F32)
    sqb = k.sbuf("sqb", [128, TB], F32)
    rs = k.sbuf("rs", [128, TB], F32)
    qa = k.sbuf("qa", [128, TB], F32)
    qb = k.sbuf("qb", [128, TB], F32)
    t1 = k.sbuf("t1", [128, TB], F32)
    t2 = k.sbuf("t2", [128, TB], F32)
    oa = Rot([k.sbuf("oa%d" % i, [128, TB], BF16) for i in range(2)])
    ob = Rot([k.sbuf("ob%d" % i, [128, TB], BF16) for i in range(2)])
    vo = Rot([k.sbuf("vo%d" % i, [128, 512], BF16) for i in range(2)])
    for (t0, n) in token_blocks(T, TB):
        load_block_fm(k, hblk, hT, KC, t0, n)
        k.dma("sp", cs.t[:, 0:n], ropeC.t[:, t0:t0 + n], ropeC, cs)
        k.dma("sp", sn.t[:, 0:n], ropeS.t[:, t0:t0 + n], ropeS, sn)
        rmsnorm_block(k, st, hblk, xin, KC, n, gcol, cfg.EPS)

        def epi(gi, pss, t0=t0, n=n):
            A, B = pss
            isk = 1 if gi >= NSH // 2 else 0
            j = gi - isk * (NSH // 2)
            k.op("act", lambda e: e.activation(out=sqa.t[:, 0:n], in_=A.t[:, 0:n], func=AF.Square), reads=[A], writes=[sqa])
            k.op("act", lambda e: e.activation(out=sqb.t[:, 0:n], in_=B.t[:, 0:n], func=AF.Square), reads=[B], writes=[sqb])

            def mm(e):
                e.matmul(psq.t[:, 0:n], lhsT=bones.t[:, :], rhs=sqa.t[:, 0:n], start=True, stop=False)
                return e.matmul(psq.t[:, 0:n], lhsT=bones.t[:, :], rhs=sqb.t[:, 0:n], start=False, stop=True)
            k.op("pe", mm, reads=[bones, sqa, sqb], writes=[psq])
            k.op("act", lambda e: e.activation(out=rs.t[:, 0:n], in_=psq.t[:, 0:n], func=AF.Sqrt,
                                               bias=st["epsb"].t[:, 0:1], scale=1.0 / 128), reads=[psq, st["epsb"]], writes=[rs])
            k.op("dve", lambda e: e.reciprocal(out=rs.t[:, 0:n], in_=rs.t[:, 0:n]), reads=[rs], writes=[rs])
            k.op("dve", lambda e: e.scalar_tensor_tensor(out=qa.t[:, 0:n], in0=A.t[:, 0:n], scalar=gq.t[:, 2 * isk:2 * isk + 1],
                                                          in1=rs.t[:, 0:n], op0=ALU.mult, op1=ALU.mult), reads=[A, gq, rs], writes=[qa])
            k.op("dve", lambda e: e.scalar_tensor_tensor(out=qb.t[:, 0:n], in0=B.t[:, 0:n], scalar=gq.t[:, 2 * isk + 1:2 * isk + 2],
                                                          in1=rs.t[:, 0:n], op0=ALU.mult, op1=ALU.mult), reads=[B, gq, rs], writes=[qb])
            o_a, o_b = oa.next(), ob.next()
            k.op("dve", lambda e: e.tensor_tensor(out=t1.t[:, 0:n], in0=qa.t[:, 0:n], in1=cs.t[:, 0:n], op=ALU.mult), reads=[qa, cs], writes=[t1])
            k.op("pool", lambda e: e.tensor_tensor(out=t2.t[:, 0:n], in0=qb.t[:, 0:n], in1=sn.t[:, 0:n], op=ALU.mult), reads=[qb, sn], writes=[t2])
            k.op("dve", lambda e: e.tensor_tensor(out=o_a.t[:, 0:n], in0=t1.t[:, 0:n], in1=t2.t[:, 0:n], op=ALU.subtract), reads=[t1, t2], writes=[o_a])
            k.op("dve", lambda e: e.tensor_tensor(out=t1.t[:, 0:n], in0=qb.t[:, 0:n], in1=cs.t[:, 0:n], op=ALU.mult), reads=[qb, cs], writes=[t1])
            k.op("pool", lambda e: e.tensor_tensor(out=t2.t[:, 0:n], in0=qa.t[:, 0:n], in1=sn.t[:, 0:n], op=ALU.mult), reads=[qa, sn], writes=[t2])
            k.op("dve", lambda e: e.tensor_tensor(out=o_b.t[:, 0:n], in0=t1.t[:, 0:n], in1=t2.t[:, 0:n], op=ALU.add), reads=[t1, t2], writes=[o_b])
            s0 = isk * NSH + 2 * j
            k.dma("sp", qk_dram.t[s0, 0:64, t0:t0 + n], o_a.t[0:64, 0:n], o_a, qk_dram)
            k.dma("sp", qk_dram.t[s0 + 1, 0:64, t0:t0 + n], o_a.t[64:128, 0:n], o_a, qk_dram)
            k.dma("sp", qk_dram.t[s0, 64:128, t0:t0 + n], o_b.t[0:64, 0:n], o_b, qk_dram)
            k.dma("sp", qk_dram.t[s0 + 1, 64:128, t0:t0 + n], o_b.t[64:128, 0:n], o_b, qk_dram)
        fm_linear(k, xin, KC, n, wqk, [[2 * p, 2 * p + 1] for p in range(NSH)], wbufs, psums, epi)
        for sl in range(cfg.D // 512):
            w = wvb.next()
            k.dma("sp", w.t[:, :, :], wv.t[sl], wv, w)
            for (tt0, tn) in token_blocks(n, 128):
                ps = psums.next()

                def mm(e, w=w, ps=ps, tt0=tt0, tn=tn):
                    ins = None
                    for kc in range(KC):
                        ins = e.matmul(ps.t[0:tn, 0:512], lhsT=xin.t[:, kc, tt0:tt0 + tn], rhs=w.t[:, kc, :],
                                       start=(kc == 0), stop=(kc == KC - 1))
                    return ins
                k.op("pe", mm, reads=[w, xin], writes=[ps])
                o = vo.next()
                k.op("act", lambda e, o=o, ps=ps, tn=tn: e.activation(out=o.t[0:tn, :], in_=ps.t[0:tn, 0:512], func=AF.Copy),
                     reads=[ps], writes=[o])
                k.dma("sp", v_dram.t[t0 + tt0:t0 + tt0 + tn, sl * 512:(sl + 1) * 512], o.t[0:tn, :], o, v_dram)
    k.end_stage()


def stage_attn(k, cfg, qk_dram, v_dram, lamv, subg, maskf, identf, oT, lam_init):
    T, NSH, H = cfg.T, cfg.NSH, cfg.H
    NT = (T + 127) // 128
    NTF = T // 128
    QB = 256
    k.begin_stage()
    mk32 = k.sbuf("mk32", [128, 128], F32)
    id32 = k.sbuf("id32", [128, 128], F32)
    mask = k.sbuf("mask", [128, 128], BF16)
    ident = k.sbuf("ident", [128, 128], BF16)
    k.dma("sp", mk32.t[:, :], maskf.t[:, :], maskf, mk32)
    k.dma("sp", id32.t[:, :], identf.t[:, :], identf, id32)
    k.op("dve", lambda e: e.tensor_copy(mask.t[:, :], mk32.t[:, :]), reads=[mk32], writes=[mask])
    k.op("dve", lambda e: e.tensor_copy(ident.t[:, :], id32.t[:, :]), reads=[id32], writes=[ident])
    sg = k.sbuf("sg", [128, 256], F32)
    k.dma("sp", sg.t[:, :], subg.t[:, :], subg, sg)
    lv = k.sbuf("lv", [128, 512], F32)
    k.dma("sp", lv.t[:, :], lamv.t[:, :], lamv, lv)
    lt = k.sbuf("lt", [128, 256], F32)
    l2 = k.sbuf("l2", [128, 2], F32)
    nlam = k.sbuf("nlam", [128, 1], F32)
    k.op("dve", lambda e: e.tensor_tensor(out=lt.t[:, 0:128], in0=lv.t[:, 0:128], in1=lv.t[:, 128:256], op=ALU.mult), reads=[lv], writes=[lt])
    k.op("dve", lambda e: e.tensor_tensor(out=lt.t[:, 128:256], in0=lv.t[:, 256:384], in1=lv.t[:, 384:512], op=ALU.mult), reads=[lv, lt], writes=[lt])
    k.op("dve", lambda e: e.tensor_reduce(out=l2.t[:, :], in_=lt.t[:, :].rearrange("p (a b) -> p a b", a=2), axis=mybir.AxisListType.X, op=ALU.add),
         reads=[lt], writes=[l2])
    k.op("act", lambda e: e.activation(out=l2.t[:, :], in_=l2.t[:, :], func=AF.Exp), reads=[l2], writes=[l2])
    k.op("dve", lambda e: e.scalar_tensor_tensor(out=nlam.t[:, :], in0=l2.t[:, 1:2], scalar=-float(lam_init), in1=l2.t[:, 0:1],
                                                  op0=ALU.add, op1=ALU.subtract), reads=[l2], writes=[nlam])
    qs_ = [k.sbuf("q%d" % c, [128, T], BF16) for c in range(2)]
    ks_ = [k.sbuf("k%d" % c, [128, T], BF16) for c in range(2)]
    vh = k.sbuf("vh", [128, NT, 257], BF16)
    k.op("dve", lambda e: e.memset(vh.t[:, :, 256:257], 1.0), writes=[vh])
    pss = Rot([k.psum("pss%d" % i, [128, 512]) for i in range(2)])
    accs = [[k.psum("acc%d%d" % (c, i), [128, 512]) for i in range(2)] for c in range(2)]
    ptr = k.psum("ptr", [128, 1024], BF16)
    pT = Rot([k.sbuf("pT%d" % i, [128, 2, QB], BF16) for i in range(3)])
    r0 = k.sbuf("r0", [128, 1], F32)
    r1 = k.sbuf("r1", [128, 1], F32)
    of = k.sbuf("of", [128, 256], F32)
    junk = k.sbuf("junk", [128, 256], F32)
    ss = k.sbuf("ss", [128, 1], F32)
    ob = k.sbuf("ob", [128, 256], BF16)
    oTs = Rot([k.sbuf("oTs%d" % i, [128, 2, 128], BF16) for i in range(2)])
    scale = 128.0 ** -0.5
    epsb_att = k.sbuf("epsb_att", [128, 1], F32)
    k.op("dve", lambda e: e.memset(epsb_att.t[:, :], 1e-6), writes=[epsb_att])
    for h in range(H):
        for c in range(2):
            k.dma("sp", qs_[c].t[:, :], qk_dram.t[2 * h + c], qk_dram, qs_[c])
            k.dma("sp", ks_[c].t[:, :], qk_dram.t[NSH + 2 * h + c], qk_dram, ks_[c])
        for i0 in range(0, NTF, 8):
            i1 = min(NTF, i0 + 8)
            k.dma("sp", vh.t[:, i0:i1, 0:256],
                  v_dram.t[i0 * 128:i1 * 128, h * 256:(h + 1) * 256].rearrange("(i p) e -> p i e", p=128), v_dram, vh)
        if NT > NTF:
            rem = T - NTF * 128
            k.dma("sp", vh.t[0:rem, NTF, 0:256], v_dram.t[NTF * 128:T, h * 256:(h + 1) * 256], v_dram, vh)
        for (q0, nq) in token_blocks(T, QB):
            tq0 = q0 // 128
            qtiles = token_blocks(nq, 128)
            last_kt = (q0 + nq - 1) // 128
            for kt in range(last_kt + 1):
                nk = min(128, T - 128 * kt)
                qs = max(q0, 128 * kt)
                nqq = q0 + nq - qs
                ps = pss.next()

                def mm(e, ps=ps, kt=kt, nk=nk, qs=qs, nqq=nqq):
                    ins = None
                    for c in range(2):
                        ins = e.matmul(ps.t[0:nk, c * QB:c * QB + nqq], lhsT=ks_[c].t[:, 128 * kt:128 * kt + nk],
                                       rhs=qs_[c].t[:, qs:qs + nqq], start=True, stop=True)
                    return ins
                k.op("pe", mm, reads=[ks_[0], ks_[1], qs_[0], qs_[1]], writes=[ps])
                p = pT.next()
                k.op("act", lambda e, p=p, ps=ps, nk=nk, nqq=nqq: e.activation(
                    out=p.t[0:nk, :, 0:nqq], in_=ps.t[0:nk, :].rearrange("p (c q) -> p c q", c=2)[:, :, 0:nqq],
                    func=AF.Exp, scale=scale), reads=[ps], writes=[p])
                if 128 * kt >= q0:
                    w = min(128, nqq)
                    for c in range(2):
                        k.op("dve", lambda e, p=p, c=c, nk=nk, w=w: e.tensor_tensor(
                            out=p.t[0:nk, c, 0:w], in0=p.t[0:nk, c, 0:w], in1=mask.t[0:nk, 0:w], op=ALU.mult),
                            reads=[p, mask], writes=[p])
                for qi, (qq0, nqt) in enumerate(qtiles):
                    tq = tq0 + qi
                    if tq < kt:
                        continue
                    off = 128 * tq - qs
                    for c in range(2):
                        acc = accs[c][qi]
                        k.op("pe", lambda e, acc=acc, p=p, c=c, nk=nk, off=off, nqt=nqt, kt=kt, tq=tq: e.matmul(
                            acc.t[0:nqt, 0:257], lhsT=p.t[0:nk, c, off:off + nqt], rhs=vh.t[0:nk, kt, :],
                            start=(kt == 0), stop=(kt == tq)),
                            reads=[p, vh], writes=[acc] if kt == 0 else [])
                        if kt == tq:
                            k._record((k.esem["pe"], k.ecnt["pe"]), [], [acc])
            for qi, (qq0, nqt) in enumerate(qtiles):
                tq = tq0 + qi
                a0, a1 = accs[0][qi], accs[1][qi]
                k.op("dve", lambda e: e.reciprocal(out=r0.t[0:nqt, :], in_=a0.t[0:nqt, 256:257]), reads=[a0], writes=[r0])
                k.op("dve", lambda e: e.reciprocal(out=r1.t[0:nqt, :], in_=a1.t[0:nqt, 256:257]), reads=[a1], writes=[r1])
                k.op("dve", lambda e: e.tensor_tensor(out=r1.t[0:nqt, :], in0=r1.t[0:nqt, :], in1=nlam.t[0:nqt, :], op=ALU.mult),
                     reads=[r1, nlam], writes=[r1])
                k.op("dve", lambda e: e.tensor_scalar(out=of.t[0:nqt, :], in0=a0.t[0:nqt, 0:256], scalar1=r0.t[0:nqt, 0:1], scalar2=None,
                                                       op0=ALU.mult), reads=[a0, r0], writes=[of])
                k.op("dve", lambda e: e.scalar_tensor_tensor(out=of.t[0:nqt, :], in0=a1.t[0:nqt, 0:256], scalar=r1.t[0:nqt, 0:1],
                                                              in1=of.t[0:nqt, :], op0=ALU.mult, op1=ALU.add), reads=[a1, r1, of], writes=[of])
                k.op("act", lambda e: e.activation(out=junk.t[0:nqt, :], in_=of.t[0:nqt, :], func=AF.Square, accum_out=ss.t[0:nqt, :]),
                     reads=[of], writes=[junk, ss])
                k.op("act", lambda e: e.activation(out=ss.t[0:nqt, :], in_=ss.t[0:nqt, :], func=AF.Sqrt, bias=epsb_att.t[0:nqt, 0:1], scale=1.0 / 256),
                     reads=[ss, epsb_att], writes=[ss])
                k.op("dve", lambda e: e.reciprocal(out=ss.t[0:nqt, :], in_=ss.t[0:nqt, :]), reads=[ss], writes=[ss])
                k.op("dve", lambda e: e.tensor_scalar(out=ss.t[0:nqt, :], in0=ss.t[0:nqt, :], scalar1=float(1.0 - lam_init), scalar2=None, op0=ALU.mult),
                     reads=[ss], writes=[ss])
                k.op("dve", lambda e: e.scalar_tensor_tensor(out=ob.t[0:nqt, :], in0=of.t[0:nqt, :], scalar=ss.t[0:nqt, 0:1], in1=sg.t[0:nqt, :],
                                                              op0=ALU.mult, op1=ALU.mult), reads=[of, ss, sg], writes=[ob])

                def tr(e):
                    ins = None
                    for e2 in range(2):
                        ins = e.transpose(ptr.t[:, e2 * 128:e2 * 128 + nqt], ob.t[0:nqt, e2 * 128:(e2 + 1) * 128], ident.t[0:nqt, 0:nqt])
                    return ins
                k.op("pe", tr, reads=[ob, ident], writes=[ptr])
                o = oTs.next()
                k.op("act", lambda e, o=o: e.activation(out=o.t[:, :, 0:nqt], in_=ptr.t[:, 0:256].rearrange("p (a b) -> p a b", a=2)[:, :, 0:nqt],
                                                        func=AF.Copy), reads=[ptr], writes=[o])
                k.dma("sp", oT.t[h * 256:(h + 1) * 256, 128 * tq:128 * tq + nqt].rearrange("(a p) t -> p a t", p=128),
                      o.t[:, :, 0:nqt], o, oT)
    k.end_stage()


def stage_linear_res(k, cfg, name, xT_bf, KCin, w, nout, resT, outT, TB, bias=None, glu=False):
    T = cfg.T
    k.begin_stage()
    xin = k.sbuf("xin", [128, KCin, TB], BF16)
    wbufs = Rot([k.sbuf("w%d" % i, [128, KCin, 128], BF16) for i in range(3)])
    psums = Rot([k.psum("ps%d" % i, [128, 512]) for i in range(4)])
    rbuf = Rot([k.sbuf("r%d" % i, [128, TB], F32) for i in range(2)])
    obuf = Rot([k.sbuf("o%d" % i, [128, TB], F32) for i in range(2)])
    sgb = k.sbuf("sgb", [128, TB], F32)
    if bias is not None:
        bcol = k.sbuf("bcol", [128, 2 * nout], F32)
        k.dma("sp", bcol.t[:, :], bias.t[:, :], bias, bcol)
    for (t0, n) in token_blocks(T, TB):
        load_block_fm(k, xin, xT_bf, KCin, t0, n)

        def epi(gi, pss, t0=t0, n=n):
            r = rbuf.next()
            k.dma("sp", r.t[:, 0:n], resT.t[gi * 128:(gi + 1) * 128, t0:t0 + n], resT, r)
            o = obuf.next()
            if glu:
                k.op("act", lambda e: e.activation(out=sgb.t[:, 0:n], in_=pss[1].t[:, 0:n], func=AF.Sigmoid,
                                                   bias=bcol.t[:, nout + gi:nout + gi + 1], scale=1.0), reads=[pss[1], bcol], writes=[sgb])
                k.op("dve", lambda e: e.scalar_tensor_tensor(out=o.t[:, 0:n], in0=pss[0].t[:, 0:n], scalar=bcol.t[:, gi:gi + 1],
                                                              in1=sgb.t[:, 0:n], op0=ALU.add, op1=ALU.mult), reads=[pss[0], bcol, sgb], writes=[o])
                k.op("dve", lambda e: e.tensor_tensor(out=o.t[:, 0:n], in0=o.t[:, 0:n], in1=r.t[:, 0:n], op=ALU.add), reads=[o, r], writes=[o])
            else:
                k.op("dve", lambda e: e.tensor_tensor(out=o.t[:, 0:n], in0=pss[0].t[:, 0:n], in1=r.t[:, 0:n], op=ALU.add), reads=[pss[0], r], writes=[o])
            k.dma("sp", outT.t[gi * 128:(gi + 1) * 128, t0:t0 + n], o.t[:, 0:n], o, outT)
        groups = [[c, nout + c] for c in range(nout)] if glu else [[c] for c in range(nout)]
        fm_linear(k, xin, KCin, n, w, groups, wbufs, psums, epi)
    k.end_stage()


def stage_up(k, cfg, hT, gmlp, layer, wup, uT):
    TB = 512
    KC, T = cfg.KC, cfg.T
    k.begin_stage()
    st = common_scratch(k, TB)
    gcol = k.sbuf("gcol", [128, KC], F32)
    k.dma("sp", gcol.t[:, :], gmlp.t[:, layer, :], gmlp, gcol)
    hblk = k.sbuf("hblk", [128, KC, TB], F32)
    xin = k.sbuf("xin", [128, KC, TB], BF16)
    wbufs = Rot([k.sbuf("w%d" % i, [128, KC, 128], BF16) for i in range(3)])
    psums = Rot([k.psum("ps%d" % i, [128, 512]) for i in range(4)])
    rl = Rot([k.sbuf("rl%d" % i, [128, TB], F32) for i in range(2)])
    ub = Rot([k.sbuf("ub%d" % i, [128, TB], BF16) for i in range(3)])
    for (t0, n) in token_blocks(T, TB):
        load_block_fm(k, hblk, hT, KC, t0, n)
        rmsnorm_block(k, st, hblk, xin, KC, n, gcol, cfg.EPS)

        def epi(gi, pss, t0=t0, n=n):
            r = rl.next()
            u = ub.next()
            k.op("act", lambda e: e.activation(out=r.t[:, 0:n], in_=pss[0].t[:, 0:n], func=AF.Relu), reads=[pss[0]], writes=[r])
            k.op("dve", lambda e: e.tensor_tensor(out=u.t[:, 0:n], in0=r.t[:, 0:n], in1=r.t[:, 0:n], op=ALU.mult), reads=[r], writes=[u])
            cpp = cfg.FC // len(uT)
            up_ = uT[gi // cpp]
            gl = gi % cpp
            k.dma("sp", up_.t[gl * 128:(gl + 1) * 128, t0:t0 + n], u.t[:, 0:n], u, up_)
        fm_linear(k, xin, KC, n, wup, [[c] for c in range(cfg.FC)], wbufs, psums, epi)
    k.end_stage()


def rope_tables(T):
    inv = (np.float32(10000.0) ** (-np.arange(0, 128, 2, dtype=np.float32) / np.float32(128))).astype(np.float32)
    ang = (np.arange(T, dtype=np.float32)[:, None] * inv[None, :]).astype(np.float32)
    c = np.cos(ang).astype(np.float32).T
    s = np.sin(ang).astype(np.float32).T
    return np.ascontiguousarray(np.concatenate([c, c], 0)), np.ascontiguousarray(np.concatenate([s, s], 0))


def arr_w(W, cw=128):
    K, N = W.shape
    return np.ascontiguousarray(W.reshape(K // 128, 128, N // cw, cw).transpose(2, 1, 0, 3))


def col_g(g):
    return np.ascontiguousarray(g.reshape(-1, 128).T)


def build_program(cfg, layers=2):
    k = KB()
    D, T, KC, NSH, FC = cfg.D, cfg.T, cfg.KC, cfg.NSH, cfg.FC
    ein = lambda name, shape: k.dram(name, shape, F32, kind="ExternalInput")
    hT0 = ein("hT0", [D, T])
    gmix = ein("gmix", [128, 2, KC])
    gmlp = ein("gmlp", [128, 2, KC])
    wqk_f = ein("wqk", [2 * NSH, 128, KC, 128])
    wv_f = ein("wv", [D // 512, 128, KC, 512])
    gqk = ein("gqk", [128, 4])
    ropeC = ein("ropeC", [128, T])
    ropeS = ein("ropeS", [128, T])
    lamv = ein("lamv", [128, 512])
    subg = ein("subg", [128, 256])
    maskf = ein("maskf", [128, 128])
    identf = ein("identf", [128, 128])
    wo_f = ein("wo", [KC, 128, KC, 128])
    wup_f = [ein("wup%d" % l, [FC, 128, KC, 128]) for l in range(layers)]
    wdn_f = [ein("wdn%d" % l, [KC, 128, FC, 128]) for l in range(layers)]
    outT = k.dram("outT", [D, T], F32, kind="ExternalOutput")
    wqk = k.dram("wqk_b", [2 * NSH, 128, KC, 128], BF16)
    wv = k.dram("wv_b", [D // 512, 128, KC, 512], BF16)
    wo = k.dram("wo_b", [KC, 128, KC, 128], BF16)
    wup = [k.dram("wup_b%d" % l, [FC, 128, KC, 128], BF16) for l in range(layers)]
    wdn = [k.dram("wdn_b%d" % l, [KC, 128, FC, 128], BF16) for l in range(layers)]
    cast_weights(k, wqk_f, wqk, 2 * NSH)
    cast_weights(k, wv_f, wv, D // 512)
    cast_weights(k, wo_f, wo, KC)
    for l in range(layers):
        cast_weights(k, wup_f[l], wup[l], FC)
        cast_weights(k, wdn_f[l], wdn[l], KC)
    qk_dram = k.dram("qk_s", [2 * NSH, 128, T], BF16)
    v_dram = k.dram("v_s", [T, D], BF16)
    oT = k.dram("oT_s", [D, T], BF16)
    h1T = k.dram("h1T", [D, T], F32)
    uT = [k.dram("uT%d" % i, [cfg.FF // 2, T], BF16) for i in range(2)]
    h2T = k.dram("h2T", [D, T], F32) if layers > 1 else outT
    lam0 = 0.8 - 0.6 * math.exp(-0.3 * 0)
    stage_qkv(k, cfg, hT0, gmix, wqk, wv, gqk, ropeC, ropeS, qk_dram, v_dram)
    stage_attn(k, cfg, qk_dram, v_dram, lamv, subg, maskf, identf, oT, lam0)
    stage_linear_res(k, cfg, "wo", oT, KC, wo, KC, hT0, h1T, 512)
    stage_up(k, cfg, h1T, gmlp, 0, wup[0], uT)
    stage_linear_res(k, cfg, "dn0", uT, FC, wdn[0], KC, h1T, h2T, 256)
    if layers > 1:
        GP = cfg.GP
        prm = {}
        for nm, shp in (("ar", [128, GP]), ("ai", [128, GP]), ("ldt", [128, GP]), ("br", [128, GP, 16]), ("bi", [128, GP, 16]),
                        ("cr", [128, GP, 16]), ("ci", [128, GP, 16]), ("m01", [128, 2]), ("dcol", [128, KC]), ("bglu", [128, 2 * KC])):
            prm[nm] = ein(nm, shp)
        prm["identf"] = identf
        wglu_f = ein("wglu", [2 * KC, 128, KC, 128])
        wglu = k.dram("wglu_b", [2 * KC, 128, KC, 128], BF16)
        cast_weights(k, wglu_f, wglu, 2 * KC)
        prm["abar_s"] = k.dram("abar_s", [128, 2, GP], F32)
        prm["lc_s"] = k.dram("lc_s", [2, 128, GP, 2, 16], BF16)
        prm["lb_s"] = k.dram("lb_s", [2, 32, GP, 128], BF16)
        hnT = k.dram("hnT", [D, T], F32)
        hnTb = k.dram("hnTb", [D, T], BF16)
        yT = k.dram("yT", [D, T], F32)
        zT = k.dram("zT", [D, T], BF16)
        h3T = k.dram("h3T", [D, T], F32)
        stage_s5(k, cfg, h2T, gmix, prm, hnT, hnTb, yT, zT)
        stage_linear_res(k, cfg, "glu", zT, KC, wglu, KC, h2T, h3T, 512, bias=prm["bglu"], glu=True)
        stage_up(k, cfg, h3T, gmlp, 1, wup[1], uT)
        stage_linear_res(k, cfg, "dn1", uT, FC, wdn[1], KC, h3T, outT, 256)
    return k.finish([outT]), k


def prep_inputs(cfg, x_b, meta_tokens, norm_mix_g, norm_mlp_g, da_w_qkv, da_q_norm_g, da_k_norm_g,
                da_lambda, da_subln_g, da_w_o, mlp_w_up, mlp_w_down, layers=2, shared=None, ssm=None):
    D, T, NSH = cfg.D, cfg.T, cfg.NSH
    f = np.float32
    m = {}
    m["hT0"] = np.ascontiguousarray(np.concatenate([meta_tokens, x_b], 0).T.astype(f))
    if shared is not None:
        m.update(shared)
        return m, shared
    sh = {}
    sh["gmix"] = np.ascontiguousarray(np.stack([col_g(norm_mix_g[min(l, len(norm_mix_g) - 1)]) for l in range(2)], 1))
    sh["gmlp"] = np.ascontiguousarray(np.stack([col_g(norm_mlp_g[min(l, len(norm_mlp_g) - 1)]) for l in range(2)], 1))
    Wqkv = da_w_qkv[0]
    cols = []
    for base in (0, D):
        for j in range(NSH // 2):
            s0, s1 = base + (2 * j) * 128, base + (2 * j + 1) * 128
            cols.append(np.concatenate([np.arange(s0, s0 + 64), np.arange(s1, s1 + 64)]))
            cols.append(np.concatenate([np.arange(s0 + 64, s0 + 128), np.arange(s1 + 64, s1 + 128)]))
    cols = np.concatenate(cols)
    sh["wqk"] = arr_w(Wqkv[:, cols])
    sh["wv"] = arr_w(Wqkv[:, 2 * D:3 * D], 512)
    gq, gk = da_q_norm_g[0], da_k_norm_g[0]
    lo = np.tile(np.arange(64), 2)
    sh["gqk"] = np.ascontiguousarray(np.stack([gq[lo], gq[64 + lo], gk[lo], gk[64 + lo]], 1).astype(f))
    c, s = rope_tables(T)
    sh["ropeC"], sh["ropeS"] = c, s
    sh["lamv"] = np.ascontiguousarray(np.broadcast_to(da_lambda[0].reshape(1, 512), (128, 512)).astype(f))
    sh["subg"] = np.ascontiguousarray(np.broadcast_to(da_subln_g[0].reshape(1, 256), (128, 256)).astype(f))
    sh["maskf"] = np.triu(np.ones((128, 128), f))
    sh["identf"] = np.eye(128, dtype=f)
    sh["wo"] = arr_w(da_w_o[0])
    for l in range(layers):
        sh["wup%d" % l] = arr_w(mlp_w_up[l])
        sh["wdn%d" % l] = arr_w(mlp_w_down[l])
    if ssm is not None:
        a_re, a_im, log_dt, b_re, b_im, c_re, c_im, d_skip, w_glu, b_glu = ssm
        G = a_re.shape[0]
        GP = G // 2
        sh["ar"] = np.ascontiguousarray(a_re.reshape(GP, 2, 64).transpose(1, 2, 0).reshape(128, GP))
        sh["ai"] = np.ascontiguousarray(a_im.reshape(GP, 2, 64).transpose(1, 2, 0).reshape(128, GP))
        sh["ldt"] = np.ascontiguousarray(np.broadcast_to(log_dt.reshape(GP, 2).T[:, None, :], (2, 64, GP)).reshape(128, GP).astype(f))
        sh["br"] = np.ascontiguousarray(b_re.reshape(GP, 2, 64, 16).transpose(1, 2, 0, 3).reshape(128, GP, 16))
        sh["bi"] = np.ascontiguousarray(b_im.reshape(GP, 2, 64, 16).transpose(1, 2, 0, 3).reshape(128, GP, 16))
        sh["cr"] = np.ascontiguousarray(c_re.reshape(GP, 2, 16, 64).transpose(1, 3, 0, 2).reshape(128, GP, 16))
        sh["ci"] = np.ascontiguousarray(c_im.reshape(GP, 2, 16, 64).transpose(1, 3, 0, 2).reshape(128, GP, 16))
        m01 = np.zeros((128, 2), f)
        m01[:64, 0] = 1.0
        m01[64:, 1] = 1.0
        sh["m01"] = m01
        sh["dcol"] = col_g(d_skip)
        sh["bglu"] = col_g(b_glu)
        sh["wglu"] = arr_w(w_glu)
    m.update(sh)
    return m, sh


def kernel(x, meta_tokens, norm_mix_g, norm_mlp_g, da_w_qkv, da_q_norm_g, da_k_norm_g, da_lambda, da_subln_g, da_w_o,
           ssm_a_re, ssm_a_im, ssm_log_dt, ssm_b_re, ssm_b_im, ssm_c_re, ssm_c_im, ssm_d, ssm_w_glu, ssm_b_glu,
           mlp_w_up, mlp_w_down):
    a = lambda v: np.asarray(v, dtype=np.float32)
    x = a(x)
    B, SEQ, D = x.shape
    cfg = Cfg(D=D, SEQ=SEQ, H=D // 256, NMETA=meta_tokens.shape[0])
    nc, _ = build_program(cfg, layers=2)
    ssm = tuple(a(v)[0] for v in (ssm_a_re, ssm_a_im, ssm_log_dt, ssm_b_re, ssm_b_im, ssm_c_re, ssm_c_im, ssm_d, ssm_w_glu, ssm_b_glu))
    maps, sh = [], None
    for b in range(B):
        m, sh = prep_inputs(cfg, x[b], a(meta_tokens), a(norm_mix_g), a(norm_mlp_g), a(da_w_qkv), a(da_q_norm_g), a(da_k_norm_g),
                            a(da_lambda), a(da_subln_g), a(da_w_o), a(mlp_w_up), a(mlp_w_down), layers=2, shared=sh, ssm=ssm)
        maps.append(m)
    res = run_bass_kernel_spmd(nc, maps, core_ids=list(range(B)))
    nm = cfg.NMETA
    out = np.stack([np.ascontiguousarray(res.results[b]["outT"][:, nm:nm + SEQ].T) for b in range(B)])
    return out.astype(np.float32)


def stage_s5(k, cfg, hT, gmix, prm, hnT, hnTb, yT, zT):
    D, T, KC, GP = cfg.D, cfg.T, cfg.KC, cfg.GP
    TWO_PI = 2.0 * math.pi
    TB = 512
    k.begin_stage()
    st = common_scratch(k, TB)
    gcol = k.sbuf("gcol", [128, KC], F32)
    k.dma("sp", gcol.t[:, :], gmix.t[:, 1, :], gmix, gcol)
    hblk = k.sbuf("hblk", [128, KC, TB], F32)
    xin = k.sbuf("xin", [128, KC, TB], BF16)
    hn = k.sbuf("hn", [128, KC, TB], F32)
    for (t0, n) in token_blocks(T, TB):
        load_block_fm(k, hblk, hT, KC, t0, n)
        rmsnorm_block(k, st, hblk, xin, KC, n, gcol, cfg.EPS)
        for c in range(KC):
            k.op("dve", lambda e, c=c: e.scalar_tensor_tensor(out=hn.t[:, c, 0:n], in0=hblk.t[:, c, 0:n], scalar=gcol.t[:, c:c + 1],
                                                               in1=st["rstd"].t[:, 0:n], op0=ALU.mult, op1=ALU.mult),
                 reads=[hblk, gcol, st["rstd"]], writes=[hn])
        for c0 in range(0, KC, 8):
            c1 = min(KC, c0 + 8)
            k.dma("sp", hnT.t[c0 * 128:c1 * 128, t0:t0 + n].rearrange("(kc p) t -> p kc t", p=128), hn.t[:, c0:c1, 0:n], hn, hnT)
            k.dma("sp", hnTb.t[c0 * 128:c1 * 128, t0:t0 + n].rearrange("(kc p) t -> p kc t", p=128), xin.t[:, c0:c1, 0:n], xin, hnTb)
    k.end_stage()
    k.begin_stage()
    ld = lambda name, shape: (lambda b: (k.dma("sp", b.t[tuple(slice(None) for _ in shape)], prm[name].t[tuple(slice(None) for _ in shape)], prm[name], b), b)[1])(k.sbuf(name, shape, F32))
    ar, ai, ldt = ld("ar", [128, GP]), ld("ai", [128, GP]), ld("ldt", [128, GP])
    br, bi = ld("br", [128, GP, 16]), ld("bi", [128, GP, 16])
    cr, ci = ld("cr", [128, GP, 16]), ld("ci", [128, GP, 16])
    m01 = ld("m01", [128, 2])
    idf = ld("identf", [128, 128])
    ident = k.sbuf("identb", [128, 128], BF16)
    k.op("dve", lambda e: e.tensor_copy(ident.t[:, :], idf.t[:, :]), reads=[idf], writes=[ident])
    S = lambda name: k.sbuf(name, [128, GP], F32)
    dt, mag, ang, kk, tmp, sn_, cs_, abr, abi, den, fr, fi, t1, t2 = [S("s5_%d" % i) for i in range(14)]
    tt = lambda out, a, b, op: k.op("dve", lambda e: e.tensor_tensor(out=out.t[:, :], in0=a.t[:, :], in1=b.t[:, :], op=op), reads=[a, b], writes=[out])
    k.op("act", lambda e: e.activation(out=dt.t[:, :], in_=ldt.t[:, :], func=AF.Exp), reads=[ldt], writes=[dt])
    tt(mag, dt, ar, ALU.mult)
    k.op("act", lambda e: e.activation(out=mag.t[:, :], in_=mag.t[:, :], func=AF.Exp), reads=[mag], writes=[mag])
    tt(ang, dt, ai, ALU.mult)

    def sin_of(out, src, shift):
        k.op("dve", lambda e: e.tensor_scalar(out=tmp.t[:, :], in0=src.t[:, :], scalar1=float(shift), scalar2=None, op0=ALU.add), reads=[src], writes=[tmp])
        k.op("dve", lambda e: e.memset(kk.t[:, :], 0.0), writes=[kk])
        for m_ in range(1, 9):
            k.op("dve", lambda e, m_=m_: e.tensor_scalar(out=t1.t[:, :], in0=tmp.t[:, :], scalar1=float((2 * m_ - 1) * math.pi), scalar2=TWO_PI,
                                                          op0=ALU.is_ge, op1=ALU.mult), reads=[tmp], writes=[t1])
            tt(kk, kk, t1, ALU.add)
        tt(tmp, tmp, kk, ALU.subtract)
        k.op("dve", lambda e: e.tensor_scalar(out=tmp.t[:, :], in0=tmp.t[:, :], scalar1=-math.pi, scalar2=math.pi, op0=ALU.max, op1=ALU.min), reads=[tmp], writes=[tmp])
        k.op("act", lambda e: e.activation(out=out.t[:, :], in_=tmp.t[:, :], func=AF.Sin), reads=[tmp], writes=[out])
    sin_of(sn_, ang, 0.0)
    sin_of(cs_, ang, math.pi / 2)
    tt(abr, mag, cs_, ALU.mult)
    tt(abi, mag, sn_, ALU.mult)
    tt(t1, ar, ar, ALU.mult)
    tt(t2, ai, ai, ALU.mult)
    tt(den, t1, t2, ALU.add)
    k.op("dve", lambda e: e.reciprocal(out=den.t[:, :], in_=den.t[:, :]), reads=[den], writes=[den])
    k.op("dve", lambda e: e.tensor_scalar(out=tmp.t[:, :], in0=abr.t[:, :], scalar1=-1.0, scalar2=None, op0=ALU.add), reads=[abr], writes=[tmp])
    tt(t1, tmp, ar, ALU.mult)
    tt(t2, abi, ai, ALU.mult)
    tt(fr, t1, t2, ALU.add)
    tt(fr, fr, den, ALU.mult)
    tt(t1, abi, ar, ALU.mult)
    tt(t2, tmp, ai, ALU.mult)
    tt(fi, t1, t2, ALU.subtract)
    tt(fi, fi, den, ALU.mult)
    k.dma("sp", prm["abar_s"].t[:, 0, :], abr.t[:, :], abr, prm["abar_s"])
    k.dma("sp", prm["abar_s"].t[:, 1, :], abi.t[:, :], abi, prm["abar_s"])
    bbr = k.sbuf("bbr", [128, GP, 16], F32)
    bbi = k.sbuf("bbi", [128, GP, 16], F32)
    u1 = k.sbuf("u1", [128, GP], F32)
    u2 = k.sbuf("u2", [128, GP], F32)
    for i in range(16):
        for (dst, a, fa, b_, fb, op) in ((bbr, br, fr, bi, fi, ALU.subtract), (bbi, bi, fr, br, fi, ALU.add)):
            k.op("dve", lambda e, a=a, fa=fa, i=i: e.tensor_tensor(out=u1.t[:, :], in0=a.t[:, :, i], in1=fa.t[:, :], op=ALU.mult), reads=[a, fa], writes=[u1])
            k.op("dve", lambda e, b_=b_, fb=fb, i=i: e.tensor_tensor(out=u2.t[:, :], in0=b_.t[:, :, i], in1=fb.t[:, :], op=ALU.mult), reads=[b_, fb], writes=[u2])
            k.op("dve", lambda e, dst=dst, i=i, op=op: e.tensor_tensor(out=dst.t[:, :, i], in0=u1.t[:, :], in1=u2.t[:, :], op=op), reads=[u1, u2], writes=[dst])
    MB = [k.sbuf("MB%d" % r, [128, GP, 2, 16], BF16) for r in range(2)]
    LCs = [k.sbuf("LC%d" % r, [128, GP, 2, 16], BF16) for r in range(2)]
    nm01 = k.sbuf("nm01", [128, 2], F32)
    k.op("dve", lambda e: e.tensor_scalar(out=nm01.t[:, :], in0=m01.t[:, :], scalar1=-1.0, scalar2=None, op0=ALU.mult), reads=[m01], writes=[nm01])
    for r, (bsrc, csrc, msk) in enumerate(((bbr, cr, m01), (bbi, ci, nm01))):
        for g2 in range(2):
            k.op("dve", lambda e, r=r, g2=g2, bsrc=bsrc: e.tensor_scalar(out=MB[r].t[:, :, g2, :], in0=bsrc.t[:, :, :], scalar1=m01.t[:, g2:g2 + 1], scalar2=None, op0=ALU.mult),
                 reads=[bsrc, m01], writes=[MB[r]])
            k.op("dve", lambda e, r=r, g2=g2, csrc=csrc, msk=msk: e.tensor_scalar(out=LCs[r].t[:, :, g2, :], in0=csrc.t[:, :, :], scalar1=msk.t[:, g2:g2 + 1], scalar2=None, op0=ALU.mult),
                 reads=[csrc, msk], writes=[LCs[r]])
        k.dma("sp", prm["lc_s"].t[r], LCs[r].t[:, :, :, :], LCs[r], prm["lc_s"])
    ptr = k.psum("ptr", [128, 1024], BF16)
    lbo = Rot([k.sbuf("lbo%d" % i, [32, 8, 128], BF16) for i in range(2)])
    for r in range(2):
        for g0 in range(0, GP, 8):
            ng = min(8, GP - g0)

            def tr(e, r=r, g0=g0, ng=ng):
                ins = None
                for j in range(ng):
                    ins = e.transpose(ptr.t[0:32, j * 128:(j + 1) * 128], MB[r].t[:, g0 + j, :, :].rearrange("p a b -> p (a b)"), ident.t[:, :])
                return ins
            k.op("pe", tr, reads=[MB[r], ident], writes=[ptr])
            o = lbo.next()
            k.op("act", lambda e, o=o, ng=ng: e.activation(out=o.t[:, 0:ng, :], in_=ptr.t[0:32, 0:ng * 128].rearrange("p (a b) -> p a b", b=128), func=AF.Copy),
                 reads=[ptr], writes=[o])
            k.dma("sp", prm["lb_s"].t[r, :, g0:g0 + ng, :], o.t[:, 0:ng, :], o, prm["lb_s"])
    k.end_stage()
    TBs = 16
    k.begin_stage()
    LB = [k.sbuf("LB%d" % r, [32, GP, 128], BF16) for r in range(2)]
    LC = [k.sbuf("LCm%d" % r, [128, GP, 32], BF16) for r in range(2)]
    for r in range(2):
        k.dma("sp", LB[r].t[:, :, :], prm["lb_s"].t[r], prm["lb_s"], LB[r])
        k.dma("sp", LC[r].t[:, :, :], prm["lc_s"].t[r].rearrange("p g a b -> p g (a b)"), prm["lc_s"], LC[r])
    AS = k.sbuf("AS", [128, 2, GP], F32)
    AC = k.sbuf("AC", [128, 2, GP], F32)
    k.dma("sp", AS.t[:, 0, :], prm["abar_s"].t[:, 0, :], prm["abar_s"], AS)
    k.dma("sp", AS.t[:, 1, :], prm["abar_s"].t[:, 0, :], prm["abar_s"], AS)
    k.dma("sp", AC.t[:, 1, :], prm["abar_s"].t[:, 1, :], prm["abar_s"], AC)
    k.dma("sp", AC.t[:, 0, :], prm["abar_s"].t[:, 1, :], prm["abar_s"], AC)
    k.op("dve", lambda e: e.tensor_scalar(out=AC.t[:, 0, :], in0=AC.t[:, 0, :], scalar1=-1.0, scalar2=None, op0=ALU.mult), reads=[AC], writes=[AC])
    X = k.sbuf("X", [128, 2, GP, TBs + 1], F32)
    k.op("dve", lambda e: e.memset(X.t[:, :, :, :], 0.0), writes=[X])
    BU = k.sbuf("BU", [128, 2, GP, TBs], F32)
    XB = k.sbuf("XB", [128, 2, GP, TBs], BF16)
    P_ = k.sbuf("P_", [128, 2, GP], F32)
    Q_ = k.sbuf("Q_", [128, 2, GP], F32)
    x32 = Rot([k.sbuf("x32_%d" % i, [32, GP, TBs], BF16) for i in range(2)])
    y32 = Rot([k.sbuf("y32_%d" % i, [32, GP, TBs], F32) for i in range(2)])
    psb = Rot([k.psum("psb%d" % i, [128, 512]) for i in range(4)])
    psy = Rot([k.psum("psy%d" % i, [128, 512]) for i in range(2)])
    per = 512 // TBs
    for (t0, n) in token_blocks(T, TBs):
        xi = x32.next()
        k.dma("sp", xi.t[:, :, 0:n], hnTb.t[:, t0:t0 + n].rearrange("(g r) t -> r g t", r=32), hnTb, xi)
        for r in range(2):
            for g0 in range(0, GP, per):
                ng = min(per, GP - g0)
                ps = psb.next()

                def mm(e, r=r, g0=g0, ng=ng, ps=ps):
                    ins = None
                    for j in range(ng):
                        ins = e.matmul(ps.t[:, j * TBs:j * TBs + n], lhsT=LB[r].t[:, g0 + j, :], rhs=xi.t[:, g0 + j, 0:n], start=True, stop=True)
                    return ins
                k.op("pe", mm, reads=[LB[r], xi], writes=[ps])
                k.op("act", lambda e, r=r, g0=g0, ng=ng, ps=ps: e.activation(
                    out=BU.t[:, r, g0:g0 + ng, 0:n], in_=ps.t[:, 0:ng * TBs].rearrange("p (g t) -> p g t", t=TBs)[:, :, 0:n], func=AF.Copy),
                    reads=[ps], writes=[BU])
        for s in range(n):
            k.op("dve", lambda e, s=s: e.tensor_tensor(out=P_.t[:, :, :], in0=X.t[:, :, :, s], in1=AS.t[:, :, :], op=ALU.mult), reads=[X, AS], writes=[P_])
            for r in range(2):
                k.op("dve", lambda e, s=s, r=r: e.tensor_tensor(out=Q_.t[:, r, :], in0=X.t[:, 1 - r, :, s], in1=AC.t[:, r, :], op=ALU.mult), reads=[X, AC], writes=[Q_])
            k.op("dve", lambda e: e.tensor_tensor(out=P_.t[:, :, :], in0=P_.t[:, :, :], in1=Q_.t[:, :, :], op=ALU.add), reads=[P_, Q_], writes=[P_])
            k.op("dve", lambda e, s=s: e.tensor_tensor(out=X.t[:, :, :, s + 1], in0=P_.t[:, :, :], in1=BU.t[:, :, :, s], op=ALU.add), reads=[P_, BU], writes=[X])
        k.op("dve", lambda e: e.tensor_copy(XB.t[:, :, :, 0:n], X.t[:, :, :, 1:n + 1]), reads=[X], writes=[XB])
        yo = y32.next()
        for g0 in range(0, GP, per):
            ng = min(per, GP - g0)
            ps = psy.next()

            def mm(e, g0=g0, ng=ng, ps=ps):
                ins = None
                for j in range(ng):
                    e.matmul(ps.t[0:32, j * TBs:j * TBs + n], lhsT=LC[0].t[:, g0 + j, :], rhs=XB.t[:, 0, g0 + j, 0:n], start=True, stop=False)
                    ins = e.matmul(ps.t[0:32, j * TBs:j * TBs + n], lhsT=LC[1].t[:, g0 + j, :], rhs=XB.t[:, 1, g0 + j, 0:n], start=False, stop=True)
                return ins
            k.op("pe", mm, reads=[LC[0], LC[1], XB], writes=[ps])
            k.op("act", lambda e, g0=g0, ng=ng, ps=ps: e.activation(
                out=yo.t[:, g0:g0 + ng, 0:n], in_=ps.t[0:32, 0:ng * TBs].rearrange("p (g t) -> p g t", t=TBs)[:, :, 0:n], func=AF.Copy),
                reads=[ps], writes=[yo])
        k.dma("sp", yT.t[:, t0:t0 + n].rearrange("(g r) t -> r g t", r=32), yo.t[:, :, 0:n], yo, yT)
        k.op("dve", lambda e: e.tensor_copy(X.t[:, :, :, 0], X.t[:, :, :, n]), reads=[X], writes=[X])
    k.end_stage()
    k.begin_stage()
    dcol = k.sbuf("dcol", [128, KC], F32)
    k.dma("sp", dcol.t[:, :], prm["dcol"].t[:, :], prm["dcol"], dcol)
    yb = Rot([k.sbuf("yb%d" % i, [128, TB], F32) for i in range(2)])
    hb = Rot([k.sbuf("hb%d" % i, [128, TB], F32) for i in range(2)])
    v1 = k.sbuf("v1", [128, TB], F32)
    v2 = k.sbuf("v2", [128, TB], F32)
    zb = Rot([k.sbuf("zb%d" % i, [128, TB], BF16) for i in range(2)])
    C0 = math.sqrt(2.0 / math.pi)
    for (t0, n) in token_blocks(T, TB):
        for c in range(KC):
            y_, h_ = yb.next(), hb.next()
            k.dma("sp", y_.t[:, 0:n], yT.t[c * 128:(c + 1) * 128, t0:t0 + n], yT, y_)
            k.dma("sp", h_.t[:, 0:n], hnT.t[c * 128:(c + 1) * 128, t0:t0 + n], hnT, h_)
            k.op("dve", lambda e: e.scalar_tensor_tensor(out=v1.t[:, 0:n], in0=h_.t[:, 0:n], scalar=dcol.t[:, c:c + 1], in1=y_.t[:, 0:n],
                                                          op0=ALU.mult, op1=ALU.add), reads=[h_, dcol, y_], writes=[v1])
            k.op("act", lambda e: e.activation(out=v2.t[:, 0:n], in_=v1.t[:, 0:n], func=AF.Square), reads=[v1], writes=[v2])
            k.op("dve", lambda e: e.tensor_scalar(out=v2.t[:, 0:n], in0=v2.t[:, 0:n], scalar1=0.044715, scalar2=1.0, op0=ALU.mult, op1=ALU.add), reads=[v2], writes=[v2])
            k.op("dve", lambda e: e.tensor_tensor(out=v2.t[:, 0:n], in0=v2.t[:, 0:n], in1=v1.t[:, 0:n], op=ALU.mult), reads=[v2, v1], writes=[v2])
            k.op("act", lambda e: e.activation(out=v2.t[:, 0:n], in_=v2.t[:, 0:n], func=AF.Sigmoid, scale=2.0 * C0), reads=[v2], writes=[v2])
            z_ = zb.next()
            k.op("dve", lambda e: e.tensor_tensor(out=z_.t[:, 0:n], in0=v2.t[:, 0:n], in1=v1.t[:, 0:n], op=ALU.mult), reads=[v2, v1], writes=[z_])
            k.dma("sp", zT.t[c * 128:(c + 1) * 128, t0:t0 + n], z_.t[:, 0:n], z_, zT)
    k.end_stage()
```

```python
import math
from contextlib import ExitStack

import numpy as np
import ml_dtypes

import concourse.bass as bass
import concourse.mybir as mybir
from concourse.bass_utils import run_bass_kernel_spmd

F32 = mybir.dt.float32
BF16 = mybir.dt.bfloat16
AF = mybir.ActivationFunctionType
ALU = mybir.AluOpType


class Buf:
    def __init__(self, k, name, t, accumulate=False):
        self.k = k
        self.name = name
        self.t = t
        self.acc = accumulate
        self.writers = {}
        self.readers = {}
        self._in = None
        self._out = None

    def __getitem__(self, idx):
        return self.t[idx]

    def bump_in(self):
        if self._in is None:
            self._in = self.k.take_dsem()
        self._in[1] += 16
        return (self._in[0], self._in[1])

    def bump_out(self):
        if self._out is None:
            self._out = self.k.take_dsem()
        self._out[1] += 16
        return (self._out[0], self._out[1])


class KB:
    def __init__(self):
        self.nc = bass.Bass("TRN2", target_bir_lowering=False)
        self.es = ExitStack()
        self.nsem = 0
        nc = self.nc
        self.eng = {"pe": nc.tensor, "act": nc.scalar, "dve": nc.vector, "pool": nc.gpsimd, "sp": nc.sync}
        self.esem = {e: self.new_sem("e_" + e) for e in ("pe", "act", "dve", "pool")}
        self.ecnt = {e: 0 for e in self.esem}
        self.seen = {e: {} for e in self.eng}
        self.uid = 0

    def new_sem(self, name):
        self.nsem += 1
        return self.es.enter_context(self.nc.semaphore(name))

    def take_dsem(self):
        if not hasattr(self, "dpool"):
            self.dpool = []
            self.allsems = []
        if self.dpool:
            return self.dpool.pop()
        ent = [self.new_sem("d%d" % self.nsem), 0]
        self.allsems.append(ent)
        return ent

    def begin_stage(self):
        self.stage_es = ExitStack()
        self.stage_bufs = []

    def end_stage(self):
        for e in ("pe", "act", "dve", "pool", "sp"):
            for e2 in self.esem:
                if self.ecnt[e2] > 0:
                    self._wait(e, self.esem[e2], self.ecnt[e2])
            for ent in getattr(self, "allsems", []):
                if ent[1] > 0:
                    self._wait(e, ent[0], ent[1])
        for b in self.stage_bufs:
            for ent in (b._in, b._out):
                if ent is not None:
                    self.dpool.append(ent)
        self.stage_es.close()
        self.stage_es = None

    def dram(self, name, shape, dtype, kind="Internal"):
        t = self.nc.dram_tensor(name, list(shape), dtype, kind=kind)
        return Buf(self, name, t.ap(), accumulate=True)

    def sbuf(self, name, shape, dtype):
        st = getattr(self, "stage_es", None)
        self.uid += 1
        t = (st or self.es).enter_context(self.nc.sbuf_tensor("%s_%d" % (name, self.uid), list(shape), dtype))
        b = Buf(self, name, t)
        if st is not None:
            self.stage_bufs.append(b)
        return b

    def psum(self, name, shape, dtype=F32):
        st = getattr(self, "stage_es", None)
        self.uid += 1
        t = (st or self.es).enter_context(self.nc.psum_tensor("%s_%d" % (name, self.uid), list(shape), dtype))
        b = Buf(self, name, t)
        if st is not None:
            self.stage_bufs.append(b)
        return b

    def _wait(self, e, sem, val):
        sid = id(sem)
        if self.seen[e].get(sid, 0) >= val:
            return
        self.seen[e][sid] = val
        self.eng[e].wait_ge(sem, val)

    def _deps(self, e, reads, writes):
        for b in reads:
            for sem, val in b.writers.values():
                self._wait(e, sem, val)
        for b in writes:
            for sem, val in b.writers.values():
                if not b.acc:
                    self._wait(e, sem, val)
            for sem, val in b.readers.values():
                self._wait(e, sem, val)

    def _record(self, ev, reads, writes):
        sem, val = ev
        for b in reads:
            b.readers[id(sem)] = ev
        for b in writes:
            if b.acc:
                b.writers[id(sem)] = ev
            else:
                b.writers = {id(sem): ev}
                b.readers = {}

    def op(self, e, fn, reads=(), writes=()):
        self._deps(e, reads, writes)
        ins = fn(self.eng[e])
        self.ecnt[e] += 1
        ins.then_inc(self.esem[e], 1)
        self._record((self.esem[e], self.ecnt[e]), reads, writes)

    def dma(self, q, out_ap, in_ap, src, dst, **kw):
        self._deps(q, [src], [dst])
        ins = self.eng[q].dma_start(out=out_ap, in_=in_ap, **kw)
        if not dst.acc or src.acc:
            ev = dst.bump_in()
        else:
            ev = src.bump_out()
        ins.then_inc(ev[0], 16)
        self._record(ev, [src], [dst])

    def finish(self, outs):
        for b in outs:
            for sem, val in b.writers.values():
                self._wait("sp", sem, val)
        self.es.close()
        return self.nc


class Rot:
    def __init__(self, bufs):
        self.bufs = bufs
        self.i = 0

    def next(self):
        b = self.bufs[self.i % len(self.bufs)]
        self.i += 1
        return b


def token_blocks(T, TB):
    out = []
    t0 = 0
    while t0 < T:
        n = min(TB, T - t0)
        out.append((t0, n))
        t0 += n
    return out


def load_block_fm(k, dst, srcT, KC, t0, n):
    parts = srcT if isinstance(srcT, list) else [srcT]
    kpp = KC // len(parts)
    half = min(8, kpp)
    for pi, part in enumerate(parts):
        view = part.t.rearrange("(kc p) t -> p kc t", p=128)
        for a in range(0, kpp, half):
            b = min(kpp, a + half)
            k.dma("sp", dst.t[:, pi * kpp + a:pi * kpp + b, 0:n], view[:, a:b, t0:t0 + n], part, dst)


def rmsnorm_block(k, st, hblk, out_bf, KC, n, gcol, eps, want_rstd=None):
    D = KC * 128
    ps = st["ps_stat"]
    sq = st["sq"]

    for c in range(KC):
        s = sq.next()
        k.op("act", lambda e, s=s, c=c: e.activation(out=s.t[:, 0:n], in_=hblk.t[:, c, 0:n], func=AF.Square),
             reads=[hblk], writes=[s])
        last = (c == KC - 1)
        k.op("pe", lambda e, s=s, c=c: e.matmul(ps.t[:, 0:n], lhsT=st["ones"].t[:, :], rhs=s.t[:, 0:n],
                                                  start=(c == 0), stop=(c == KC - 1)),
             reads=[s, st["ones"]], writes=[ps] if c == 0 else [])
        if last:
            k._record((k.esem["pe"], k.ecnt["pe"]), [], [ps])
    rstd = st["rstd"]
    k.op("act", lambda e: e.activation(out=rstd.t[:, 0:n], in_=ps.t[:, 0:n], func=AF.Sqrt,
                                       bias=st["epsb"].t[:, 0:1], scale=1.0 / D),
         reads=[ps, st["epsb"]], writes=[rstd])
    k.op("dve", lambda e: e.reciprocal(out=rstd.t[:, 0:n], in_=rstd.t[:, 0:n]), reads=[rstd], writes=[rstd])
    for c in range(KC):
        k.op("dve", lambda e, c=c: e.scalar_tensor_tensor(out=out_bf.t[:, c, 0:n], in0=hblk.t[:, c, 0:n],
                                                          scalar=gcol.t[:, c:c + 1], in1=rstd.t[:, 0:n],
                                                          op0=ALU.mult, op1=ALU.mult),
             reads=[hblk, gcol, rstd], writes=[out_bf])
    if want_rstd is not None:
        k.op("act", lambda e: e.activation(out=want_rstd.t[:, 0:n], in_=rstd.t[:, 0:n], func=AF.Copy),
             reads=[rstd], writes=[want_rstd])


def fm_linear(k, xin, KC, n, w_dram, groups, wbufs, psums, epilogue):
    for gi, grp in enumerate(groups):
        pss = []
        for ch in grp:
            w = wbufs.next()
            k.dma("sp", w.t[:, :, :], w_dram.t[ch], w_dram, w)
            ps = psums.next()

            def mm(e, w=w, ps=ps):
                ins = None
                for kc in range(KC):
                    ins = e.matmul(ps.t[:, 0:n], lhsT=w.t[:, kc, :], rhs=xin.t[:, kc, 0:n],
                                   start=(kc == 0), stop=(kc == KC - 1))
                return ins
            k.op("pe", mm, reads=[w, xin], writes=[ps])
            pss.append(ps)
        epilogue(gi, pss)


class Cfg:
    def __init__(self, D=4096, SEQ=8192, H=16, NMETA=16):
        self.D, self.SEQ, self.H, self.NMETA = D, SEQ, H, NMETA
        self.T = SEQ + NMETA
        self.KC = D // 128
        self.NSH = 2 * H
        self.FF = 4 * D
        self.FC = self.FF // 128
        self.G = D // 16
        self.GP = self.G // 2
        self.EPS = 1e-6


def common_scratch(k, TB):
    ones = k.sbuf("ones", [128, 128], F32)
    epsb = k.sbuf("epsb", [128, 1], F32)
    k.op("dve", lambda e: e.memset(ones.t[:, :], 1.0), writes=[ones])
    k.op("dve", lambda e: e.memset(epsb.t[:, :], 1e-6), writes=[epsb])
    return dict(ones=ones, epsb=epsb, ps_stat=k.psum("ps_stat", [128, 512]),
                rstd=k.sbuf("rstd", [128, TB], F32),
                sq=Rot([k.sbuf("sq%d" % i, [128, TB], F32) for i in range(2)]))


def cast_weights(k, wf, wb, nchunks):
    for c in range(nchunks):
        k.dma("pool", wb.t[c], wf.t[c], wf, wb)


def stage_qkv(k, cfg, hT, gmix, wqk, wv, gqk, ropeC, ropeS, qk_dram, v_dram):
    TB = 512
    KC, T, NSH = cfg.KC, cfg.T, cfg.NSH
    k.begin_stage()
    st = common_scratch(k, TB)
    gcol = k.sbuf("gcol", [128, KC], F32)
    k.dma("sp", gcol.t[:, :], gmix.t[:, 0, :], gmix, gcol)
    gq = k.sbuf("gq", [128, 4], F32)
    k.dma("sp", gq.t[:, :], gqk.t[:, :], gqk, gq)
    bones = k.sbuf("bones", [128, 128], F32)
    k.op("dve", lambda e: e.memset(bones.t[:, :], 0.0), writes=[bones])
    k.op("dve", lambda e: e.memset(bones.t[0:64, 0:64], 1.0), reads=[bones], writes=[bones])
    k.op("dve", lambda e: e.memset(bones.t[64:128, 64:128], 1.0), reads=[bones], writes=[bones])
    hblk = k.sbuf("hblk", [128, KC, TB], F32)
    xin = k.sbuf("xin", [128, KC, TB], BF16)
    cs = k.sbuf("cs", [128, TB], F32)
    sn = k.sbuf("sn", [128, TB], F32)
    wbufs = Rot([k.sbuf("w%d" % i, [128, KC, 128], BF16) for i in range(3)])
    wvb = Rot([k.sbuf("wv%d" % i, [128, KC, 512], BF16) for i in range(1)])
    psums = Rot([k.psum("ps%d" % i, [128, 512]) for i in range(4)])
    psq = k.psum("psq", [128, 512])
    sqa = k.sbuf("sqa", [128, TB], F32)
    sqb = k.sbuf("sqb", [128, TB], F32)
    rs = k.sbuf("rs", [128, TB], F32)
    qa = k.sbuf("qa", [128, TB], F32)
    qb = k.sbuf("qb", [128, TB], F32)
    t1 = k.sbuf("t1", [128, TB], F32)
    t2 = k.sbuf("t2", [128, TB], F32)
    oa = Rot([k.sbuf("oa%d" % i, [128, TB], BF16) for i in range(2)])
    ob = Rot([k.sbuf("ob%d" % i, [128, TB], BF16) for i in range(2)])
    vo = Rot([k.sbuf("vo%d" % i, [128, 512], BF16) for i in range(2)])
    for (t0, n) in token_blocks(T, TB):
        load_block_fm(k, hblk, hT, KC, t0, n)
        k.dma("sp", cs.t[:, 0:n], ropeC.t[:, t0:t0 + n], ropeC, cs)
        k.dma("sp", sn.t[:, 0:n], ropeS.t[:, t0:t0 + n], ropeS, sn)
        rmsnorm_block(k, st, hblk, xin, KC, n, gcol, cfg.EPS)

        def epi(gi, pss, t0=t0, n=n):
            A, B = pss
            isk = 1 if gi >= NSH // 2 else 0
            j = gi - isk * (NSH // 2)
            k.op("act", lambda e: e.activation(out=sqa.t[:, 0:n], in_=A.t[:, 0:n], func=AF.Square), reads=[A], writes=[sqa])
            k.op("act", lambda e: e.activation(out=sqb.t[:, 0:n], in_=B.t[:, 0:n], func=AF.Square), reads=[B], writes=[sqb])

            def mm(e):
                e.matmul(psq.t[:, 0:n], lhsT=bones.t[:, :], rhs=sqa.t[:, 0:n], start=True, stop=False)
                return e.matmul(psq.t[:, 0:n], lhsT=bones.t[:, :], rhs=sqb.t[:, 0:n], start=False, stop=True)
            k.op("pe", mm, reads=[bones, sqa, sqb], writes=[psq])
            k.op("act", lambda e: e.activation(out=rs.t[:, 0:n], in_=psq.t[:, 0:n], func=AF.Sqrt,
                                               bias=st["epsb"].t[:, 0:1], scale=1.0 / 128), reads=[psq, st["epsb"]], writes=[rs])
            k.op("dve", lambda e: e.reciprocal(out=rs.t[:, 0:n], in_=rs.t[:, 0:n]), reads=[rs], writes=[rs])
            k.op("dve", lambda e: e.scalar_tensor_tensor(out=qa.t[:, 0:n], in0=A.t[:, 0:n], scalar=gq.t[:, 2 * isk:2 * isk + 1],
                                                          in1=rs.t[:, 0:n], op0=ALU.mult, op1=ALU.mult), reads=[A, gq, rs], writes=[qa])
            k.op("dve", lambda e: e.scalar_tensor_tensor(out=qb.t[:, 0:n], in0=B.t[:, 0:n], scalar=gq.t[:, 2 * isk + 1:2 * isk + 2],
                                                          in1=rs.t[:, 0:n], op0=ALU.mult, op1=ALU.mult), reads=[B, gq, rs], writes=[qb])
            o_a, o_b = oa.next(), ob.next()
            k.op("dve", lambda e: e.tensor_tensor(out=t1.t[:, 0:n], in0=qa.t[:, 0:n], in1=cs.t[:, 0:n], op=ALU.mult), reads=[qa, cs], writes=[t1])
            k.op("pool", lambda e: e.tensor_tensor(out=t2.t[:, 0:n], in0=qb.t[:, 0:n], in1=sn.t[:, 0:n], op=ALU.mult), reads=[qb, sn], writes=[t2])
            k.op("dve", lambda e: e.tensor_tensor(out=o_a.t[:, 0:n], in0=t1.t[:, 0:n], in1=t2.t[:, 0:n], op=ALU.subtract), reads=[t1, t2], writes=[o_a])
            k.op("dve", lambda e: e.tensor_tensor(out=t1.t[:, 0:n], in0=qb.t[:, 0:n], in1=cs.t[:, 0:n], op=ALU.mult), reads=[qb, cs], writes=[t1])
            k.op("pool", lambda e: e.tensor_tensor(out=t2.t[:, 0:n], in0=qa.t[:, 0:n], in1=sn.t[:, 0:n], op=ALU.mult), reads=[qa, sn], writes=[t2])
            k.op("dve", lambda e: e.tensor_tensor(out=o_b.t[:, 0:n], in0=t1.t[:, 0:n], in1=t2.t[:, 0:n], op=ALU.add), reads=[t1, t2], writes=[o_b])
            s0 = isk * NSH + 2 * j
            k.dma("pool", qk_dram.t[s0, 0:64, t0:t0 + n], o_a.t[0:64, 0:n], o_a, qk_dram)
            k.dma("pool", qk_dram.t[s0 + 1, 0:64, t0:t0 + n], o_a.t[64:128, 0:n], o_a, qk_dram)
            k.dma("pool", qk_dram.t[s0, 64:128, t0:t0 + n], o_b.t[0:64, 0:n], o_b, qk_dram)
            k.dma("pool", qk_dram.t[s0 + 1, 64:128, t0:t0 + n], o_b.t[64:128, 0:n], o_b, qk_dram)
        fm_linear(k, xin, KC, n, wqk, [[2 * p, 2 * p + 1] for p in range(NSH)], wbufs, psums, epi)
        for sl in range(cfg.D // 512):
            w = wvb.next()
            k.dma("sp", w.t[:, :, :], wv.t[sl], wv, w)
            for (tt0, tn) in token_blocks(n, 128):
                ps = psums.next()

                def mm(e, w=w, ps=ps, tt0=tt0, tn=tn):
                    ins = None
                    for kc in range(KC):
                        ins = e.matmul(ps.t[0:tn, 0:512], lhsT=xin.t[:, kc, tt0:tt0 + tn], rhs=w.t[:, kc, :],
                                       start=(kc == 0), stop=(kc == KC - 1))
                    return ins
                k.op("pe", mm, reads=[w, xin], writes=[ps])
                o = vo.next()
                k.op("act", lambda e, o=o, ps=ps, tn=tn: e.activation(out=o.t[0:tn, :], in_=ps.t[0:tn, 0:512], func=AF.Copy),
                     reads=[ps], writes=[o])
                k.dma("pool", v_dram.t[t0 + tt0:t0 + tt0 + tn, sl * 512:(sl + 1) * 512], o.t[0:tn, :], o, v_dram)
    k.end_stage()


def stage_attn(k, cfg, qk_dram, v_dram, lamv, subg, maskf, identf, oT, lam_init):
    T, NSH, H = cfg.T, cfg.NSH, cfg.H
    NT = (T + 127) // 128
    NTF = T // 128
    QB = 256
    k.begin_stage()
    mk32 = k.sbuf("mk32", [128, 128], F32)
    id32 = k.sbuf("id32", [128, 128], F32)
    mask = k.sbuf("mask", [128, 128], BF16)
    ident = k.sbuf("ident", [128, 128], BF16)
    k.dma("sp", mk32.t[:, :], maskf.t[:, :], maskf, mk32)
    k.dma("sp", id32.t[:, :], identf.t[:, :], identf, id32)
    k.op("dve", lambda e: e.tensor_copy(mask.t[:, :], mk32.t[:, :]), reads=[mk32], writes=[mask])
    k.op("dve", lambda e: e.tensor_copy(ident.t[:, :], id32.t[:, :]), reads=[id32], writes=[ident])
    sg = k.sbuf("sg", [128, 256], F32)
    k.dma("sp", sg.t[:, :], subg.t[:, :], subg, sg)
    lv = k.sbuf("lv", [128, 512], F32)
    k.dma("sp", lv.t[:, :], lamv.t[:, :], lamv, lv)
    lt = k.sbuf("lt", [128, 256], F32)
    l2 = k.sbuf("l2", [128, 2], F32)
    nlam = k.sbuf("nlam", [128, 1], F32)
    k.op("dve", lambda e: e.tensor_tensor(out=lt.t[:, 0:128], in0=lv.t[:, 0:128], in1=lv.t[:, 128:256], op=ALU.mult), reads=[lv], writes=[lt])
    k.op("dve", lambda e: e.tensor_tensor(out=lt.t[:, 128:256], in0=lv.t[:, 256:384], in1=lv.t[:, 384:512], op=ALU.mult), reads=[lv, lt], writes=[lt])
    k.op("dve", lambda e: e.tensor_reduce(out=l2.t[:, :], in_=lt.t[:, :].rearrange("p (a b) -> p a b", a=2), axis=mybir.AxisListType.X, op=ALU.add),
         reads=[lt], writes=[l2])
    k.op("act", lambda e: e.activation(out=l2.t[:, :], in_=l2.t[:, :], func=AF.Exp), reads=[l2], writes=[l2])
    k.op("dve", lambda e: e.scalar_tensor_tensor(out=nlam.t[:, :], in0=l2.t[:, 1:2], scalar=-float(lam_init), in1=l2.t[:, 0:1],
                                                  op0=ALU.add, op1=ALU.subtract), reads=[l2], writes=[nlam])
    qs_ = [k.sbuf("q%d" % c, [128, T], BF16) for c in range(2)]
    ks_ = [k.sbuf("k%d" % c, [128, T], BF16) for c in range(2)]
    vh = k.sbuf("vh", [128, NT, 257], BF16)
    k.op("dve", lambda e: e.memset(vh.t[:, :, 256:257], 1.0), writes=[vh])
    pss = Rot([k.psum("pss%d" % i, [128, 512]) for i in range(2)])
    accs = [[k.psum("acc%d%d" % (c, i), [128, 512]) for i in range(2)] for c in range(2)]
    ptr = k.psum("ptr", [128, 1024], BF16)
    pT = Rot([k.sbuf("pT%d" % i, [128, 2, QB], BF16) for i in range(4)])
    r0 = k.sbuf("r0", [128, 1], F32)
    r1 = k.sbuf("r1", [128, 1], F32)
    of = k.sbuf("of", [128, 256], F32)
    junk = k.sbuf("junk", [128, 256], F32)
    ss = k.sbuf("ss", [128, 1], F32)
    ob = k.sbuf("ob", [128, 256], BF16)
    oTs = Rot([k.sbuf("oTs%d" % i, [128, 2, 128], BF16) for i in range(2)])
    scale = 128.0 ** -0.5
    epsb_att = k.sbuf("epsb_att", [128, 1], F32)
    k.op("dve", lambda e: e.memset(epsb_att.t[:, :], 1e-6), writes=[epsb_att])
    for h in range(H):
        for c in range(2):
            k.dma("sp", qs_[c].t[:, :], qk_dram.t[2 * h + c], qk_dram, qs_[c])
            k.dma("sp", ks_[c].t[:, :], qk_dram.t[NSH + 2 * h + c], qk_dram, ks_[c])
        for i0 in range(0, NTF, 8):
            i1 = min(NTF, i0 + 8)
            k.dma("sp", vh.t[:, i0:i1, 0:256],
                  v_dram.t[i0 * 128:i1 * 128, h * 256:(h + 1) * 256].rearrange("(i p) e -> p i e", p=128), v_dram, vh)
        if NT > NTF:
            rem = T - NTF * 128
            k.dma("sp", vh.t[0:rem, NTF, 0:256], v_dram.t[NTF * 128:T, h * 256:(h + 1) * 256], v_dram, vh)
        for (q0, nq) in token_blocks(T, QB):
            tq0 = q0 // 128
            qtiles = token_blocks(nq, 128)
            last_kt = (q0 + nq - 1) // 128
            pend = None

            def emit_pv(p, kt, nk, qs):
                for qi, (qq0, nqt) in enumerate(qtiles):
                    tq = tq0 + qi
                    if tq < kt:
                        continue
                    off = 128 * tq - qs
                    for c in range(2):
                        acc = accs[c][qi]
                        k.op("pe", lambda e, acc=acc, c=c, off=off, nqt=nqt, tq=tq: e.matmul(
                            acc.t[0:nqt, 0:257], lhsT=p.t[0:nk, c, off:off + nqt], rhs=vh.t[0:nk, kt, :],
                            start=(kt == 0), stop=(kt == tq)),
                            reads=[p, vh], writes=[acc] if kt == 0 else [])
                        if kt == tq:
                            k._record((k.esem["pe"], k.ecnt["pe"]), [], [acc])
            for kt in range(last_kt + 1):
                nk = min(128, T - 128 * kt)
                qs = max(q0, 128 * kt)
                nqq = q0 + nq - qs
                ps = pss.next()

                def mm(e, ps=ps, kt=kt, nk=nk, qs=qs, nqq=nqq):
                    ins = None
                    for c in range(2):
                        ins = e.matmul(ps.t[0:nk, c * QB:c * QB + nqq], lhsT=ks_[c].t[:, 128 * kt:128 * kt + nk],
                                       rhs=qs_[c].t[:, qs:qs + nqq], start=True, stop=True)
                    return ins
                k.op("pe", mm, reads=[ks_[0], ks_[1], qs_[0], qs_[1]], writes=[ps])
                p = pT.next()
                k.op("act", lambda e, p=p, ps=ps, nk=nk, nqq=nqq: e.activation(
                    out=p.t[0:nk, :, 0:nqq], in_=ps.t[0:nk, :].rearrange("p (c q) -> p c q", c=2)[:, :, 0:nqq],
                    func=AF.Exp, scale=scale), reads=[ps], writes=[p])
                if 128 * kt >= q0:
                    w = min(128, nqq)
                    for c in range(2):
                        k.op("pool", lambda e, p=p, c=c, nk=nk, w=w: e.tensor_tensor(
                            out=p.t[0:nk, c, 0:w], in0=p.t[0:nk, c, 0:w], in1=mask.t[0:nk, 0:w], op=ALU.mult),
                            reads=[p, mask], writes=[p])
                if pend is not None:
                    emit_pv(*pend)
                pend = (p, kt, nk, qs)
            emit_pv(*pend)
            for qi, (qq0, nqt) in enumerate(qtiles):
                tq = tq0 + qi
                a0, a1 = accs[0][qi], accs[1][qi]
                k.op("dve", lambda e: e.reciprocal(out=r0.t[0:nqt, :], in_=a0.t[0:nqt, 256:257]), reads=[a0], writes=[r0])
                k.op("dve", lambda e: e.reciprocal(out=r1.t[0:nqt, :], in_=a1.t[0:nqt, 256:257]), reads=[a1], writes=[r1])
                k.op("dve", lambda e: e.tensor_tensor(out=r1.t[0:nqt, :], in0=r1.t[0:nqt, :], in1=nlam.t[0:nqt, :], op=ALU.mult),
                     reads=[r1, nlam], writes=[r1])
                k.op("dve", lambda e: e.tensor_scalar(out=of.t[0:nqt, :], in0=a0.t[0:nqt, 0:256], scalar1=r0.t[0:nqt, 0:1], scalar2=None,
                                                       op0=ALU.mult), reads=[a0, r0], writes=[of])
                k.op("dve", lambda e: e.scalar_tensor_tensor(out=of.t[0:nqt, :], in0=a1.t[0:nqt, 0:256], scalar=r1.t[0:nqt, 0:1],
                                                              in1=of.t[0:nqt, :], op0=ALU.mult, op1=ALU.add), reads=[a1, r1, of], writes=[of])
                k.op("act", lambda e: e.activation(out=junk.t[0:nqt, :], in_=of.t[0:nqt, :], func=AF.Square, accum_out=ss.t[0:nqt, :]),
                     reads=[of], writes=[junk, ss])
                k.op("act", lambda e: e.activation(out=ss.t[0:nqt, :], in_=ss.t[0:nqt, :], func=AF.Sqrt, bias=epsb_att.t[0:nqt, 0:1], scale=1.0 / 256),
                     reads=[ss, epsb_att], writes=[ss])
                k.op("dve", lambda e: e.reciprocal(out=ss.t[0:nqt, :], in_=ss.t[0:nqt, :]), reads=[ss], writes=[ss])
                k.op("dve", lambda e: e.tensor_scalar(out=ss.t[0:nqt, :], in0=ss.t[0:nqt, :], scalar1=float(1.0 - lam_init), scalar2=None, op0=ALU.mult),
                     reads=[ss], writes=[ss])
                k.op("dve", lambda e: e.scalar_tensor_tensor(out=ob.t[0:nqt, :], in0=of.t[0:nqt, :], scalar=ss.t[0:nqt, 0:1], in1=sg.t[0:nqt, :],
                                                              op0=ALU.mult, op1=ALU.mult), reads=[of, ss, sg], writes=[ob])

                def tr(e):
                    ins = None
                    for e2 in range(2):
                        ins = e.transpose(ptr.t[:, e2 * 128:e2 * 128 + nqt], ob.t[0:nqt, e2 * 128:(e2 + 1) * 128], ident.t[0:nqt, 0:nqt])
                    return ins
                k.op("pe", tr, reads=[ob, ident], writes=[ptr])
                o = oTs.next()
                k.op("act", lambda e, o=o: e.activation(out=o.t[:, :, 0:nqt], in_=ptr.t[:, 0:256].rearrange("p (a b) -> p a b", a=2)[:, :, 0:nqt],
                                                        func=AF.Copy), reads=[ptr], writes=[o])
                k.dma("pool", oT.t[h * 256:(h + 1) * 256, 128 * tq:128 * tq + nqt].rearrange("(a p) t -> p a t", p=128),
                      o.t[:, :, 0:nqt], o, oT)
    k.end_stage()


def stage_linear_res(k, cfg, name, xT_bf, KCin, w, nout, resT, outT, TB, bias=None, glu=False):
    T = cfg.T
    k.begin_stage()
    xin = k.sbuf("xin", [128, KCin, TB], BF16)
    wbufs = Rot([k.sbuf("w%d" % i, [128, KCin, 128], BF16) for i in range(3)])
    psums = Rot([k.psum("ps%d" % i, [128, 512]) for i in range(4)])
    rbuf = Rot([k.sbuf("r%d" % i, [128, TB], F32) for i in range(2)])
    obuf = Rot([k.sbuf("o%d" % i, [128, TB], F32) for i in range(2)])
    sgb = k.sbuf("sgb", [128, TB], F32)
    if bias is not None:
        bcol = k.sbuf("bcol", [128, 2 * nout], F32)
        k.dma("sp", bcol.t[:, :], bias.t[:, :], bias, bcol)
    for (t0, n) in token_blocks(T, TB):
        load_block_fm(k, xin, xT_bf, KCin, t0, n)

        def epi(gi, pss, t0=t0, n=n):
            r = rbuf.next()
            k.dma("pool", r.t[:, 0:n], resT.t[gi * 128:(gi + 1) * 128, t0:t0 + n], resT, r)
            o = obuf.next()
            if glu:
                k.op("act", lambda e: e.activation(out=sgb.t[:, 0:n], in_=pss[1].t[:, 0:n], func=AF.Sigmoid,
                                                   bias=bcol.t[:, nout + gi:nout + gi + 1], scale=1.0), reads=[pss[1], bcol], writes=[sgb])
                k.op("dve", lambda e: e.scalar_tensor_tensor(out=o.t[:, 0:n], in0=pss[0].t[:, 0:n], scalar=bcol.t[:, gi:gi + 1],
                                                              in1=sgb.t[:, 0:n], op0=ALU.add, op1=ALU.mult), reads=[pss[0], bcol, sgb], writes=[o])
                k.op("dve", lambda e: e.tensor_tensor(out=o.t[:, 0:n], in0=o.t[:, 0:n], in1=r.t[:, 0:n], op=ALU.add), reads=[o, r], writes=[o])
            else:
                k.op("dve", lambda e: e.tensor_tensor(out=o.t[:, 0:n], in0=pss[0].t[:, 0:n], in1=r.t[:, 0:n], op=ALU.add), reads=[pss[0], r], writes=[o])
            k.dma("pool", outT.t[gi * 128:(gi + 1) * 128, t0:t0 + n], o.t[:, 0:n], o, outT)
        groups = [[c, nout + c] for c in range(nout)] if glu else [[c] for c in range(nout)]
        fm_linear(k, xin, KCin, n, w, groups, wbufs, psums, epi)
    k.end_stage()


def stage_up(k, cfg, hT, gmlp, layer, wup, uT):
    TB = 512
    KC, T = cfg.KC, cfg.T
    k.begin_stage()
    st = common_scratch(k, TB)
    gcol = k.sbuf("gcol", [128, KC], F32)
    k.dma("sp", gcol.t[:, :], gmlp.t[:, layer, :], gmlp, gcol)
    hblk = k.sbuf("hblk", [128, KC, TB], F32)
    xin = k.sbuf("xin", [128, KC, TB], BF16)
    wbufs = Rot([k.sbuf("w%d" % i, [128, KC, 128], BF16) for i in range(3)])
    psums = Rot([k.psum("ps%d" % i, [128, 512]) for i in range(4)])
    rl = Rot([k.sbuf("rl%d" % i, [128, TB], F32) for i in range(2)])
    ub = Rot([k.sbuf("ub%d" % i, [128, TB], BF16) for i in range(3)])
    for (t0, n) in token_blocks(T, TB):
        load_block_fm(k, hblk, hT, KC, t0, n)
        rmsnorm_block(k, st, hblk, xin, KC, n, gcol, cfg.EPS)

        def epi(gi, pss, t0=t0, n=n):
            r = rl.next()
            u = ub.next()
            k.op("act", lambda e: e.activation(out=r.t[:, 0:n], in_=pss[0].t[:, 0:n], func=AF.Relu), reads=[pss[0]], writes=[r])
            k.op("dve", lambda e: e.tensor_tensor(out=u.t[:, 0:n], in0=r.t[:, 0:n], in1=r.t[:, 0:n], op=ALU.mult), reads=[r], writes=[u])
            cpp = cfg.FC // len(uT)
            up_ = uT[gi // cpp]
            gl = gi % cpp
            k.dma("pool", up_.t[gl * 128:(gl + 1) * 128, t0:t0 + n], u.t[:, 0:n], u, up_)
        fm_linear(k, xin, KC, n, wup, [[c] for c in range(cfg.FC)], wbufs, psums, epi)
    k.end_stage()


def rope_tables(T):
    inv = (np.float32(10000.0) ** (-np.arange(0, 128, 2, dtype=np.float32) / np.float32(128))).astype(np.float32)
    ang = (np.arange(T, dtype=np.float32)[:, None] * inv[None, :]).astype(np.float32)
    c = np.cos(ang).astype(np.float32).T
    s = np.sin(ang).astype(np.float32).T
    return np.ascontiguousarray(np.concatenate([c, c], 0)), np.ascontiguousarray(np.concatenate([s, s], 0))


def arr_w(W, cw=128):
    K, N = W.shape
    return np.ascontiguousarray(W.reshape(K // 128, 128, N // cw, cw).transpose(2, 1, 0, 3))


def col_g(g):
    return np.ascontiguousarray(g.reshape(-1, 128).T)


def build_program(cfg, layers=2):
    k = KB()
    D, T, KC, NSH, FC = cfg.D, cfg.T, cfg.KC, cfg.NSH, cfg.FC
    ein = lambda name, shape: k.dram(name, shape, F32, kind="ExternalInput")
    hT0 = ein("hT0", [D, T])
    gmix = ein("gmix", [128, 2, KC])
    gmlp = ein("gmlp", [128, 2, KC])
    wqk_f = ein("wqk", [2 * NSH, 128, KC, 128])
    wv_f = ein("wv", [D // 512, 128, KC, 512])
    gqk = ein("gqk", [128, 4])
    ropeC = ein("ropeC", [128, T])
    ropeS = ein("ropeS", [128, T])
    lamv = ein("lamv", [128, 512])
    subg = ein("subg", [128, 256])
    maskf = ein("maskf", [128, 128])
    identf = ein("identf", [128, 128])
    wo_f = ein("wo", [KC, 128, KC, 128])
    wup_f = [ein("wup%d" % l, [FC, 128, KC, 128]) for l in range(layers)]
    wdn_f = [ein("wdn%d" % l, [KC, 128, FC, 128]) for l in range(layers)]
    outT = k.dram("outT", [D, T], F32, kind="ExternalOutput")
    wqk = k.dram("wqk_b", [2 * NSH, 128, KC, 128], BF16)
    wv = k.dram("wv_b", [D // 512, 128, KC, 512], BF16)
    wo = k.dram("wo_b", [KC, 128, KC, 128], BF16)
    wup = [k.dram("wup_b%d" % l, [FC, 128, KC, 128], BF16) for l in range(layers)]
    wdn = [k.dram("wdn_b%d" % l, [KC, 128, FC, 128], BF16) for l in range(layers)]
    cast_weights(k, wqk_f, wqk, 2 * NSH)
    cast_weights(k, wv_f, wv, D // 512)
    cast_weights(k, wo_f, wo, KC)
    for l in range(layers):
        cast_weights(k, wup_f[l], wup[l], FC)
        cast_weights(k, wdn_f[l], wdn[l], KC)
    qk_dram = k.dram("qk_s", [2 * NSH, 128, T], BF16)
    v_dram = k.dram("v_s", [T, D], BF16)
    oT = k.dram("oT_s", [D, T], BF16)
    h1T = k.dram("h1T", [D, T], F32)
    uT = [k.dram("uT%d" % i, [cfg.FF // 2, T], BF16) for i in range(2)]
    h2T = k.dram("h2T", [D, T], F32) if layers > 1 else outT
    lam0 = 0.8 - 0.6 * math.exp(-0.3 * 0)
    stage_qkv(k, cfg, hT0, gmix, wqk, wv, gqk, ropeC, ropeS, qk_dram, v_dram)
    stage_attn(k, cfg, qk_dram, v_dram, lamv, subg, maskf, identf, oT, lam0)
    stage_linear_res(k, cfg, "wo", oT, KC, wo, KC, hT0, h1T, 512)
    stage_up(k, cfg, h1T, gmlp, 0, wup[0], uT)
    stage_linear_res(k, cfg, "dn0", uT, FC, wdn[0], KC, h1T, h2T, 256)
    if layers > 1:
        GP = cfg.GP
        prm = {}
        for nm, shp in (("ar", [128, GP]), ("ai", [128, GP]), ("ldt", [128, GP]), ("br", [128, GP, 16]), ("bi", [128, GP, 16]),
                        ("cr", [128, GP, 16]), ("ci", [128, GP, 16]), ("m01", [128, 2]), ("dcol", [128, KC]), ("bglu", [128, 2 * KC])):
            prm[nm] = ein(nm, shp)
        prm["identf"] = identf
        wglu_f = ein("wglu", [2 * KC, 128, KC, 128])
        wglu = k.dram("wglu_b", [2 * KC, 128, KC, 128], BF16)
        cast_weights(k, wglu_f, wglu, 2 * KC)
        prm["abar_s"] = k.dram("abar_s", [128, 2, GP], F32)
        prm["lc_s"] = k.dram("lc_s", [2, 128, GP, 2, 16], BF16)
        prm["lb_s"] = k.dram("lb_s", [2, 32, GP, 128], BF16)
        hnT = k.dram("hnT", [D, T], F32)
        hnTb = k.dram("hnTb", [D, T], BF16)
        yT = k.dram("yT", [D, T], F32)
        zT = k.dram("zT", [D, T], BF16)
        h3T = k.dram("h3T", [D, T], F32)
        stage_s5(k, cfg, h2T, gmix, prm, hnT, hnTb, yT, zT)
        stage_linear_res(k, cfg, "glu", zT, KC, wglu, KC, h2T, h3T, 512, bias=prm["bglu"], glu=True)
        stage_up(k, cfg, h3T, gmlp, 1, wup[1], uT)
        stage_linear_res(k, cfg, "dn1", uT, FC, wdn[1], KC, h3T, outT, 256)
    return k.finish([outT]), k


def prep_inputs(cfg, x_b, meta_tokens, norm_mix_g, norm_mlp_g, da_w_qkv, da_q_norm_g, da_k_norm_g,
                da_lambda, da_subln_g, da_w_o, mlp_w_up, mlp_w_down, layers=2, shared=None, ssm=None):
    D, T, NSH = cfg.D, cfg.T, cfg.NSH
    f = np.float32
    m = {}
    m["hT0"] = np.ascontiguousarray(np.concatenate([meta_tokens, x_b], 0).T.astype(f))
    if shared is not None:
        m.update(shared)
        return m, shared
    sh = {}
    sh["gmix"] = np.ascontiguousarray(np.stack([col_g(norm_mix_g[min(l, len(norm_mix_g) - 1)]) for l in range(2)], 1))
    sh["gmlp"] = np.ascontiguousarray(np.stack([col_g(norm_mlp_g[min(l, len(norm_mlp_g) - 1)]) for l in range(2)], 1))
    Wqkv = da_w_qkv[0]
    cols = []
    for base in (0, D):
        for j in range(NSH // 2):
            s0, s1 = base + (2 * j) * 128, base + (2 * j + 1) * 128
            cols.append(np.concatenate([np.arange(s0, s0 + 64), np.arange(s1, s1 + 64)]))
            cols.append(np.concatenate([np.arange(s0 + 64, s0 + 128), np.arange(s1 + 64, s1 + 128)]))
    cols = np.concatenate(cols)
    sh["wqk"] = arr_w(Wqkv[:, cols])
    sh["wv"] = arr_w(Wqkv[:, 2 * D:3 * D], 512)
    gq, gk = da_q_norm_g[0], da_k_norm_g[0]
    lo = np.tile(np.arange(64), 2)
    sh["gqk"] = np.ascontiguousarray(np.stack([gq[lo], gq[64 + lo], gk[lo], gk[64 + lo]], 1).astype(f))
    c, s = rope_tables(T)
    sh["ropeC"], sh["ropeS"] = c, s
    sh["lamv"] = np.ascontiguousarray(np.broadcast_to(da_lambda[0].reshape(1, 512), (128, 512)).astype(f))
    sh["subg"] = np.ascontiguousarray(np.broadcast_to(da_subln_g[0].reshape(1, 256), (128, 256)).astype(f))
    sh["maskf"] = np.triu(np.ones((128, 128), f))
    sh["identf"] = np.eye(128, dtype=f)
    sh["wo"] = arr_w(da_w_o[0])
    for l in range(layers):
        sh["wup%d" % l] = arr_w(mlp_w_up[l])
        sh["wdn%d" % l] = arr_w(mlp_w_down[l])
    if ssm is not None:
        a_re, a_im, log_dt, b_re, b_im, c_re, c_im, d_skip, w_glu, b_glu = ssm
        G = a_re.shape[0]
        GP = G // 2
        sh["ar"] = np.ascontiguousarray(a_re.reshape(GP, 2, 64).transpose(1, 2, 0).reshape(128, GP))
        sh["ai"] = np.ascontiguousarray(a_im.reshape(GP, 2, 64).transpose(1, 2, 0).reshape(128, GP))
        sh["ldt"] = np.ascontiguousarray(np.broadcast_to(log_dt.reshape(GP, 2).T[:, None, :], (2, 64, GP)).reshape(128, GP).astype(f))
        sh["br"] = np.ascontiguousarray(b_re.reshape(GP, 2, 64, 16).transpose(1, 2, 0, 3).reshape(128, GP, 16))
        sh["bi"] = np.ascontiguousarray(b_im.reshape(GP, 2, 64, 16).transpose(1, 2, 0, 3).reshape(128, GP, 16))
        sh["cr"] = np.ascontiguousarray(c_re.reshape(GP, 2, 16, 64).transpose(1, 3, 0, 2).reshape(128, GP, 16))
        sh["ci"] = np.ascontiguousarray(c_im.reshape(GP, 2, 16, 64).transpose(1, 3, 0, 2).reshape(128, GP, 16))
        m01 = np.zeros((128, 2), f)
        m01[:64, 0] = 1.0
        m01[64:, 1] = 1.0
        sh["m01"] = m01
        sh["dcol"] = col_g(d_skip)
        sh["bglu"] = col_g(b_glu)
        sh["wglu"] = arr_w(w_glu)
    m.update(sh)
    return m, sh


def kernel(x, meta_tokens, norm_mix_g, norm_mlp_g, da_w_qkv, da_q_norm_g, da_k_norm_g, da_lambda, da_subln_g, da_w_o,
           ssm_a_re, ssm_a_im, ssm_log_dt, ssm_b_re, ssm_b_im, ssm_c_re, ssm_c_im, ssm_d, ssm_w_glu, ssm_b_glu,
           mlp_w_up, mlp_w_down):
    a = lambda v: np.asarray(v, dtype=np.float32)
    x = a(x)
    B, SEQ, D = x.shape
    cfg = Cfg(D=D, SEQ=SEQ, H=D // 256, NMETA=meta_tokens.shape[0])
    nc, _ = build_program(cfg, layers=2)
    ssm = tuple(a(v)[0] for v in (ssm_a_re, ssm_a_im, ssm_log_dt, ssm_b_re, ssm_b_im, ssm_c_re, ssm_c_im, ssm_d, ssm_w_glu, ssm_b_glu))
    maps, sh = [], None
    for b in range(B):
        m, sh = prep_inputs(cfg, x[b], a(meta_tokens), a(norm_mix_g), a(norm_mlp_g), a(da_w_qkv), a(da_q_norm_g), a(da_k_norm_g),
                            a(da_lambda), a(da_subln_g), a(da_w_o), a(mlp_w_up), a(mlp_w_down), layers=2, shared=sh, ssm=ssm)
        maps.append(m)
    res = run_bass_kernel_spmd(nc, maps, core_ids=list(range(B)))
    nm = cfg.NMETA
    out = np.stack([np.ascontiguousarray(res.results[b]["outT"][:, nm:nm + SEQ].T) for b in range(B)])
    return out.astype(np.float32)


def stage_s5(k, cfg, hT, gmix, prm, hnT, hnTb, yT, zT):
    D, T, KC, GP = cfg.D, cfg.T, cfg.KC, cfg.GP
    TWO_PI = 2.0 * math.pi
    TB = 512
    k.begin_stage()
    st = common_scratch(k, TB)
    gcol = k.sbuf("gcol", [128, KC], F32)
    k.dma("sp", gcol.t[:, :], gmix.t[:, 1, :], gmix, gcol)
    hblk = k.sbuf("hblk", [128, KC, TB], F32)
    xin = k.sbuf("xin", [128, KC, TB], BF16)
    hn = k.sbuf("hn", [128, KC, TB], F32)
    for (t0, n) in token_blocks(T, TB):
        load_block_fm(k, hblk, hT, KC, t0, n)
        rmsnorm_block(k, st, hblk, xin, KC, n, gcol, cfg.EPS)
        for c in range(KC):
            k.op("dve", lambda e, c=c: e.scalar_tensor_tensor(out=hn.t[:, c, 0:n], in0=hblk.t[:, c, 0:n], scalar=gcol.t[:, c:c + 1],
                                                               in1=st["rstd"].t[:, 0:n], op0=ALU.mult, op1=ALU.mult),
                 reads=[hblk, gcol, st["rstd"]], writes=[hn])
        for c0 in range(0, KC, 8):
            c1 = min(KC, c0 + 8)
            k.dma("pool", hnT.t[c0 * 128:c1 * 128, t0:t0 + n].rearrange("(kc p) t -> p kc t", p=128), hn.t[:, c0:c1, 0:n], hn, hnT)
            k.dma("pool", hnTb.t[c0 * 128:c1 * 128, t0:t0 + n].rearrange("(kc p) t -> p kc t", p=128), xin.t[:, c0:c1, 0:n], xin, hnTb)
    k.end_stage()
    k.begin_stage()
    ld = lambda name, shape: (lambda b: (k.dma("sp", b.t[tuple(slice(None) for _ in shape)], prm[name].t[tuple(slice(None) for _ in shape)], prm[name], b), b)[1])(k.sbuf(name, shape, F32))
    ar, ai, ldt = ld("ar", [128, GP]), ld("ai", [128, GP]), ld("ldt", [128, GP])
    br, bi = ld("br", [128, GP, 16]), ld("bi", [128, GP, 16])
    cr, ci = ld("cr", [128, GP, 16]), ld("ci", [128, GP, 16])
    m01 = ld("m01", [128, 2])
    idf = ld("identf", [128, 128])
    ident = k.sbuf("identb", [128, 128], BF16)
    k.op("dve", lambda e: e.tensor_copy(ident.t[:, :], idf.t[:, :]), reads=[idf], writes=[ident])
    S = lambda name: k.sbuf(name, [128, GP], F32)
    dt, mag, ang, kk, tmp, sn_, cs_, abr, abi, den, fr, fi, t1, t2 = [S("s5_%d" % i) for i in range(14)]
    tt = lambda out, a, b, op: k.op("dve", lambda e: e.tensor_tensor(out=out.t[:, :], in0=a.t[:, :], in1=b.t[:, :], op=op), reads=[a, b], writes=[out])
    k.op("act", lambda e: e.activation(out=dt.t[:, :], in_=ldt.t[:, :], func=AF.Exp), reads=[ldt], writes=[dt])
    tt(mag, dt, ar, ALU.mult)
    k.op("act", lambda e: e.activation(out=mag.t[:, :], in_=mag.t[:, :], func=AF.Exp), reads=[mag], writes=[mag])
    tt(ang, dt, ai, ALU.mult)

    def sin_of(out, src, shift):
        k.op("dve", lambda e: e.tensor_scalar(out=tmp.t[:, :], in0=src.t[:, :], scalar1=float(shift), scalar2=None, op0=ALU.add), reads=[src], writes=[tmp])
        k.op("dve", lambda e: e.memset(kk.t[:, :], 0.0), writes=[kk])
        for m_ in range(1, 9):
            k.op("dve", lambda e, m_=m_: e.tensor_scalar(out=t1.t[:, :], in0=tmp.t[:, :], scalar1=float((2 * m_ - 1) * math.pi), scalar2=TWO_PI,
                                                          op0=ALU.is_ge, op1=ALU.mult), reads=[tmp], writes=[t1])
            tt(kk, kk, t1, ALU.add)
        tt(tmp, tmp, kk, ALU.subtract)
        k.op("dve", lambda e: e.tensor_scalar(out=tmp.t[:, :], in0=tmp.t[:, :], scalar1=-math.pi, scalar2=math.pi, op0=ALU.max, op1=ALU.min), reads=[tmp], writes=[tmp])
        k.op("act", lambda e: e.activation(out=out.t[:, :], in_=tmp.t[:, :], func=AF.Sin), reads=[tmp], writes=[out])
    sin_of(sn_, ang, 0.0)
    sin_of(cs_, ang, math.pi / 2)
    tt(abr, mag, cs_, ALU.mult)
    tt(abi, mag, sn_, ALU.mult)
    tt(t1, ar, ar, ALU.mult)
    tt(t2, ai, ai, ALU.mult)
    tt(den, t1, t2, ALU.add)
    k.op("dve", lambda e: e.reciprocal(out=den.t[:, :], in_=den.t[:, :]), reads=[den], writes=[den])
    k.op("dve", lambda e: e.tensor_scalar(out=tmp.t[:, :], in0=abr.t[:, :], scalar1=-1.0, scalar2=None, op0=ALU.add), reads=[abr], writes=[tmp])
    tt(t1, tmp, ar, ALU.mult)
    tt(t2, abi, ai, ALU.mult)
    tt(fr, t1, t2, ALU.add)
    tt(fr, fr, den, ALU.mult)
    tt(t1, abi, ar, ALU.mult)
    tt(t2, tmp, ai, ALU.mult)
    tt(fi, t1, t2, ALU.subtract)
    tt(fi, fi, den, ALU.mult)
    k.dma("sp", prm["abar_s"].t[:, 0, :], abr.t[:, :], abr, prm["abar_s"])
    k.dma("sp", prm["abar_s"].t[:, 1, :], abi.t[:, :], abi, prm["abar_s"])
    bbr = k.sbuf("bbr", [128, GP, 16], F32)
    bbi = k.sbuf("bbi", [128, GP, 16], F32)
    u1 = k.sbuf("u1", [128, GP], F32)
    u2 = k.sbuf("u2", [128, GP], F32)
    for i in range(16):
        for (dst, a, fa, b_, fb, op) in ((bbr, br, fr, bi, fi, ALU.subtract), (bbi, bi, fr, br, fi, ALU.add)):
            k.op("dve", lambda e, a=a, fa=fa, i=i: e.tensor_tensor(out=u1.t[:, :], in0=a.t[:, :, i], in1=fa.t[:, :], op=ALU.mult), reads=[a, fa], writes=[u1])
            k.op("dve", lambda e, b_=b_, fb=fb, i=i: e.tensor_tensor(out=u2.t[:, :], in0=b_.t[:, :, i], in1=fb.t[:, :], op=ALU.mult), reads=[b_, fb], writes=[u2])
            k.op("dve", lambda e, dst=dst, i=i, op=op: e.tensor_tensor(out=dst.t[:, :, i], in0=u1.t[:, :], in1=u2.t[:, :], op=op), reads=[u1, u2], writes=[dst])
    MB = [k.sbuf("MB%d" % r, [128, GP, 2, 16], BF16) for r in range(2)]
    LCs = [k.sbuf("LC%d" % r, [128, GP, 2, 16], BF16) for r in range(2)]
    nm01 = k.sbuf("nm01", [128, 2], F32)
    k.op("dve", lambda e: e.tensor_scalar(out=nm01.t[:, :], in0=m01.t[:, :], scalar1=-1.0, scalar2=None, op0=ALU.mult), reads=[m01], writes=[nm01])
    for r, (bsrc, csrc, msk) in enumerate(((bbr, cr, m01), (bbi, ci, nm01))):
        for g2 in range(2):
            k.op("dve", lambda e, r=r, g2=g2, bsrc=bsrc: e.tensor_scalar(out=MB[r].t[:, :, g2, :], in0=bsrc.t[:, :, :], scalar1=m01.t[:, g2:g2 + 1], scalar2=None, op0=ALU.mult),
                 reads=[bsrc, m01], writes=[MB[r]])
            k.op("dve", lambda e, r=r, g2=g2, csrc=csrc, msk=msk: e.tensor_scalar(out=LCs[r].t[:, :, g2, :], in0=csrc.t[:, :, :], scalar1=msk.t[:, g2:g2 + 1], scalar2=None, op0=ALU.mult),
                 reads=[csrc, msk], writes=[LCs[r]])
        k.dma("sp", prm["lc_s"].t[r], LCs[r].t[:, :, :, :], LCs[r], prm["lc_s"])
    ptr = k.psum("ptr", [128, 1024], BF16)
    lbo = Rot([k.sbuf("lbo%d" % i, [32, 8, 128], BF16) for i in range(2)])
    for r in range(2):
        for g0 in range(0, GP, 8):
            ng = min(8, GP - g0)

            def tr(e, r=r, g0=g0, ng=ng):
                ins = None
                for j in range(ng):
                    ins = e.transpose(ptr.t[0:32, j * 128:(j + 1) * 128], MB[r].t[:, g0 + j, :, :].rearrange("p a b -> p (a b)"), ident.t[:, :])
                return ins
            k.op("pe", tr, reads=[MB[r], ident], writes=[ptr])
            o = lbo.next()
            k.op("act", lambda e, o=o, ng=ng: e.activation(out=o.t[:, 0:ng, :], in_=ptr.t[0:32, 0:ng * 128].rearrange("p (a b) -> p a b", b=128), func=AF.Copy),
                 reads=[ptr], writes=[o])
            k.dma("sp", prm["lb_s"].t[r, :, g0:g0 + ng, :], o.t[:, 0:ng, :], o, prm["lb_s"])
    k.end_stage()
    TBs = 16
    k.begin_stage()
    LB = [k.sbuf("LB%d" % r, [32, GP, 128], BF16) for r in range(2)]
    LC = [k.sbuf("LCm%d" % r, [128, GP, 32], BF16) for r in range(2)]
    for r in range(2):
        k.dma("sp", LB[r].t[:, :, :], prm["lb_s"].t[r], prm["lb_s"], LB[r])
        k.dma("sp", LC[r].t[:, :, :], prm["lc_s"].t[r].rearrange("p g a b -> p g (a b)"), prm["lc_s"], LC[r])
    AS = k.sbuf("AS", [128, 2, GP], F32)
    AC = k.sbuf("AC", [128, 2, GP], F32)
    k.dma("sp", AS.t[:, 0, :], prm["abar_s"].t[:, 0, :], prm["abar_s"], AS)
    k.dma("sp", AS.t[:, 1, :], prm["abar_s"].t[:, 0, :], prm["abar_s"], AS)
    k.dma("sp", AC.t[:, 1, :], prm["abar_s"].t[:, 1, :], prm["abar_s"], AC)
    k.dma("sp", AC.t[:, 0, :], prm["abar_s"].t[:, 1, :], prm["abar_s"], AC)
    k.op("dve", lambda e: e.tensor_scalar(out=AC.t[:, 0, :], in0=AC.t[:, 0, :], scalar1=-1.0, scalar2=None, op0=ALU.mult), reads=[AC], writes=[AC])
    NH = 2 if GP % 2 == 0 else 1
    GH = GP // NH
    Xs = [k.sbuf("X%d" % i, [128, 2, GH, TBs + 1], F32) for i in range(NH)]
    Ps = [k.sbuf("P%d" % i, [128, 2, GH], F32) for i in range(NH)]
    Qs = [k.sbuf("Q%d" % i, [128, 2, GH], F32) for i in range(NH)]
    for Xh in Xs:
        k.op("dve", lambda e, Xh=Xh: e.memset(Xh.t[:, :, :, :], 0.0), writes=[Xh])
    BU = k.sbuf("BU", [128, 2, GP, TBs], F32)
    XB = k.sbuf("XB", [128, 2, GP, TBs], BF16)
    P_ = k.sbuf("P_", [128, 2, GP], F32)
    Q_ = k.sbuf("Q_", [128, 2, GP], F32)
    x32 = Rot([k.sbuf("x32_%d" % i, [32, GP, TBs], BF16) for i in range(2)])
    y32 = Rot([k.sbuf("y32_%d" % i, [32, GP, TBs], F32) for i in range(2)])
    psb = Rot([k.psum("psb%d" % i, [128, 512]) for i in range(4)])
    psy = Rot([k.psum("psy%d" % i, [128, 512]) for i in range(2)])
    per = 512 // TBs
    for (t0, n) in token_blocks(T, TBs):
        xi = x32.next()
        k.dma("sp", xi.t[:, :, 0:n], hnTb.t[:, t0:t0 + n].rearrange("(g r) t -> r g t", r=32), hnTb, xi)
        for r in range(2):
            for g0 in range(0, GP, per):
                ng = min(per, GP - g0)
                ps = psb.next()

                def mm(e, r=r, g0=g0, ng=ng, ps=ps):
                    ins = None
                    for j in range(ng):
                        ins = e.matmul(ps.t[:, j * TBs:j * TBs + n], lhsT=LB[r].t[:, g0 + j, :], rhs=xi.t[:, g0 + j, 0:n], start=True, stop=True)
                    return ins
                k.op("pe", mm, reads=[LB[r], xi], writes=[ps])
                k.op("act", lambda e, r=r, g0=g0, ng=ng, ps=ps: e.activation(
                    out=BU.t[:, r, g0:g0 + ng, 0:n], in_=ps.t[:, 0:ng * TBs].rearrange("p (g t) -> p g t", t=TBs)[:, :, 0:n], func=AF.Copy),
                    reads=[ps], writes=[BU])
        for s in range(n):
            for hf in range(NH):
                g = slice(hf * GH, (hf + 1) * GH)
                Xh, Ph, Qh = Xs[hf], Ps[hf], Qs[hf]
                k.op("dve", lambda e, s=s, g=g, Xh=Xh, Ph=Ph: e.tensor_tensor(out=Ph.t[:, :, :], in0=Xh.t[:, :, :, s], in1=AS.t[:, :, g], op=ALU.mult), reads=[Xh, AS], writes=[Ph])
                for r in range(2):
                    k.op("dve", lambda e, s=s, r=r, g=g, Xh=Xh, Qh=Qh: e.tensor_tensor(out=Qh.t[:, r, :], in0=Xh.t[:, 1 - r, :, s], in1=AC.t[:, r, g], op=ALU.mult), reads=[Xh, AC], writes=[Qh])
            for hf in range(NH):
                g = slice(hf * GH, (hf + 1) * GH)
                Xh, Ph, Qh = Xs[hf], Ps[hf], Qs[hf]
                k.op("dve", lambda e, Ph=Ph, Qh=Qh: e.tensor_tensor(out=Ph.t[:, :, :], in0=Ph.t[:, :, :], in1=Qh.t[:, :, :], op=ALU.add), reads=[Ph, Qh], writes=[Ph])
            for hf in range(NH):
                g = slice(hf * GH, (hf + 1) * GH)
                Xh, Ph = Xs[hf], Ps[hf]
                k.op("dve", lambda e, s=s, g=g, Xh=Xh, Ph=Ph: e.tensor_tensor(out=Xh.t[:, :, :, s + 1], in0=Ph.t[:, :, :], in1=BU.t[:, :, g, s], op=ALU.add), reads=[Ph, BU], writes=[Xh])
        for hf in range(NH):
            g = slice(hf * GH, (hf + 1) * GH)
            k.op("dve", lambda e, hf=hf, g=g: e.tensor_copy(XB.t[:, :, g, 0:n], Xs[hf].t[:, :, :, 1:n + 1]), reads=[Xs[hf]], writes=[XB])
            k.op("dve", lambda e, hf=hf: e.tensor_copy(Xs[hf].t[:, :, :, 0], Xs[hf].t[:, :, :, n]), reads=[Xs[hf]], writes=[Xs[hf]])
        yo = y32.next()
        for g0 in range(0, GP, per):
            ng = min(per, GP - g0)
            ps = psy.next()

            def mm(e, g0=g0, ng=ng, ps=ps):
                ins = None
                for j in range(ng):
                    e.matmul(ps.t[0:32, j * TBs:j * TBs + n], lhsT=LC[0].t[:, g0 + j, :], rhs=XB.t[:, 0, g0 + j, 0:n], start=True, stop=False)
                    ins = e.matmul(ps.t[0:32, j * TBs:j * TBs + n], lhsT=LC[1].t[:, g0 + j, :], rhs=XB.t[:, 1, g0 + j, 0:n], start=False, stop=True)
                return ins
            k.op("pe", mm, reads=[LC[0], LC[1], XB], writes=[ps])
            k.op("act", lambda e, g0=g0, ng=ng, ps=ps: e.activation(
                out=yo.t[:, g0:g0 + ng, 0:n], in_=ps.t[0:32, 0:ng * TBs].rearrange("p (g t) -> p g t", t=TBs)[:, :, 0:n], func=AF.Copy),
                reads=[ps], writes=[yo])
        k.dma("pool", yT.t[:, t0:t0 + n].rearrange("(g r) t -> r g t", r=32), yo.t[:, :, 0:n], yo, yT)
    k.end_stage()
    k.begin_stage()
    dcol = k.sbuf("dcol", [128, KC], F32)
    k.dma("sp", dcol.t[:, :], prm["dcol"].t[:, :], prm["dcol"], dcol)
    yb = Rot([k.sbuf("yb%d" % i, [128, TB], F32) for i in range(2)])
    hb = Rot([k.sbuf("hb%d" % i, [128, TB], F32) for i in range(2)])
    v1 = k.sbuf("v1", [128, TB], F32)
    v2 = k.sbuf("v2", [128, TB], F32)
    zb = Rot([k.sbuf("zb%d" % i, [128, TB], BF16) for i in range(2)])
    C0 = math.sqrt(2.0 / math.pi)
    for (t0, n) in token_blocks(T, TB):
        for c in range(KC):
            y_, h_ = yb.next(), hb.next()
            k.dma("sp", y_.t[:, 0:n], yT.t[c * 128:(c + 1) * 128, t0:t0 + n], yT, y_)
            k.dma("sp", h_.t[:, 0:n], hnT.t[c * 128:(c + 1) * 128, t0:t0 + n], hnT, h_)
            k.op("dve", lambda e: e.scalar_tensor_tensor(out=v1.t[:, 0:n], in0=h_.t[:, 0:n], scalar=dcol.t[:, c:c + 1], in1=y_.t[:, 0:n],
                                                          op0=ALU.mult, op1=ALU.add), reads=[h_, dcol, y_], writes=[v1])
            k.op("act", lambda e: e.activation(out=v2.t[:, 0:n], in_=v1.t[:, 0:n], func=AF.Square), reads=[v1], writes=[v2])
            k.op("dve", lambda e: e.tensor_scalar(out=v2.t[:, 0:n], in0=v2.t[:, 0:n], scalar1=0.044715, scalar2=1.0, op0=ALU.mult, op1=ALU.add), reads=[v2], writes=[v2])
            k.op("dve", lambda e: e.tensor_tensor(out=v2.t[:, 0:n], in0=v2.t[:, 0:n], in1=v1.t[:, 0:n], op=ALU.mult), reads=[v2, v1], writes=[v2])
            k.op("act", lambda e: e.activation(out=v2.t[:, 0:n], in_=v2.t[:, 0:n], func=AF.Sigmoid, scale=2.0 * C0), reads=[v2], writes=[v2])
            z_ = zb.next()
            k.op("dve", lambda e: e.tensor_tensor(out=z_.t[:, 0:n], in0=v2.t[:, 0:n], in1=v1.t[:, 0:n], op=ALU.mult), reads=[v2, v1], writes=[z_])
            k.dma("pool", zT.t[c * 128:(c + 1) * 128, t0:t0 + n], z_.t[:, 0:n], z_, zT)
    k.end_stage()
```

```python
import math
from contextlib import ExitStack

import numpy as np
import ml_dtypes

import concourse.bass as bass
import concourse.mybir as mybir
from concourse.bass_utils import run_bass_kernel_spmd

F32 = mybir.dt.float32
BF16 = mybir.dt.bfloat16
AF = mybir.ActivationFunctionType
ALU = mybir.AluOpType


class Buf:
    def __init__(self, k, name, t, accumulate=False):
        self.k = k
        self.name = name
        self.t = t
        self.acc = accumulate
        self.writers = {}
        self.readers = {}
        self._in = None
        self._out = None

    def __getitem__(self, idx):
        return self.t[idx]

    def bump_in(self):
        if self._in is None:
            self._in = self.k.take_dsem()
        self._in[1] += 16
        return (self._in[0], self._in[1])

    def bump_out(self):
        if self._out is None:
            self._out = self.k.take_dsem()
        self._out[1] += 16
        return (self._out[0], self._out[1])


class KB:
    def __init__(self):
        self.nc = bass.Bass("TRN2", target_bir_lowering=False)
        self.es = ExitStack()
        self.nsem = 0
        nc = self.nc
        self.eng = {"pe": nc.tensor, "act": nc.scalar, "dve": nc.vector, "pool": nc.gpsimd, "sp": nc.sync}
        self.esem = {e: self.new_sem("e_" + e) for e in ("pe", "act", "dve", "pool")}
        self.ecnt = {e: 0 for e in self.esem}
        self.seen = {e: {} for e in self.eng}
        self.uid = 0

    def new_sem(self, name):
        self.nsem += 1
        return self.es.enter_context(self.nc.semaphore(name))

    def take_dsem(self):
        if not hasattr(self, "dpool"):
            self.dpool = []
            self.allsems = []
        if self.dpool:
            return self.dpool.pop()
        ent = [self.new_sem("d%d" % self.nsem), 0]
        self.allsems.append(ent)
        return ent

    def begin_stage(self):
        self.stage_es = ExitStack()
        self.stage_bufs = []

    def end_stage(self):
        for e in ("pe", "act", "dve", "pool", "sp"):
            for e2 in self.esem:
                if self.ecnt[e2] > 0:
                    self._wait(e, self.esem[e2], self.ecnt[e2])
            for ent in getattr(self, "allsems", []):
                if ent[1] > 0:
                    self._wait(e, ent[0], ent[1])
        for b in self.stage_bufs:
            for ent in (b._in, b._out):
                if ent is not None:
                    self.dpool.append(ent)
        self.stage_es.close()
        self.stage_es = None

    def dram(self, name, shape, dtype, kind="Internal"):
        t = self.nc.dram_tensor(name, list(shape), dtype, kind=kind)
        return Buf(self, name, t.ap(), accumulate=True)

    def sbuf(self, name, shape, dtype):
        st = getattr(self, "stage_es", None)
        self.uid += 1
        t = (st or self.es).enter_context(self.nc.sbuf_tensor("%s_%d" % (name, self.uid), list(shape), dtype))
        b = Buf(self, name, t)
        if st is not None:
            self.stage_bufs.append(b)
        return b

    def psum(self, name, shape, dtype=F32):
        st = getattr(self, "stage_es", None)
        self.uid += 1
        t = (st or self.es).enter_context(self.nc.psum_tensor("%s_%d" % (name, self.uid), list(shape), dtype))
        b = Buf(self, name, t)
        if st is not None:
            self.stage_bufs.append(b)
        return b

    def _wait(self, e, sem, val):
        sid = id(sem)
        if self.seen[e].get(sid, 0) >= val:
            return
        self.seen[e][sid] = val
        self.eng[e].wait_ge(sem, val)

    def _deps(self, e, reads, writes):
        for b in reads:
            for sem, val in b.writers.values():
                self._wait(e, sem, val)
        for b in writes:
            for sem, val in b.writers.values():
                if not b.acc:
                    self._wait(e, sem, val)
            for sem, val in b.readers.values():
                self._wait(e, sem, val)

    def _record(self, ev, reads, writes):
        sem, val = ev
        for b in reads:
            b.readers[id(sem)] = ev
        for b in writes:
            if b.acc:
                b.writers[id(sem)] = ev
            else:
                b.writers = {id(sem): ev}
                b.readers = {}

    def op(self, e, fn, reads=(), writes=()):
        self._deps(e, reads, writes)
        ins = fn(self.eng[e])
        self.ecnt[e] += 1
        ins.then_inc(self.esem[e], 1)
        self._record((self.esem[e], self.ecnt[e]), reads, writes)

    def dma(self, q, out_ap, in_ap, src, dst, **kw):
        self._deps(q, [src], [dst])
        ins = self.eng[q].dma_start(out=out_ap, in_=in_ap, **kw)
        if not dst.acc or src.acc:
            ev = dst.bump_in()
        else:
            ev = src.bump_out()
        ins.then_inc(ev[0], 16)
        self._record(ev, [src], [dst])

    def finish(self, outs):
        for b in outs:
            for sem, val in b.writers.values():
                self._wait("sp", sem, val)
        self.es.close()
        return self.nc


class Rot:
    def __init__(self, bufs):
        self.bufs = bufs
        self.i = 0

    def next(self):
        b = self.bufs[self.i % len(self.bufs)]
        self.i += 1
        return b


def token_blocks(T, TB):
    out = []
    t0 = 0
    while t0 < T:
        n = min(TB, T - t0)
        out.append((t0, n))
        t0 += n
    return out


def load_block_fm(k, dst, srcT, KC, t0, n):
    parts = srcT if isinstance(srcT, list) else [srcT]
    kpp = KC // len(parts)
    half = min(8, kpp)
    for pi, part in enumerate(parts):
        view = part.t.rearrange("(kc p) t -> p kc t", p=128)
        for a in range(0, kpp, half):
            b = min(kpp, a + half)
            k.dma("sp", dst.t[:, pi * kpp + a:pi * kpp + b, 0:n], view[:, a:b, t0:t0 + n], part, dst)


def rmsnorm_block(k, st, hblk, out_bf, KC, n, gcol, eps, want_rstd=None):
    D = KC * 128
    ps = st["ps_stat"]
    sq = st["sq"]

    for c in range(KC):
        s = sq.next()
        k.op("act", lambda e, s=s, c=c: e.activation(out=s.t[:, 0:n], in_=hblk.t[:, c, 0:n], func=AF.Square),
             reads=[hblk], writes=[s])
        last = (c == KC - 1)
        k.op("pe", lambda e, s=s, c=c: e.matmul(ps.t[:, 0:n], lhsT=st["ones"].t[:, :], rhs=s.t[:, 0:n],
                                                  start=(c == 0), stop=(c == KC - 1)),
             reads=[s, st["ones"]], writes=[ps] if c == 0 else [])
        if last:
            k._record((k.esem["pe"], k.ecnt["pe"]), [], [ps])
    rstd = st["rstd"]
    k.op("act", lambda e: e.activation(out=rstd.t[:, 0:n], in_=ps.t[:, 0:n], func=AF.Sqrt,
                                       bias=st["epsb"].t[:, 0:1], scale=1.0 / D),
         reads=[ps, st["epsb"]], writes=[rstd])
    k.op("dve", lambda e: e.reciprocal(out=rstd.t[:, 0:n], in_=rstd.t[:, 0:n]), reads=[rstd], writes=[rstd])
    for c in range(KC):
        k.op("dve", lambda e, c=c: e.scalar_tensor_tensor(out=out_bf.t[:, c, 0:n], in0=hblk.t[:, c, 0:n],
                                                          scalar=gcol.t[:, c:c + 1], in1=rstd.t[:, 0:n],
                                                          op0=ALU.mult, op1=ALU.mult),
             reads=[hblk, gcol, rstd], writes=[out_bf])
    if want_rstd is not None:
        k.op("act", lambda e: e.activation(out=want_rstd.t[:, 0:n], in_=rstd.t[:, 0:n], func=AF.Copy),
             reads=[rstd], writes=[want_rstd])


def fm_linear(k, xin, KC, n, w_dram, groups, wbufs, psums, epilogue, wk0=0):
    for gi, grp in enumerate(groups):
        pss = []
        for ch in grp:
            w = wbufs.next()
            k.dma("sp", w.t[:, :, :], w_dram.t[ch][:, wk0:wk0 + KC, :], w_dram, w)
            ps = psums.next()

            def mm(e, w=w, ps=ps):
                ins = None
                for kc in range(KC):
                    ins = e.matmul(ps.t[:, 0:n], lhsT=w.t[:, kc, :], rhs=xin.t[:, kc, 0:n],
                                   start=(kc == 0), stop=(kc == KC - 1))
                return ins
            k.op("pe", mm, reads=[w, xin], writes=[ps])
            pss.append(ps)
        epilogue(gi, pss)


class Cfg:
    def __init__(self, D=4096, SEQ=8192, H=16, NMETA=16):
        self.D, self.SEQ, self.H, self.NMETA = D, SEQ, H, NMETA
        self.T = SEQ + NMETA
        self.KC = D // 128
        self.NSH = 2 * H
        self.FF = 4 * D
        self.FC = self.FF // 128
        self.G = D // 16
        self.GP = self.G // 2
        self.EPS = 1e-6


def common_scratch(k, TB):
    ones = k.sbuf("ones", [128, 128], F32)
    epsb = k.sbuf("epsb", [128, 1], F32)
    k.op("dve", lambda e: e.memset(ones.t[:, :], 1.0), writes=[ones])
    k.op("dve", lambda e: e.memset(epsb.t[:, :], 1e-6), writes=[epsb])
    return dict(ones=ones, epsb=epsb, ps_stat=k.psum("ps_stat", [128, 512]),
                rstd=k.sbuf("rstd", [128, TB], F32),
                sq=Rot([k.sbuf("sq%d" % i, [128, TB], F32) for i in range(2)]))


def cast_weights(k, wf, wb, nchunks):
    for c in range(nchunks):
        k.dma("pool", wb.t[c], wf.t[c], wf, wb)


def stage_qkv(k, cfg, hT, gmix, wqk, wv, gqk, ropeC, ropeS, qk_dram, v_dram):
    TB = 512
    KC, T, NSH = cfg.KC, cfg.T, cfg.NSH
    k.begin_stage()
    st = common_scratch(k, TB)
    gcol = k.sbuf("gcol", [128, KC], F32)
    k.dma("sp", gcol.t[:, :], gmix.t[:, 0, :], gmix, gcol)
    gq = k.sbuf("gq", [128, 4], F32)
    k.dma("sp", gq.t[:, :], gqk.t[:, :], gqk, gq)
    bones = k.sbuf("bones", [128, 128], F32)
    k.op("dve", lambda e: e.memset(bones.t[:, :], 0.0), writes=[bones])
    k.op("dve", lambda e: e.memset(bones.t[0:64, 0:64], 1.0), reads=[bones], writes=[bones])
    k.op("dve", lambda e: e.memset(bones.t[64:128, 64:128], 1.0), reads=[bones], writes=[bones])
    hblk = k.sbuf("hblk", [128, KC, TB], F32)
    xin = k.sbuf("xin", [128, KC, TB], BF16)
    cs = k.sbuf("cs", [128, TB], F32)
    sn = k.sbuf("sn", [128, TB], F32)
    wbufs = Rot([k.sbuf("w%d" % i, [128, KC, 128], BF16) for i in range(3)])
    wvb = Rot([k.sbuf("wv%d" % i, [128, KC, 512], BF16) for i in range(1)])
    psums = Rot([k.psum("ps%d" % i, [128, 512]) for i in range(4)])
    psq = k.psum("psq", [128, 512])
    sqa = k.sbuf("sqa", [128, TB], F32)
    sqb = k.sbuf("sqb", [128, TB], F32)
    rs = k.sbuf("rs", [128, TB], F32)
    qa = k.sbuf("qa", [128, TB], F32)
    qb = k.sbuf("qb", [128, TB], F32)
    t1 = k.sbuf("t1", [128, TB], F32)
    t2 = k.sbuf("t2", [128, TB], F32)
    oa = Rot([k.sbuf("oa%d" % i, [128, TB], BF16) for i in range(2)])
    ob = Rot([k.sbuf("ob%d" % i, [128, TB], BF16) for i in range(2)])
    vo = Rot([k.sbuf("vo%d" % i, [128, 512], BF16) for i in range(2)])
    for (t0, n) in token_blocks(T, TB):
        load_block_fm(k, hblk, hT, KC, t0, n)
        k.dma("sp", cs.t[:, 0:n], ropeC.t[:, t0:t0 + n], ropeC, cs)
        k.dma("sp", sn.t[:, 0:n], ropeS.t[:, t0:t0 + n], ropeS, sn)
        rmsnorm_block(k, st, hblk, xin, KC, n, gcol, cfg.EPS)

        def epi(gi, pss, t0=t0, n=n):
            A, B = pss
            isk = 1 if gi >= NSH // 2 else 0
            j = gi - isk * (NSH // 2)
            k.op("act", lambda e: e.activation(out=sqa.t[:, 0:n], in_=A.t[:, 0:n], func=AF.Square), reads=[A], writes=[sqa])
            k.op("act", lambda e: e.activation(out=sqb.t[:, 0:n], in_=B.t[:, 0:n], func=AF.Square), reads=[B], writes=[sqb])

            def mm(e):
                e.matmul(psq.t[:, 0:n], lhsT=bones.t[:, :], rhs=sqa.t[:, 0:n], start=True, stop=False)
                return e.matmul(psq.t[:, 0:n], lhsT=bones.t[:, :], rhs=sqb.t[:, 0:n], start=False, stop=True)
            k.op("pe", mm, reads=[bones, sqa, sqb], writes=[psq])
            k.op("act", lambda e: e.activation(out=rs.t[:, 0:n], in_=psq.t[:, 0:n], func=AF.Sqrt,
                                               bias=st["epsb"].t[:, 0:1], scale=1.0 / 128), reads=[psq, st["epsb"]], writes=[rs])
            k.op("dve", lambda e: e.reciprocal(out=rs.t[:, 0:n], in_=rs.t[:, 0:n]), reads=[rs], writes=[rs])
            k.op("dve", lambda e: e.scalar_tensor_tensor(out=qa.t[:, 0:n], in0=A.t[:, 0:n], scalar=gq.t[:, 2 * isk:2 * isk + 1],
                                                          in1=rs.t[:, 0:n], op0=ALU.mult, op1=ALU.mult), reads=[A, gq, rs], writes=[qa])
            k.op("dve", lambda e: e.scalar_tensor_tensor(out=qb.t[:, 0:n], in0=B.t[:, 0:n], scalar=gq.t[:, 2 * isk + 1:2 * isk + 2],
                                                          in1=rs.t[:, 0:n], op0=ALU.mult, op1=ALU.mult), reads=[B, gq, rs], writes=[qb])
            o_a, o_b = oa.next(), ob.next()
            k.op("dve", lambda e: e.tensor_tensor(out=t1.t[:, 0:n], in0=qa.t[:, 0:n], in1=cs.t[:, 0:n], op=ALU.mult), reads=[qa, cs], writes=[t1])
            k.op("pool", lambda e: e.tensor_tensor(out=t2.t[:, 0:n], in0=qb.t[:, 0:n], in1=sn.t[:, 0:n], op=ALU.mult), reads=[qb, sn], writes=[t2])
            k.op("dve", lambda e: e.tensor_tensor(out=o_a.t[:, 0:n], in0=t1.t[:, 0:n], in1=t2.t[:, 0:n], op=ALU.subtract), reads=[t1, t2], writes=[o_a])
            k.op("dve", lambda e: e.tensor_tensor(out=t1.t[:, 0:n], in0=qb.t[:, 0:n], in1=cs.t[:, 0:n], op=ALU.mult), reads=[qb, cs], writes=[t1])
            k.op("pool", lambda e: e.tensor_tensor(out=t2.t[:, 0:n], in0=qa.t[:, 0:n], in1=sn.t[:, 0:n], op=ALU.mult), reads=[qa, sn], writes=[t2])
            k.op("dve", lambda e: e.tensor_tensor(out=o_b.t[:, 0:n], in0=t1.t[:, 0:n], in1=t2.t[:, 0:n], op=ALU.add), reads=[t1, t2], writes=[o_b])
            s0 = isk * NSH + 2 * j
            k.dma("pool", qk_dram.t[s0, 0:64, t0:t0 + n], o_a.t[0:64, 0:n], o_a, qk_dram)
            k.dma("pool", qk_dram.t[s0 + 1, 0:64, t0:t0 + n], o_a.t[64:128, 0:n], o_a, qk_dram)
            k.dma("pool", qk_dram.t[s0, 64:128, t0:t0 + n], o_b.t[0:64, 0:n], o_b, qk_dram)
            k.dma("pool", qk_dram.t[s0 + 1, 64:128, t0:t0 + n], o_b.t[64:128, 0:n], o_b, qk_dram)
        fm_linear(k, xin, KC, n, wqk, [[2 * p, 2 * p + 1] for p in range(NSH)], wbufs, psums, epi)
        for sl in range(cfg.D // 512):
            w = wvb.next()
            k.dma("sp", w.t[:, :, :], wv.t[sl], wv, w)
            for (tt0, tn) in token_blocks(n, 128):
                ps = psums.next()

                def mm(e, w=w, ps=ps, tt0=tt0, tn=tn):
                    ins = None
                    for kc in range(KC):
                        ins = e.matmul(ps.t[0:tn, 0:512], lhsT=xin.t[:, kc, tt0:tt0 + tn], rhs=w.t[:, kc, :],
                                       start=(kc == 0), stop=(kc == KC - 1))
                    return ins
                k.op("pe", mm, reads=[w, xin], writes=[ps])
                o = vo.next()
                k.op("act", lambda e, o=o, ps=ps, tn=tn: e.activation(out=o.t[0:tn, :], in_=ps.t[0:tn, 0:512], func=AF.Copy),
                     reads=[ps], writes=[o])
                k.dma("pool", v_dram.t[t0 + tt0:t0 + tt0 + tn, sl * 512:(sl + 1) * 512], o.t[0:tn, :], o, v_dram)
    k.end_stage()


def stage_attn(k, cfg, qk_dram, v_dram, lamv, subg, maskf, identf, oT, lam_init):
    T, NSH, H = cfg.T, cfg.NSH, cfg.H
    NT = (T + 127) // 128
    NTF = T // 128
    QB = 256
    k.begin_stage()
    mk32 = k.sbuf("mk32", [128, 128], F32)
    id32 = k.sbuf("id32", [128, 128], F32)
    mask = k.sbuf("mask", [128, 128], BF16)
    ident = k.sbuf("ident", [128, 128], BF16)
    k.dma("sp", mk32.t[:, :], maskf.t[:, :], maskf, mk32)
    k.dma("sp", id32.t[:, :], identf.t[:, :], identf, id32)
    k.op("dve", lambda e: e.tensor_copy(mask.t[:, :], mk32.t[:, :]), reads=[mk32], writes=[mask])
    k.op("dve", lambda e: e.tensor_copy(ident.t[:, :], id32.t[:, :]), reads=[id32], writes=[ident])
    sg = k.sbuf("sg", [128, 256], F32)
    k.dma("sp", sg.t[:, :], subg.t[:, :], subg, sg)
    lv = k.sbuf("lv", [128, 512], F32)
    k.dma("sp", lv.t[:, :], lamv.t[:, :], lamv, lv)
    lt = k.sbuf("lt", [128, 256], F32)
    l2 = k.sbuf("l2", [128, 2], F32)
    nlam = k.sbuf("nlam", [128, 1], F32)
    k.op("dve", lambda e: e.tensor_tensor(out=lt.t[:, 0:128], in0=lv.t[:, 0:128], in1=lv.t[:, 128:256], op=ALU.mult), reads=[lv], writes=[lt])
    k.op("dve", lambda e: e.tensor_tensor(out=lt.t[:, 128:256], in0=lv.t[:, 256:384], in1=lv.t[:, 384:512], op=ALU.mult), reads=[lv, lt], writes=[lt])
    k.op("dve", lambda e: e.tensor_reduce(out=l2.t[:, :], in_=lt.t[:, :].rearrange("p (a b) -> p a b", a=2), axis=mybir.AxisListType.X, op=ALU.add),
         reads=[lt], writes=[l2])
    k.op("act", lambda e: e.activation(out=l2.t[:, :], in_=l2.t[:, :], func=AF.Exp), reads=[l2], writes=[l2])
    k.op("dve", lambda e: e.scalar_tensor_tensor(out=nlam.t[:, :], in0=l2.t[:, 1:2], scalar=-float(lam_init), in1=l2.t[:, 0:1],
                                                  op0=ALU.add, op1=ALU.subtract), reads=[l2], writes=[nlam])
    qs_ = [k.sbuf("q%d" % c, [128, T], BF16) for c in range(2)]
    ks_ = [k.sbuf("k%d" % c, [128, T], BF16) for c in range(2)]
    vh = k.sbuf("vh", [128, NT, 257], BF16)
    k.op("dve", lambda e: e.memset(vh.t[:, :, 256:257], 1.0), writes=[vh])
    pss = Rot([k.psum("pss%d" % i, [128, 512]) for i in range(3)])
    accs = [[k.psum("acc%d%d" % (c, i), [128, 512]) for i in range(2)] for c in range(2)]
    ptr = k.psum("ptr", [128, 1024], BF16)
    pT = Rot([k.sbuf("pT%d" % i, [128, 2, QB], BF16) for i in range(6)])
    r0 = k.sbuf("r0", [128, 1], F32)
    r1 = k.sbuf("r1", [128, 1], F32)
    of = k.sbuf("of", [128, 256], F32)
    junk = k.sbuf("junk", [128, 256], F32)
    ss = k.sbuf("ss", [128, 1], F32)
    ob = k.sbuf("ob", [128, 256], BF16)
    oTs = Rot([k.sbuf("oTs%d" % i, [128, 2, 128], BF16) for i in range(2)])
    scale = 128.0 ** -0.5
    epsb_att = k.sbuf("epsb_att", [128, 1], F32)
    k.op("dve", lambda e: e.memset(epsb_att.t[:, :], 1e-6), writes=[epsb_att])
    for h in range(H):
        for c in range(2):
            k.dma("sp", qs_[c].t[:, :], qk_dram.t[2 * h + c], qk_dram, qs_[c])
            k.dma("sp", ks_[c].t[:, :], qk_dram.t[NSH + 2 * h + c], qk_dram, ks_[c])
        for i0 in range(0, NTF, 8):
            i1 = min(NTF, i0 + 8)
            k.dma("sp", vh.t[:, i0:i1, 0:256],
                  v_dram.t[i0 * 128:i1 * 128, h * 256:(h + 1) * 256].rearrange("(i p) e -> p i e", p=128), v_dram, vh)
        if NT > NTF:
            rem = T - NTF * 128
            k.dma("sp", vh.t[0:rem, NTF, 0:256], v_dram.t[NTF * 128:T, h * 256:(h + 1) * 256], v_dram, vh)
        for (q0, nq) in token_blocks(T, QB):
            tq0 = q0 // 128
            qtiles = token_blocks(nq, 128)
            last_kt = (q0 + nq - 1) // 128
            pend = []

            def emit_pv(p, kt, nk, qs):
                for qi, (qq0, nqt) in enumerate(qtiles):
                    tq = tq0 + qi
                    if tq < kt:
                        continue
                    off = 128 * tq - qs
                    for c in range(2):
                        acc = accs[c][qi]
                        k.op("pe", lambda e, acc=acc, c=c, off=off, nqt=nqt, tq=tq: e.matmul(
                            acc.t[0:nqt, 0:257], lhsT=p.t[0:nk, c, off:off + nqt], rhs=vh.t[0:nk, kt, :],
                            start=(kt == 0), stop=(kt == tq)),
                            reads=[p, vh], writes=[acc] if kt == 0 else [])
                        if kt == tq:
                            k._record((k.esem["pe"], k.ecnt["pe"]), [], [acc])
            for kt in range(last_kt + 1):
                nk = min(128, T - 128 * kt)
                qs = max(q0, 128 * kt)
                nqq = q0 + nq - qs
                ps = pss.next()

                def mm(e, ps=ps, kt=kt, nk=nk, qs=qs, nqq=nqq):
                    ins = None
                    for c in range(2):
                        ins = e.matmul(ps.t[0:nk, c * QB:c * QB + nqq], lhsT=ks_[c].t[:, 128 * kt:128 * kt + nk],
                                       rhs=qs_[c].t[:, qs:qs + nqq], start=True, stop=True)
                    return ins
                k.op("pe", mm, reads=[ks_[0], ks_[1], qs_[0], qs_[1]], writes=[ps])
                p = pT.next()
                k.op("act", lambda e, p=p, ps=ps, nk=nk, nqq=nqq: e.activation(
                    out=p.t[0:nk, :, 0:nqq], in_=ps.t[0:nk, :].rearrange("p (c q) -> p c q", c=2)[:, :, 0:nqq],
                    func=AF.Exp, scale=scale), reads=[ps], writes=[p])
                if 128 * kt >= q0:
                    w = min(128, nqq)
                    for c in range(2):
                        k.op("pool", lambda e, p=p, c=c, nk=nk, w=w: e.tensor_tensor(
                            out=p.t[0:nk, c, 0:w], in0=p.t[0:nk, c, 0:w], in1=mask.t[0:nk, 0:w], op=ALU.mult),
                            reads=[p, mask], writes=[p])
                pend.append((p, kt, nk, qs))
                if len(pend) > 2:
                    emit_pv(*pend.pop(0))
            while pend:
                emit_pv(*pend.pop(0))
            for qi, (qq0, nqt) in enumerate(qtiles):
                tq = tq0 + qi
                a0, a1 = accs[0][qi], accs[1][qi]
                k.op("dve", lambda e: e.reciprocal(out=r0.t[0:nqt, :], in_=a0.t[0:nqt, 256:257]), reads=[a0], writes=[r0])
                k.op("dve", lambda e: e.reciprocal(out=r1.t[0:nqt, :], in_=a1.t[0:nqt, 256:257]), reads=[a1], writes=[r1])
                k.op("dve", lambda e: e.tensor_tensor(out=r1.t[0:nqt, :], in0=r1.t[0:nqt, :], in1=nlam.t[0:nqt, :], op=ALU.mult),
                     reads=[r1, nlam], writes=[r1])
                k.op("dve", lambda e: e.tensor_scalar(out=of.t[0:nqt, :], in0=a0.t[0:nqt, 0:256], scalar1=r0.t[0:nqt, 0:1], scalar2=None,
                                                       op0=ALU.mult), reads=[a0, r0], writes=[of])
                k.op("dve", lambda e: e.scalar_tensor_tensor(out=of.t[0:nqt, :], in0=a1.t[0:nqt, 0:256], scalar=r1.t[0:nqt, 0:1],
                                                              in1=of.t[0:nqt, :], op0=ALU.mult, op1=ALU.add), reads=[a1, r1, of], writes=[of])
                k.op("act", lambda e: e.activation(out=junk.t[0:nqt, :], in_=of.t[0:nqt, :], func=AF.Square, accum_out=ss.t[0:nqt, :]),
                     reads=[of], writes=[junk, ss])
                k.op("act", lambda e: e.activation(out=ss.t[0:nqt, :], in_=ss.t[0:nqt, :], func=AF.Sqrt, bias=epsb_att.t[0:nqt, 0:1], scale=1.0 / 256),
                     reads=[ss, epsb_att], writes=[ss])
                k.op("dve", lambda e: e.reciprocal(out=ss.t[0:nqt, :], in_=ss.t[0:nqt, :]), reads=[ss], writes=[ss])
                k.op("dve", lambda e: e.tensor_scalar(out=ss.t[0:nqt, :], in0=ss.t[0:nqt, :], scalar1=float(1.0 - lam_init), scalar2=None, op0=ALU.mult),
                     reads=[ss], writes=[ss])
                k.op("dve", lambda e: e.scalar_tensor_tensor(out=ob.t[0:nqt, :], in0=of.t[0:nqt, :], scalar=ss.t[0:nqt, 0:1], in1=sg.t[0:nqt, :],
                                                              op0=ALU.mult, op1=ALU.mult), reads=[of, ss, sg], writes=[ob])

                def tr(e):
                    ins = None
                    for e2 in range(2):
                        ins = e.transpose(ptr.t[:, e2 * 128:e2 * 128 + nqt], ob.t[0:nqt, e2 * 128:(e2 + 1) * 128], ident.t[0:nqt, 0:nqt])
                    return ins
                k.op("pe", tr, reads=[ob, ident], writes=[ptr])
                o = oTs.next()
                k.op("act", lambda e, o=o: e.activation(out=o.t[:, :, 0:nqt], in_=ptr.t[:, 0:256].rearrange("p (a b) -> p a b", a=2)[:, :, 0:nqt],
                                                        func=AF.Copy), reads=[ptr], writes=[o])
                k.dma("pool", oT.t[h * 256:(h + 1) * 256, 128 * tq:128 * tq + nqt].rearrange("(a p) t -> p a t", p=128),
                      o.t[:, :, 0:nqt], o, oT)
    k.end_stage()


def stage_linear_ksplit(k, cfg, name, xparts, KCin, w, nout, resT, outT, TB):
    T = cfg.T
    NP = len(xparts)
    KCh = KCin // NP
    k.begin_stage()
    xin = k.sbuf("xin", [128, KCh, TB], BF16)
    wbufs = Rot([k.sbuf("w%d" % i, [128, KCh, 128], BF16) for i in range(3)])
    psums = Rot([k.psum("ps%d" % i, [128, 512]) for i in range(4)])
    part = k.sbuf("part", [128, nout, TB], F32)
    rbuf = Rot([k.sbuf("r%d" % i, [128, TB], F32) for i in range(2)])
    obuf = Rot([k.sbuf("o%d" % i, [128, TB], F32) for i in range(2)])
    for (t0, n) in token_blocks(T, TB):
        for h in range(NP):
            load_block_fm(k, xin, xparts[h], KCh, t0, n)

            def epi(gi, pss, t0=t0, n=n, h=h):
                if h == 0:
                    k.op("act", lambda e: e.activation(out=part.t[:, gi, 0:n], in_=pss[0].t[:, 0:n], func=AF.Copy), reads=[pss[0]], writes=[part])
                    return
                if h < NP - 1:
                    k.op("dve", lambda e: e.tensor_tensor(out=part.t[:, gi, 0:n], in0=pss[0].t[:, 0:n], in1=part.t[:, gi, 0:n], op=ALU.add),
                         reads=[pss[0], part], writes=[part])
                    return
                r = rbuf.next()
                k.dma("pool", r.t[:, 0:n], resT.t[gi * 128:(gi + 1) * 128, t0:t0 + n], resT, r)
                o = obuf.next()
                k.op("dve", lambda e: e.tensor_tensor(out=o.t[:, 0:n], in0=pss[0].t[:, 0:n], in1=part.t[:, gi, 0:n], op=ALU.add), reads=[pss[0], part], writes=[o])
                k.op("pool", lambda e: e.tensor_tensor(out=o.t[:, 0:n], in0=o.t[:, 0:n], in1=r.t[:, 0:n], op=ALU.add), reads=[o, r], writes=[o])
                k.dma("pool", outT.t[gi * 128:(gi + 1) * 128, t0:t0 + n], o.t[:, 0:n], o, outT)
            fm_linear(k, xin, KCh, n, w, [[c] for c in range(nout)], wbufs, psums, epi, wk0=h * KCh)
    k.end_stage()


def stage_linear_res(k, cfg, name, xT_bf, KCin, w, nout, resT, outT, TB, bias=None, glu=False):
    T = cfg.T
    k.begin_stage()
    xin = k.sbuf("xin", [128, KCin, TB], BF16)
    wbufs = Rot([k.sbuf("w%d" % i, [128, KCin, 128], BF16) for i in range(3)])
    psums = Rot([k.psum("ps%d" % i, [128, 512]) for i in range(4)])
    rbuf = Rot([k.sbuf("r%d" % i, [128, TB], F32) for i in range(2)])
    obuf = Rot([k.sbuf("o%d" % i, [128, TB], F32) for i in range(2)])
    sgb = k.sbuf("sgb", [128, TB], F32)
    if bias is not None:
        bcol = k.sbuf("bcol", [128, 2 * nout], F32)
        k.dma("sp", bcol.t[:, :], bias.t[:, :], bias, bcol)
    for (t0, n) in token_blocks(T, TB):
        load_block_fm(k, xin, xT_bf, KCin, t0, n)

        def epi(gi, pss, t0=t0, n=n):
            r = rbuf.next()
            k.dma("pool", r.t[:, 0:n], resT.t[gi * 128:(gi + 1) * 128, t0:t0 + n], resT, r)
            o = obuf.next()
            if glu:
                k.op("act", lambda e: e.activation(out=sgb.t[:, 0:n], in_=pss[1].t[:, 0:n], func=AF.Sigmoid,
                                                   bias=bcol.t[:, nout + gi:nout + gi + 1], scale=1.0), reads=[pss[1], bcol], writes=[sgb])
                k.op("dve", lambda e: e.scalar_tensor_tensor(out=o.t[:, 0:n], in0=pss[0].t[:, 0:n], scalar=bcol.t[:, gi:gi + 1],
                                                              in1=sgb.t[:, 0:n], op0=ALU.add, op1=ALU.mult), reads=[pss[0], bcol, sgb], writes=[o])
                k.op("dve", lambda e: e.tensor_tensor(out=o.t[:, 0:n], in0=o.t[:, 0:n], in1=r.t[:, 0:n], op=ALU.add), reads=[o, r], writes=[o])
            else:
                k.op("dve", lambda e: e.tensor_tensor(out=o.t[:, 0:n], in0=pss[0].t[:, 0:n], in1=r.t[:, 0:n], op=ALU.add), reads=[pss[0], r], writes=[o])
            k.dma("pool", outT.t[gi * 128:(gi + 1) * 128, t0:t0 + n], o.t[:, 0:n], o, outT)
        groups = [[c, nout + c] for c in range(nout)] if glu else [[c] for c in range(nout)]
        fm_linear(k, xin, KCin, n, w, groups, wbufs, psums, epi)
    k.end_stage()


def stage_up(k, cfg, hT, gmlp, layer, wup, uT):
    TB = 512
    KC, T = cfg.KC, cfg.T
    k.begin_stage()
    st = common_scratch(k, TB)
    gcol = k.sbuf("gcol", [128, KC], F32)
    k.dma("sp", gcol.t[:, :], gmlp.t[:, layer, :], gmlp, gcol)
    hblk = k.sbuf("hblk", [128, KC, TB], F32)
    xin = k.sbuf("xin", [128, KC, TB], BF16)
    wbufs = Rot([k.sbuf("w%d" % i, [128, KC, 128], BF16) for i in range(3)])
    psums = Rot([k.psum("ps%d" % i, [128, 512]) for i in range(4)])
    rl = Rot([k.sbuf("rl%d" % i, [128, TB], F32) for i in range(2)])
    ub = Rot([k.sbuf("ub%d" % i, [128, TB], BF16) for i in range(3)])
    for (t0, n) in token_blocks(T, TB):
        load_block_fm(k, hblk, hT, KC, t0, n)
        rmsnorm_block(k, st, hblk, xin, KC, n, gcol, cfg.EPS)

        def epi(gi, pss, t0=t0, n=n):
            r = rl.next()
            u = ub.next()
            k.op("act", lambda e: e.activation(out=r.t[:, 0:n], in_=pss[0].t[:, 0:n], func=AF.Relu), reads=[pss[0]], writes=[r])
            k.op("dve", lambda e: e.tensor_tensor(out=u.t[:, 0:n], in0=r.t[:, 0:n], in1=r.t[:, 0:n], op=ALU.mult), reads=[r], writes=[u])
            cpp = cfg.FC // len(uT)
            up_ = uT[gi // cpp]
            gl = gi % cpp
            k.dma("pool", up_.t[gl * 128:(gl + 1) * 128, t0:t0 + n], u.t[:, 0:n], u, up_)
        fm_linear(k, xin, KC, n, wup, [[c] for c in range(cfg.FC)], wbufs, psums, epi)
    k.end_stage()


def rope_tables(T):
    inv = (np.float32(10000.0) ** (-np.arange(0, 128, 2, dtype=np.float32) / np.float32(128))).astype(np.float32)
    ang = (np.arange(T, dtype=np.float32)[:, None] * inv[None, :]).astype(np.float32)
    c = np.cos(ang).astype(np.float32).T
    s = np.sin(ang).astype(np.float32).T
    return np.ascontiguousarray(np.concatenate([c, c], 0)), np.ascontiguousarray(np.concatenate([s, s], 0))


def arr_w(W, cw=128):
    K, N = W.shape
    return np.ascontiguousarray(W.reshape(K // 128, 128, N // cw, cw).transpose(2, 1, 0, 3))


def col_g(g):
    return np.ascontiguousarray(g.reshape(-1, 128).T)


def build_program(cfg, layers=2):
    k = KB()
    D, T, KC, NSH, FC = cfg.D, cfg.T, cfg.KC, cfg.NSH, cfg.FC
    ein = lambda name, shape: k.dram(name, shape, F32, kind="ExternalInput")
    hT0 = ein("hT0", [D, T])
    gmix = ein("gmix", [128, 2, KC])
    gmlp = ein("gmlp", [128, 2, KC])
    wqk_f = ein("wqk", [2 * NSH, 128, KC, 128])
    wv_f = ein("wv", [D // 512, 128, KC, 512])
    gqk = ein("gqk", [128, 4])
    ropeC = ein("ropeC", [128, T])
    ropeS = ein("ropeS", [128, T])
    lamv = ein("lamv", [128, 512])
    subg = ein("subg", [128, 256])
    maskf = ein("maskf", [128, 128])
    identf = ein("identf", [128, 128])
    wo_f = ein("wo", [KC, 128, KC, 128])
    wup_f = [ein("wup%d" % l, [FC, 128, KC, 128]) for l in range(layers)]
    wdn_f = [ein("wdn%d" % l, [KC, 128, FC, 128]) for l in range(layers)]
    outT = k.dram("outT", [D, T], F32, kind="ExternalOutput")
    wqk = k.dram("wqk_b", [2 * NSH, 128, KC, 128], BF16)
    wv = k.dram("wv_b", [D // 512, 128, KC, 512], BF16)
    wo = k.dram("wo_b", [KC, 128, KC, 128], BF16)
    wup = [k.dram("wup_b%d" % l, [FC, 128, KC, 128], BF16) for l in range(layers)]
    wdn = [k.dram("wdn_b%d" % l, [KC, 128, FC, 128], BF16) for l in range(layers)]
    cast_weights(k, wqk_f, wqk, 2 * NSH)
    cast_weights(k, wv_f, wv, D // 512)
    cast_weights(k, wo_f, wo, KC)
    for l in range(layers):
        cast_weights(k, wup_f[l], wup[l], FC)
        cast_weights(k, wdn_f[l], wdn[l], KC)
    qk_dram = k.dram("qk_s", [2 * NSH, 128, T], BF16)
    v_dram = k.dram("v_s", [T, D], BF16)
    oT = k.dram("oT_s", [D, T], BF16)
    h1T = k.dram("h1T", [D, T], F32)
    uT = [k.dram("uT%d" % i, [cfg.FF // 2, T], BF16) for i in range(2)]
    h2T = k.dram("h2T", [D, T], F32) if layers > 1 else outT
    lam0 = 0.8 - 0.6 * math.exp(-0.3 * 0)
    stage_qkv(k, cfg, hT0, gmix, wqk, wv, gqk, ropeC, ropeS, qk_dram, v_dram)
    stage_attn(k, cfg, qk_dram, v_dram, lamv, subg, maskf, identf, oT, lam0)
    stage_linear_res(k, cfg, "wo", oT, KC, wo, KC, hT0, h1T, 512)
    stage_up(k, cfg, h1T, gmlp, 0, wup[0], uT)
    stage_linear_ksplit(k, cfg, "dn0", uT, FC, wdn[0], KC, h1T, h2T, 512)
    if layers > 1:
        GP = cfg.GP
        prm = {}
        for nm, shp in (("ar", [128, GP]), ("ai", [128, GP]), ("ldt", [128, GP]), ("br", [128, GP, 16]), ("bi", [128, GP, 16]),
                        ("cr", [128, GP, 16]), ("ci", [128, GP, 16]), ("m01", [128, 2]), ("dcol", [128, KC]), ("bglu", [128, 2 * KC])):
            prm[nm] = ein(nm, shp)
        prm["identf"] = identf
        wglu_f = ein("wglu", [2 * KC, 128, KC, 128])
        wglu = k.dram("wglu_b", [2 * KC, 128, KC, 128], BF16)
        cast_weights(k, wglu_f, wglu, 2 * KC)
        prm["abar_s"] = k.dram("abar_s", [128, 2, GP], F32)
        prm["lc_s"] = k.dram("lc_s", [2, 128, GP, 2, 16], BF16)
        prm["lb_s"] = k.dram("lb_s", [2, 32, GP, 128], BF16)
        hnT = k.dram("hnT", [D, T], F32)
        hnTb = k.dram("hnTb", [D, T], BF16)
        yT = k.dram("yT", [D, T], F32)
        zT = k.dram("zT", [D, T], BF16)
        h3T = k.dram("h3T", [D, T], F32)
        stage_s5(k, cfg, h2T, gmix, prm, hnT, hnTb, yT, zT)
        stage_linear_res(k, cfg, "glu", zT, KC, wglu, KC, h2T, h3T, 512, bias=prm["bglu"], glu=True)
        stage_up(k, cfg, h3T, gmlp, 1, wup[1], uT)
        stage_linear_ksplit(k, cfg, "dn1", uT, FC, wdn[1], KC, h3T, outT, 512)
    return k.finish([outT]), k


def prep_inputs(cfg, x_b, meta_tokens, norm_mix_g, norm_mlp_g, da_w_qkv, da_q_norm_g, da_k_norm_g,
                da_lambda, da_subln_g, da_w_o, mlp_w_up, mlp_w_down, layers=2, shared=None, ssm=None):
    D, T, NSH = cfg.D, cfg.T, cfg.NSH
    f = np.float32
    m = {}
    m["hT0"] = np.ascontiguousarray(np.concatenate([meta_tokens, x_b], 0).T.astype(f))
    if shared is not None:
        m.update(shared)
        return m, shared
    sh = {}
    sh["gmix"] = np.ascontiguousarray(np.stack([col_g(norm_mix_g[min(l, len(norm_mix_g) - 1)]) for l in range(2)], 1))
    sh["gmlp"] = np.ascontiguousarray(np.stack([col_g(norm_mlp_g[min(l, len(norm_mlp_g) - 1)]) for l in range(2)], 1))
    Wqkv = da_w_qkv[0]
    cols = []
    for base in (0, D):
        for j in range(NSH // 2):
            s0, s1 = base + (2 * j) * 128, base + (2 * j + 1) * 128
            cols.append(np.concatenate([np.arange(s0, s0 + 64), np.arange(s1, s1 + 64)]))
            cols.append(np.concatenate([np.arange(s0 + 64, s0 + 128), np.arange(s1 + 64, s1 + 128)]))
    cols = np.concatenate(cols)
    sh["wqk"] = arr_w(Wqkv[:, cols])
    sh["wv"] = arr_w(Wqkv[:, 2 * D:3 * D], 512)
    gq, gk = da_q_norm_g[0], da_k_norm_g[0]
    lo = np.tile(np.arange(64), 2)
    sh["gqk"] = np.ascontiguousarray(np.stack([gq[lo], gq[64 + lo], gk[lo], gk[64 + lo]], 1).astype(f))
    c, s = rope_tables(T)
    sh["ropeC"], sh["ropeS"] = c, s
    sh["lamv"] = np.ascontiguousarray(np.broadcast_to(da_lambda[0].reshape(1, 512), (128, 512)).astype(f))
    sh["subg"] = np.ascontiguousarray(np.broadcast_to(da_subln_g[0].reshape(1, 256), (128, 256)).astype(f))
    sh["maskf"] = np.triu(np.ones((128, 128), f))
    sh["identf"] = np.eye(128, dtype=f)
    sh["wo"] = arr_w(da_w_o[0])
    for l in range(layers):
        sh["wup%d" % l] = arr_w(mlp_w_up[l])
        sh["wdn%d" % l] = arr_w(mlp_w_down[l])
    if ssm is not None:
        a_re, a_im, log_dt, b_re, b_im, c_re, c_im, d_skip, w_glu, b_glu = ssm
        G = a_re.shape[0]
        GP = G // 2
        sh["ar"] = np.ascontiguousarray(a_re.reshape(GP, 2, 64).transpose(1, 2, 0).reshape(128, GP))
        sh["ai"] = np.ascontiguousarray(a_im.reshape(GP, 2, 64).transpose(1, 2, 0).reshape(128, GP))
        sh["ldt"] = np.ascontiguousarray(np.broadcast_to(log_dt.reshape(GP, 2).T[:, None, :], (2, 64, GP)).reshape(128, GP).astype(f))
        sh["br"] = np.ascontiguousarray(b_re.reshape(GP, 2, 64, 16).transpose(1, 2, 0, 3).reshape(128, GP, 16))
        sh["bi"] = np.ascontiguousarray(b_im.reshape(GP, 2, 64, 16).transpose(1, 2, 0, 3).reshape(128, GP, 16))
        sh["cr"] = np.ascontiguousarray(c_re.reshape(GP, 2, 16, 64).transpose(1, 3, 0, 2).reshape(128, GP, 16))
        sh["ci"] = np.ascontiguousarray(c_im.reshape(GP, 2, 16, 64).transpose(1, 3, 0, 2).reshape(128, GP, 16))
        m01 = np.zeros((128, 2), f)
        m01[:64, 0] = 1.0
        m01[64:, 1] = 1.0
        sh["m01"] = m01
        sh["dcol"] = col_g(d_skip)
        sh["bglu"] = col_g(b_glu)
        sh["wglu"] = arr_w(w_glu)
    m.update(sh)
    return m, sh


def kernel(x, meta_tokens, norm_mix_g, norm_mlp_g, da_w_qkv, da_q_norm_g, da_k_norm_g, da_lambda, da_subln_g, da_w_o,
           ssm_a_re, ssm_a_im, ssm_log_dt, ssm_b_re, ssm_b_im, ssm_c_re, ssm_c_im, ssm_d, ssm_w_glu, ssm_b_glu,
           mlp_w_up, mlp_w_down):
    a = lambda v: np.asarray(v, dtype=np.float32)
    x = a(x)
    B, SEQ, D = x.shape
    cfg = Cfg(D=D, SEQ=SEQ, H=D // 256, NMETA=meta_tokens.shape[0])
    nc, _ = build_program(cfg, layers=2)
    ssm = tuple(a(v)[0] for v in (ssm_a_re, ssm_a_im, ssm_log_dt, ssm_b_re, ssm_b_im, ssm_c_re, ssm_c_im, ssm_d, ssm_w_glu, ssm_b_glu))
    maps, sh = [], None
    for b in range(B):
        m, sh = prep_inputs(cfg, x[b], a(meta_tokens), a(norm_mix_g), a(norm_mlp_g), a(da_w_qkv), a(da_q_norm_g), a(da_k_norm_g),
                            a(da_lambda), a(da_subln_g), a(da_w_o), a(mlp_w_up), a(mlp_w_down), layers=2, shared=sh, ssm=ssm)
        maps.append(m)
    res = run_bass_kernel_spmd(nc, maps, core_ids=list(range(B)))
    nm = cfg.NMETA
    out = np.stack([np.ascontiguousarray(res.results[b]["outT"][:, nm:nm + SEQ].T) for b in range(B)])
    return out.astype(np.float32)


def stage_s5(k, cfg, hT, gmix, prm, hnT, hnTb, yT, zT):
    D, T, KC, GP = cfg.D, cfg.T, cfg.KC, cfg.GP
    TWO_PI = 2.0 * math.pi
    TB = 512
    k.begin_stage()
    st = common_scratch(k, TB)
    gcol = k.sbuf("gcol", [128, KC], F32)
    k.dma("sp", gcol.t[:, :], gmix.t[:, 1, :], gmix, gcol)
    hblk = k.sbuf("hblk", [128, KC, TB], F32)
    xin = k.sbuf("xin", [128, KC, TB], BF16)
    hn = k.sbuf("hn", [128, KC, TB], F32)
    for (t0, n) in token_blocks(T, TB):
        load_block_fm(k, hblk, hT, KC, t0, n)
        rmsnorm_block(k, st, hblk, xin, KC, n, gcol, cfg.EPS)
        for c in range(KC):
            k.op("dve", lambda e, c=c: e.scalar_tensor_tensor(out=hn.t[:, c, 0:n], in0=hblk.t[:, c, 0:n], scalar=gcol.t[:, c:c + 1],
                                                               in1=st["rstd"].t[:, 0:n], op0=ALU.mult, op1=ALU.mult),
                 reads=[hblk, gcol, st["rstd"]], writes=[hn])
        for c0 in range(0, KC, 8):
            c1 = min(KC, c0 + 8)
            k.dma("pool", hnT.t[c0 * 128:c1 * 128, t0:t0 + n].rearrange("(kc p) t -> p kc t", p=128), hn.t[:, c0:c1, 0:n], hn, hnT)
            k.dma("pool", hnTb.t[c0 * 128:c1 * 128, t0:t0 + n].rearrange("(kc p) t -> p kc t", p=128), xin.t[:, c0:c1, 0:n], xin, hnTb)
    k.end_stage()
    k.begin_stage()
    ld = lambda name, shape: (lambda b: (k.dma("sp", b.t[tuple(slice(None) for _ in shape)], prm[name].t[tuple(slice(None) for _ in shape)], prm[name], b), b)[1])(k.sbuf(name, shape, F32))
    ar, ai, ldt = ld("ar", [128, GP]), ld("ai", [128, GP]), ld("ldt", [128, GP])
    br, bi = ld("br", [128, GP, 16]), ld("bi", [128, GP, 16])
    cr, ci = ld("cr", [128, GP, 16]), ld("ci", [128, GP, 16])
    m01 = ld("m01", [128, 2])
    idf = ld("identf", [128, 128])
    ident = k.sbuf("identb", [128, 128], BF16)
    k.op("dve", lambda e: e.tensor_copy(ident.t[:, :], idf.t[:, :]), reads=[idf], writes=[ident])
    S = lambda name: k.sbuf(name, [128, GP], F32)
    dt, mag, ang, kk, tmp, sn_, cs_, abr, abi, den, fr, fi, t1, t2 = [S("s5_%d" % i) for i in range(14)]
    tt = lambda out, a, b, op: k.op("dve", lambda e: e.tensor_tensor(out=out.t[:, :], in0=a.t[:, :], in1=b.t[:, :], op=op), reads=[a, b], writes=[out])
    k.op("act", lambda e: e.activation(out=dt.t[:, :], in_=ldt.t[:, :], func=AF.Exp), reads=[ldt], writes=[dt])
    tt(mag, dt, ar, ALU.mult)
    k.op("act", lambda e: e.activation(out=mag.t[:, :], in_=mag.t[:, :], func=AF.Exp), reads=[mag], writes=[mag])
    tt(ang, dt, ai, ALU.mult)

    def sin_of(out, src, shift):
        k.op("dve", lambda e: e.tensor_scalar(out=tmp.t[:, :], in0=src.t[:, :], scalar1=float(shift), scalar2=None, op0=ALU.add), reads=[src], writes=[tmp])
        k.op("dve", lambda e: e.memset(kk.t[:, :], 0.0), writes=[kk])
        for m_ in range(1, 9):
            k.op("dve", lambda e, m_=m_: e.tensor_scalar(out=t1.t[:, :], in0=tmp.t[:, :], scalar1=float((2 * m_ - 1) * math.pi), scalar2=TWO_PI,
                                                          op0=ALU.is_ge, op1=ALU.mult), reads=[tmp], writes=[t1])
            tt(kk, kk, t1, ALU.add)
        tt(tmp, tmp, kk, ALU.subtract)
        k.op("dve", lambda e: e.tensor_scalar(out=tmp.t[:, :], in0=tmp.t[:, :], scalar1=-math.pi, scalar2=math.pi, op0=ALU.max, op1=ALU.min), reads=[tmp], writes=[tmp])
        k.op("act", lambda e: e.activation(out=out.t[:, :], in_=tmp.t[:, :], func=AF.Sin), reads=[tmp], writes=[out])
    sin_of(sn_, ang, 0.0)
    sin_of(cs_, ang, math.pi / 2)
    tt(abr, mag, cs_, ALU.mult)
    tt(abi, mag, sn_, ALU.mult)
    tt(t1, ar, ar, ALU.mult)
    tt(t2, ai, ai, ALU.mult)
    tt(den, t1, t2, ALU.add)
    k.op("dve", lambda e: e.reciprocal(out=den.t[:, :], in_=den.t[:, :]), reads=[den], writes=[den])
    k.op("dve", lambda e: e.tensor_scalar(out=tmp.t[:, :], in0=abr.t[:, :], scalar1=-1.0, scalar2=None, op0=ALU.add), reads=[abr], writes=[tmp])
    tt(t1, tmp, ar, ALU.mult)
    tt(t2, abi, ai, ALU.mult)
    tt(fr, t1, t2, ALU.add)
    tt(fr, fr, den, ALU.mult)
    tt(t1, abi, ar, ALU.mult)
    tt(t2, tmp, ai, ALU.mult)
    tt(fi, t1, t2, ALU.subtract)
    tt(fi, fi, den, ALU.mult)
    k.dma("sp", prm["abar_s"].t[:, 0, :], abr.t[:, :], abr, prm["abar_s"])
    k.dma("sp", prm["abar_s"].t[:, 1, :], abi.t[:, :], abi, prm["abar_s"])
    bbr = k.sbuf("bbr", [128, GP, 16], F32)
    bbi = k.sbuf("bbi", [128, GP, 16], F32)
    u1 = k.sbuf("u1", [128, GP], F32)
    u2 = k.sbuf("u2", [128, GP], F32)
    for i in range(16):
        for (dst, a, fa, b_, fb, op) in ((bbr, br, fr, bi, fi, ALU.subtract), (bbi, bi, fr, br, fi, ALU.add)):
            k.op("dve", lambda e, a=a, fa=fa, i=i: e.tensor_tensor(out=u1.t[:, :], in0=a.t[:, :, i], in1=fa.t[:, :], op=ALU.mult), reads=[a, fa], writes=[u1])
            k.op("dve", lambda e, b_=b_, fb=fb, i=i: e.tensor_tensor(out=u2.t[:, :], in0=b_.t[:, :, i], in1=fb.t[:, :], op=ALU.mult), reads=[b_, fb], writes=[u2])
            k.op("dve", lambda e, dst=dst, i=i, op=op: e.tensor_tensor(out=dst.t[:, :, i], in0=u1.t[:, :], in1=u2.t[:, :], op=op), reads=[u1, u2], writes=[dst])
    MB = [k.sbuf("MB%d" % r, [128, GP, 2, 16], BF16) for r in range(2)]
    LCs = [k.sbuf("LC%d" % r, [128, GP, 2, 16], BF16) for r in range(2)]
    nm01 = k.sbuf("nm01", [128, 2], F32)
    k.op("dve", lambda e: e.tensor_scalar(out=nm01.t[:, :], in0=m01.t[:, :], scalar1=-1.0, scalar2=None, op0=ALU.mult), reads=[m01], writes=[nm01])
    for r, (bsrc, csrc, msk) in enumerate(((bbr, cr, m01), (bbi, ci, nm01))):
        for g2 in range(2):
            k.op("dve", lambda e, r=r, g2=g2, bsrc=bsrc: e.tensor_scalar(out=MB[r].t[:, :, g2, :], in0=bsrc.t[:, :, :], scalar1=m01.t[:, g2:g2 + 1], scalar2=None, op0=ALU.mult),
                 reads=[bsrc, m01], writes=[MB[r]])
            k.op("dve", lambda e, r=r, g2=g2, csrc=csrc, msk=msk: e.tensor_scalar(out=LCs[r].t[:, :, g2, :], in0=csrc.t[:, :, :], scalar1=msk.t[:, g2:g2 + 1], scalar2=None, op0=ALU.mult),
                 reads=[csrc, msk], writes=[LCs[r]])
        k.dma("sp", prm["lc_s"].t[r], LCs[r].t[:, :, :, :], LCs[r], prm["lc_s"])
    ptr = k.psum("ptr", [128, 1024], BF16)
    lbo = Rot([k.sbuf("lbo%d" % i, [32, 8, 128], BF16) for i in range(2)])
    for r in range(2):
        for g0 in range(0, GP, 8):
            ng = min(8, GP - g0)

            def tr(e, r=r, g0=g0, ng=ng):
                ins = None
                for j in range(ng):
                    ins = e.transpose(ptr.t[0:32, j * 128:(j + 1) * 128], MB[r].t[:, g0 + j, :, :].rearrange("p a b -> p (a b)"), ident.t[:, :])
                return ins
            k.op("pe", tr, reads=[MB[r], ident], writes=[ptr])
            o = lbo.next()
            k.op("act", lambda e, o=o, ng=ng: e.activation(out=o.t[:, 0:ng, :], in_=ptr.t[0:32, 0:ng * 128].rearrange("p (a b) -> p a b", b=128), func=AF.Copy),
                 reads=[ptr], writes=[o])
            k.dma("sp", prm["lb_s"].t[r, :, g0:g0 + ng, :], o.t[:, 0:ng, :], o, prm["lb_s"])
    k.end_stage()
    TBs = 16
    k.begin_stage()
    LB = [k.sbuf("LB%d" % r, [32, GP, 128], BF16) for r in range(2)]
    LC = [k.sbuf("LCm%d" % r, [128, GP, 32], BF16) for r in range(2)]
    for r in range(2):
        k.dma("sp", LB[r].t[:, :, :], prm["lb_s"].t[r], prm["lb_s"], LB[r])
        k.dma("sp", LC[r].t[:, :, :], prm["lc_s"].t[r].rearrange("p g a b -> p g (a b)"), prm["lc_s"], LC[r])
    AS = k.sbuf("AS", [128, 2, GP], F32)
    AC = k.sbuf("AC", [128, 2, GP], F32)
    k.dma("sp", AS.t[:, 0, :], prm["abar_s"].t[:, 0, :], prm["abar_s"], AS)
    k.dma("sp", AS.t[:, 1, :], prm["abar_s"].t[:, 0, :], prm["abar_s"], AS)
    k.dma("sp", AC.t[:, 1, :], prm["abar_s"].t[:, 1, :], prm["abar_s"], AC)
    k.dma("sp", AC.t[:, 0, :], prm["abar_s"].t[:, 1, :], prm["abar_s"], AC)
    k.op("dve", lambda e: e.tensor_scalar(out=AC.t[:, 0, :], in0=AC.t[:, 0, :], scalar1=-1.0, scalar2=None, op0=ALU.mult), reads=[AC], writes=[AC])
    NH = 2 if GP % 2 == 0 else 1
    GH = GP // NH
    Xs = [k.sbuf("X%d" % i, [128, 2, GH, TBs + 1], F32) for i in range(NH)]
    pcon = k.psum("pcon", [128, 512])
    ptmp = k.psum("ptmp", [128, 512])
    ASp = pcon.t[:, 0:2 * GP].rearrange("p (a g) -> p a g", a=2)
    ACp = pcon.t[:, 2 * GP:4 * GP].rearrange("p (a g) -> p a g", a=2)
    k.op("dve", lambda e: e.tensor_copy(ASp, AS.t[:, :, :]), reads=[AS], writes=[pcon])
    k.op("dve", lambda e: e.tensor_copy(ACp, AC.t[:, :, :]), reads=[AC, pcon], writes=[pcon])
    Ps = [Buf(k, "Pp%d" % i, None) for i in range(NH)]
    Rs = [Buf(k, "Rp%d" % i, None) for i in range(NH)]
    for i in range(NH):
        Ps[i].ap = ptmp.t[:, (2 * i) * 2 * GH:(2 * i + 1) * 2 * GH].rearrange("p (a g) -> p a g", a=2)
        Rs[i].ap = ptmp.t[:, (2 * i + 1) * 2 * GH:(2 * i + 2) * 2 * GH].rearrange("p (a g) -> p a g", a=2)
    Qs = [k.sbuf("Q%d" % i, [128, 2, GH], F32) for i in range(NH)]
    for Xh in Xs:
        k.op("dve", lambda e, Xh=Xh: e.memset(Xh.t[:, :, :, :], 0.0), writes=[Xh])
    BU = k.sbuf("BU", [128, 2, GP, TBs], F32)
    XB = k.sbuf("XB", [128, 2, GP, TBs], BF16)
    x32 = Rot([k.sbuf("x32_%d" % i, [32, GP, TBs], BF16) for i in range(2)])
    y32 = Rot([k.sbuf("y32_%d" % i, [32, GP, TBs], F32) for i in range(2)])
    psb = Rot([k.psum("psb%d" % i, [128, 512]) for i in range(4)])
    psy = Rot([k.psum("psy%d" % i, [128, 512]) for i in range(2)])
    per = 512 // TBs
    for (t0, n) in token_blocks(T, TBs):
        xi = x32.next()
        k.dma("sp", xi.t[:, :, 0:n], hnTb.t[:, t0:t0 + n].rearrange("(g r) t -> r g t", r=32), hnTb, xi)
        for r in range(2):
            for g0 in range(0, GP, per):
                ng = min(per, GP - g0)
                ps = psb.next()

                def mm(e, r=r, g0=g0, ng=ng, ps=ps):
                    ins = None
                    for j in range(ng):
                        ins = e.matmul(ps.t[:, j * TBs:j * TBs + n], lhsT=LB[r].t[:, g0 + j, :], rhs=xi.t[:, g0 + j, 0:n], start=True, stop=True)
                    return ins
                k.op("pe", mm, reads=[LB[r], xi], writes=[ps])
                k.op("act", lambda e, r=r, g0=g0, ng=ng, ps=ps: e.activation(
                    out=BU.t[:, r, g0:g0 + ng, 0:n], in_=ps.t[:, 0:ng * TBs].rearrange("p (g t) -> p g t", t=TBs)[:, :, 0:n], func=AF.Copy),
                    reads=[ps], writes=[BU])
        for s in range(n):
            for hf in range(NH):
                g = slice(hf * GH, (hf + 1) * GH)
                Xh, Ph, Qh = Xs[hf], Ps[hf], Qs[hf]
                k.op("dve", lambda e, s=s, g=g, Xh=Xh, Ph=Ph: e.tensor_tensor(out=Ph.ap, in0=Xh.t[:, :, :, s], in1=ASp[:, :, g], op=ALU.mult), reads=[Xh, pcon], writes=[Ph])
                for r in range(2):
                    k.op("dve", lambda e, s=s, r=r, g=g, Xh=Xh, Qh=Qh: e.tensor_tensor(out=Qh.t[:, r, :], in0=Xh.t[:, 1 - r, :, s], in1=ACp[:, r, g], op=ALU.mult), reads=[Xh, pcon], writes=[Qh])
            for hf in range(NH):
                Ph, Qh, Rh = Ps[hf], Qs[hf], Rs[hf]
                k.op("dve", lambda e, Ph=Ph, Qh=Qh, Rh=Rh: e.tensor_tensor(out=Rh.ap, in0=Ph.ap, in1=Qh.t[:, :, :], op=ALU.add), reads=[Ph, Qh], writes=[Rh])
            for hf in range(NH):
                g = slice(hf * GH, (hf + 1) * GH)
                Xh, Rh = Xs[hf], Rs[hf]
                k.op("dve", lambda e, s=s, g=g, Xh=Xh, Rh=Rh: e.tensor_tensor(out=Xh.t[:, :, :, s + 1], in0=Rh.ap, in1=BU.t[:, :, g, s], op=ALU.add), reads=[Rh, BU], writes=[Xh])
        for hf in range(NH):
            g = slice(hf * GH, (hf + 1) * GH)
            k.op("dve", lambda e, hf=hf, g=g: e.tensor_copy(XB.t[:, :, g, 0:n], Xs[hf].t[:, :, :, 1:n + 1]), reads=[Xs[hf]], writes=[XB])
            k.op("dve", lambda e, hf=hf: e.tensor_copy(Xs[hf].t[:, :, :, 0], Xs[hf].t[:, :, :, n]), reads=[Xs[hf]], writes=[Xs[hf]])
        yo = y32.next()
        for g0 in range(0, GP, per):
            ng = min(per, GP - g0)
            ps = psy.next()

            def mm(e, g0=g0, ng=ng, ps=ps):
                ins = None
                for j in range(ng):
                    e.matmul(ps.t[0:32, j * TBs:j * TBs + n], lhsT=LC[0].t[:, g0 + j, :], rhs=XB.t[:, 0, g0 + j, 0:n], start=True, stop=False)
                    ins = e.matmul(ps.t[0:32, j * TBs:j * TBs + n], lhsT=LC[1].t[:, g0 + j, :], rhs=XB.t[:, 1, g0 + j, 0:n], start=False, stop=True)
                return ins
            k.op("pe", mm, reads=[LC[0], LC[1], XB], writes=[ps])
            k.op("act", lambda e, g0=g0, ng=ng, ps=ps: e.activation(
                out=yo.t[:, g0:g0 + ng, 0:n], in_=ps.t[0:32, 0:ng * TBs].rearrange("p (g t) -> p g t", t=TBs)[:, :, 0:n], func=AF.Copy),
                reads=[ps], writes=[yo])
        k.dma("pool", yT.t[:, t0:t0 + n].rearrange("(g r) t -> r g t", r=32), yo.t[:, :, 0:n], yo, yT)
    k.end_stage()
    k.begin_stage()
    dcol = k.sbuf("dcol", [128, KC], F32)
    k.dma("sp", dcol.t[:, :], prm["dcol"].t[:, :], prm["dcol"], dcol)
    yb = Rot([k.sbuf("yb%d" % i, [128, TB], F32) for i in range(2)])
    hb = Rot([k.sbuf("hb%d" % i, [128, TB], F32) for i in range(2)])
    v1 = k.sbuf("v1", [128, TB], F32)
    v2 = k.sbuf("v2", [128, TB], F32)
    zb = Rot([k.sbuf("zb%d" % i, [128, TB], BF16) for i in range(2)])
    C0 = math.sqrt(2.0 / math.pi)
    for (t0, n) in token_blocks(T, TB):
        for c in range(KC):
            y_, h_ = yb.next(), hb.next()
            k.dma("sp", y_.t[:, 0:n], yT.t[c * 128:(c + 1) * 128, t0:t0 + n], yT, y_)
            k.dma("sp", h_.t[:, 0:n], hnT.t[c * 128:(c + 1) * 128, t0:t0 + n], hnT, h_)
            k.op("dve", lambda e: e.scalar_tensor_tensor(out=v1.t[:, 0:n], in0=h_.t[:, 0:n], scalar=dcol.t[:, c:c + 1], in1=y_.t[:, 0:n],
                                                          op0=ALU.mult, op1=ALU.add), reads=[h_, dcol, y_], writes=[v1])
            k.op("act", lambda e: e.activation(out=v2.t[:, 0:n], in_=v1.t[:, 0:n], func=AF.Square), reads=[v1], writes=[v2])
            k.op("dve", lambda e: e.tensor_scalar(out=v2.t[:, 0:n], in0=v2.t[:, 0:n], scalar1=0.044715, scalar2=1.0, op0=ALU.mult, op1=ALU.add), reads=[v2], writes=[v2])
            k.op("dve", lambda e: e.tensor_tensor(out=v2.t[:, 0:n], in0=v2.t[:, 0:n], in1=v1.t[:, 0:n], op=ALU.mult), reads=[v2, v1], writes=[v2])
            k.op("act", lambda e: e.activation(out=v2.t[:, 0:n], in_=v2.t[:, 0:n], func=AF.Sigmoid, scale=2.0 * C0), reads=[v2], writes=[v2])
            z_ = zb.next()
            k.op("dve", lambda e: e.tensor_tensor(out=z_.t[:, 0:n], in0=v2.t[:, 0:n], in1=v1.t[:, 0:n], op=ALU.mult), reads=[v2, v1], writes=[z_])
            k.dma("pool", zT.t[c * 128:(c + 1) * 128, t0:t0 + n], z_.t[:, 0:n], z_, zT)
    k.end_stage()
```

```python
import math
from contextlib import ExitStack

import numpy as np
import ml_dtypes

import concourse.bass as bass
import concourse.mybir as mybir
from concourse.bass_utils import run_bass_kernel_spmd

F32 = mybir.dt.float32
BF16 = mybir.dt.bfloat16
AF = mybir.ActivationFunctionType
ALU = mybir.AluOpType


class Buf:
    def __init__(self, k, name, t, accumulate=False):
        self.k = k
        self.name = name
        self.t = t
        self.acc = accumulate
        self.writers = {}
        self.readers = {}
        self._in = None
        self._out = None

    def __getitem__(self, idx):
        return self.t[idx]

    def bump_in(self):
        if self._in is None:
            self._in = self.k.take_dsem()
        self._in[1] += 16
        return (self._in[0], self._in[1])

    def bump_out(self):
        if self._out is None:
            self._out = self.k.take_dsem()
        self._out[1] += 16
        return (self._out[0], self._out[1])


class KB:
    def __init__(self):
        self.nc = bass.Bass("TRN2", target_bir_lowering=False)
        self.es = ExitStack()
        self.nsem = 0
        nc = self.nc
        self.eng = {"pe": nc.tensor, "act": nc.scalar, "dve": nc.vector, "pool": nc.gpsimd, "sp": nc.sync}
        self.esem = {e: self.new_sem("e_" + e) for e in ("pe", "act", "dve", "pool")}
        self.ecnt = {e: 0 for e in self.esem}
        self.seen = {e: {} for e in self.eng}
        self.uid = 0

    def new_sem(self, name):
        self.nsem += 1
        return self.es.enter_context(self.nc.semaphore(name))

    def take_dsem(self):
        if not hasattr(self, "dpool"):
            self.dpool = []
            self.allsems = []
        if self.dpool:
            return self.dpool.pop()
        ent = [self.new_sem("d%d" % self.nsem), 0]
        self.allsems.append(ent)
        return ent

    def begin_stage(self):
        self.stage_es = ExitStack()
        self.stage_bufs = []

    def end_stage(self):
        for e in ("pe", "act", "dve", "pool", "sp"):
            for e2 in self.esem:
                if self.ecnt[e2] > 0:
                    self._wait(e, self.esem[e2], self.ecnt[e2])
            for ent in getattr(self, "allsems", []):
                if ent[1] > 0:
                    self._wait(e, ent[0], ent[1])
        for b in self.stage_bufs:
            for ent in (b._in, b._out):
                if ent is not None:
                    self.dpool.append(ent)
        self.stage_es.close()
        self.stage_es = None

    def dram(self, name, shape, dtype, kind="Internal"):
        t = self.nc.dram_tensor(name, list(shape), dtype, kind=kind)
        return Buf(self, name, t.ap(), accumulate=True)

    def sbuf(self, name, shape, dtype):
        st = getattr(self, "stage_es", None)
        self.uid += 1
        t = (st or self.es).enter_context(self.nc.sbuf_tensor("%s_%d" % (name, self.uid), list(shape), dtype))
        b = Buf(self, name, t)
        if st is not None:
            self.stage_bufs.append(b)
        return b

    def psum(self, name, shape, dtype=F32):
        st = getattr(self, "stage_es", None)
        self.uid += 1
        t = (st or self.es).enter_context(self.nc.psum_tensor("%s_%d" % (name, self.uid), list(shape), dtype))
        b = Buf(self, name, t)
        if st is not None:
            self.stage_bufs.append(b)
        return b

    def _wait(self, e, sem, val):
        sid = id(sem)
        if self.seen[e].get(sid, 0) >= val:
            return
        self.seen[e][sid] = val
        self.eng[e].wait_ge(sem, val)

    def _deps(self, e, reads, writes):
        for b in reads:
            for sem, val in b.writers.values():
                self._wait(e, sem, val)
        for b in writes:
            for sem, val in b.writers.values():
                if not b.acc:
                    self._wait(e, sem, val)
            for sem, val in b.readers.values():
                self._wait(e, sem, val)

    def _record(self, ev, reads, writes):
        sem, val = ev
        for b in reads:
            b.readers[id(sem)] = ev
        for b in writes:
            if b.acc:
                b.writers[id(sem)] = ev
            else:
                b.writers = {id(sem): ev}
                b.readers = {}

    def op(self, e, fn, reads=(), writes=()):
        self._deps(e, reads, writes)
        ins = fn(self.eng[e])
        self.ecnt[e] += 1
        ins.then_inc(self.esem[e], 1)
        self._record((self.esem[e], self.ecnt[e]), reads, writes)

    def dma(self, q, out_ap, in_ap, src, dst, **kw):
        self._deps(q, [src], [dst])
        ins = self.eng[q].dma_start(out=out_ap, in_=in_ap, **kw)
        if not dst.acc or src.acc:
            ev = dst.bump_in()
        else:
            ev = src.bump_out()
        ins.then_inc(ev[0], 16)
        self._record(ev, [src], [dst])

    def finish(self, outs):
        for b in outs:
            for sem, val in b.writers.values():
                self._wait("sp", sem, val)
        self.es.close()
        return self.nc


class Rot:
    def __init__(self, bufs):
        self.bufs = bufs
        self.i = 0

    def next(self):
        b = self.bufs[self.i % len(self.bufs)]
        self.i += 1
        return b


def token_blocks(T, TB):
    out = []
    t0 = 0
    while t0 < T:
        n = min(TB, T - t0)
        out.append((t0, n))
        t0 += n
    return out


def load_block_fm(k, dst, srcT, KC, t0, n):
    parts = srcT if isinstance(srcT, list) else [srcT]
    kpp = KC // len(parts)
    half = min(8, kpp)
    for pi, part in enumerate(parts):
        view = part.t.rearrange("(kc p) t -> p kc t", p=128)
        for a in range(0, kpp, half):
            b = min(kpp, a + half)
            k.dma("sp", dst.t[:, pi * kpp + a:pi * kpp + b, 0:n], view[:, a:b, t0:t0 + n], part, dst)


def rmsnorm_block(k, st, hblk, out_bf, KC, n, gcol, eps, want_rstd=None):
    D = KC * 128
    ps = st["ps_stat"]
    sq = st["sq"]

    for c in range(KC):
        s = sq.next()
        k.op("act", lambda e, s=s, c=c: e.activation(out=s.t[:, 0:n], in_=hblk.t[:, c, 0:n], func=AF.Square),
             reads=[hblk], writes=[s])
        last = (c == KC - 1)
        k.op("pe", lambda e, s=s, c=c: e.matmul(ps.t[:, 0:n], lhsT=st["ones"].t[:, :], rhs=s.t[:, 0:n],
                                                  start=(c == 0), stop=(c == KC - 1)),
             reads=[s, st["ones"]], writes=[ps] if c == 0 else [])
        if last:
            k._record((k.esem["pe"], k.ecnt["pe"]), [], [ps])
    rstd = st["rstd"]
    k.op("act", lambda e: e.activation(out=rstd.t[:, 0:n], in_=ps.t[:, 0:n], func=AF.Sqrt,
                                       bias=st["epsb"].t[:, 0:1], scale=1.0 / D),
         reads=[ps, st["epsb"]], writes=[rstd])
    k.op("dve", lambda e: e.reciprocal(out=rstd.t[:, 0:n], in_=rstd.t[:, 0:n]), reads=[rstd], writes=[rstd])
    for c in range(KC):
        k.op("dve", lambda e, c=c: e.scalar_tensor_tensor(out=out_bf.t[:, c, 0:n], in0=hblk.t[:, c, 0:n],
                                                          scalar=gcol.t[:, c:c + 1], in1=rstd.t[:, 0:n],
                                                          op0=ALU.mult, op1=ALU.mult),
             reads=[hblk, gcol, rstd], writes=[out_bf])
    if want_rstd is not None:
        k.op("act", lambda e: e.activation(out=want_rstd.t[:, 0:n], in_=rstd.t[:, 0:n], func=AF.Copy),
             reads=[rstd], writes=[want_rstd])


def fm_linear(k, xin, KC, n, w_dram, groups, wbufs, psums, epilogue, wk0=0):
    for gi, grp in enumerate(groups):
        pss = []
        for ch in grp:
            w = wbufs.next()
            k.dma("sp", w.t[:, :, :], w_dram.t[ch][:, wk0:wk0 + KC, :], w_dram, w)
            ps = psums.next()

            def mm(e, w=w, ps=ps):
                ins = None
                for kc in range(KC):
                    ins = e.matmul(ps.t[:, 0:n], lhsT=w.t[:, kc, :], rhs=xin.t[:, kc, 0:n],
                                   start=(kc == 0), stop=(kc == KC - 1))
                return ins
            k.op("pe", mm, reads=[w, xin], writes=[ps])
            pss.append(ps)
        epilogue(gi, pss)


class Cfg:
    def __init__(self, D=4096, SEQ=8192, H=16, NMETA=16):
        self.D, self.SEQ, self.H, self.NMETA = D, SEQ, H, NMETA
        self.T = SEQ + NMETA
        self.KC = D // 128
        self.NSH = 2 * H
        self.FF = 4 * D
        self.FC = self.FF // 128
        self.G = D // 16
        self.GP = self.G // 2
        self.EPS = 1e-6


def common_scratch(k, TB):
    ones = k.sbuf("ones", [128, 128], F32)
    epsb = k.sbuf("epsb", [128, 1], F32)
    k.op("dve", lambda e: e.memset(ones.t[:, :], 1.0), writes=[ones])
    k.op("dve", lambda e: e.memset(epsb.t[:, :], 1e-6), writes=[epsb])
    return dict(ones=ones, epsb=epsb, ps_stat=k.psum("ps_stat", [128, 512]),
                rstd=k.sbuf("rstd", [128, TB], F32),
                sq=Rot([k.sbuf("sq%d" % i, [128, TB], F32) for i in range(2)]))


def cast_weights(k, wf, wb, nchunks):
    for c in range(nchunks):
        k.dma("pool", wb.t[c], wf.t[c], wf, wb)


def stage_qkv(k, cfg, hT, gmix, wqk, wv, gqk, ropeC, ropeS, qk_dram, v_dram):
    TB = 512
    KC, T, NSH = cfg.KC, cfg.T, cfg.NSH
    k.begin_stage()
    st = common_scratch(k, TB)
    gcol = k.sbuf("gcol", [128, KC], F32)
    k.dma("sp", gcol.t[:, :], gmix.t[:, 0, :], gmix, gcol)
    gq = k.sbuf("gq", [128, 4], F32)
    k.dma("sp", gq.t[:, :], gqk.t[:, :], gqk, gq)
    bones = k.sbuf("bones", [128, 128], F32)
    k.op("dve", lambda e: e.memset(bones.t[:, :], 0.0), writes=[bones])
    k.op("dve", lambda e: e.memset(bones.t[0:64, 0:64], 1.0), reads=[bones], writes=[bones])
    k.op("dve", lambda e: e.memset(bones.t[64:128, 64:128], 1.0), reads=[bones], writes=[bones])
    hblk = k.sbuf("hblk", [128, KC, TB], F32)
    xin = k.sbuf("xin", [128, KC, TB], BF16)
    cs = k.sbuf("cs", [128, TB], F32)
    sn = k.sbuf("sn", [128, TB], F32)
    wbufs = Rot([k.sbuf("w%d" % i, [128, KC, 128], BF16) for i in range(3)])
    wvb = Rot([k.sbuf("wv%d" % i, [128, KC, 256], BF16) for i in range(2)])
    psums = Rot([k.psum("ps%d" % i, [128, 512]) for i in range(4)])
    psq = k.psum("psq", [128, 512])
    sqa = k.sbuf("sqa", [128, TB], F32)
    sqb = k.sbuf("sqb", [128, TB], F32)
    rs = k.sbuf("rs", [128, TB], F32)
    qa = k.sbuf("qa", [128, TB], F32)
    qb = k.sbuf("qb", [128, TB], F32)
    t1 = k.sbuf("t1", [128, TB], F32)
    t2 = k.sbuf("t2", [128, TB], F32)
    oa = Rot([k.sbuf("oa%d" % i, [128, TB], BF16) for i in range(2)])
    ob = Rot([k.sbuf("ob%d" % i, [128, TB], BF16) for i in range(2)])
    vo = Rot([k.sbuf("vo%d" % i, [128, 256], BF16) for i in range(2)])
    for (t0, n) in token_blocks(T, TB):
        load_block_fm(k, hblk, hT, KC, t0, n)
        k.dma("sp", cs.t[:, 0:n], ropeC.t[:, t0:t0 + n], ropeC, cs)
        k.dma("sp", sn.t[:, 0:n], ropeS.t[:, t0:t0 + n], ropeS, sn)
        rmsnorm_block(k, st, hblk, xin, KC, n, gcol, cfg.EPS)

        def epi(gi, pss, t0=t0, n=n):
            A, B = pss
            isk = 1 if gi >= NSH // 2 else 0
            j = gi - isk * (NSH // 2)
            k.op("act", lambda e: e.activation(out=sqa.t[:, 0:n], in_=A.t[:, 0:n], func=AF.Square), reads=[A], writes=[sqa])
            k.op("act", lambda e: e.activation(out=sqb.t[:, 0:n], in_=B.t[:, 0:n], func=AF.Square), reads=[B], writes=[sqb])

            def mm(e):
                e.matmul(psq.t[:, 0:n], lhsT=bones.t[:, :], rhs=sqa.t[:, 0:n], start=True, stop=False)
                return e.matmul(psq.t[:, 0:n], lhsT=bones.t[:, :], rhs=sqb.t[:, 0:n], start=False, stop=True)
            k.op("pe", mm, reads=[bones, sqa, sqb], writes=[psq])
            k.op("act", lambda e: e.activation(out=rs.t[:, 0:n], in_=psq.t[:, 0:n], func=AF.Sqrt,
                                               bias=st["epsb"].t[:, 0:1], scale=1.0 / 128), reads=[psq, st["epsb"]], writes=[rs])
            k.op("dve", lambda e: e.reciprocal(out=rs.t[:, 0:n], in_=rs.t[:, 0:n]), reads=[rs], writes=[rs])
            k.op("dve", lambda e: e.scalar_tensor_tensor(out=qa.t[:, 0:n], in0=A.t[:, 0:n], scalar=gq.t[:, 2 * isk:2 * isk + 1],
                                                          in1=rs.t[:, 0:n], op0=ALU.mult, op1=ALU.mult), reads=[A, gq, rs], writes=[qa])
            k.op("dve", lambda e: e.scalar_tensor_tensor(out=qb.t[:, 0:n], in0=B.t[:, 0:n], scalar=gq.t[:, 2 * isk + 1:2 * isk + 2],
                                                          in1=rs.t[:, 0:n], op0=ALU.mult, op1=ALU.mult), reads=[B, gq, rs], writes=[qb])
            o_a, o_b = oa.next(), ob.next()
            k.op("dve", lambda e: e.tensor_tensor(out=t1.t[:, 0:n], in0=qa.t[:, 0:n], in1=cs.t[:, 0:n], op=ALU.mult), reads=[qa, cs], writes=[t1])
            k.op("pool", lambda e: e.tensor_tensor(out=t2.t[:, 0:n], in0=qb.t[:, 0:n], in1=sn.t[:, 0:n], op=ALU.mult), reads=[qb, sn], writes=[t2])
            k.op("dve", lambda e: e.tensor_tensor(out=o_a.t[:, 0:n], in0=t1.t[:, 0:n], in1=t2.t[:, 0:n], op=ALU.subtract), reads=[t1, t2], writes=[o_a])
            k.op("dve", lambda e: e.tensor_tensor(out=t1.t[:, 0:n], in0=qb.t[:, 0:n], in1=cs.t[:, 0:n], op=ALU.mult), reads=[qb, cs], writes=[t1])
            k.op("pool", lambda e: e.tensor_tensor(out=t2.t[:, 0:n], in0=qa.t[:, 0:n], in1=sn.t[:, 0:n], op=ALU.mult), reads=[qa, sn], writes=[t2])
            k.op("dve", lambda e: e.tensor_tensor(out=o_b.t[:, 0:n], in0=t1.t[:, 0:n], in1=t2.t[:, 0:n], op=ALU.add), reads=[t1, t2], writes=[o_b])
            s0 = isk * NSH + 2 * j
            k.dma("pool", qk_dram.t[s0, 0:64, t0:t0 + n], o_a.t[0:64, 0:n], o_a, qk_dram)
            k.dma("pool", qk_dram.t[s0 + 1, 0:64, t0:t0 + n], o_a.t[64:128, 0:n], o_a, qk_dram)
            k.dma("pool", qk_dram.t[s0, 64:128, t0:t0 + n], o_b.t[0:64, 0:n], o_b, qk_dram)
            k.dma("pool", qk_dram.t[s0 + 1, 64:128, t0:t0 + n], o_b.t[64:128, 0:n], o_b, qk_dram)
        fm_linear(k, xin, KC, n, wqk, [[2 * p, 2 * p + 1] for p in range(NSH)], wbufs, psums, epi)
        for sl in range(cfg.D // 256):
            w = wvb.next()
            k.dma("sp", w.t[:, :, :], wv.t[sl], wv, w)
            for (tt0, tn) in token_blocks(n, 128):
                ps = psums.next()

                def mm(e, w=w, ps=ps, tt0=tt0, tn=tn):
                    ins = None
                    for kc in range(KC):
                        ins = e.matmul(ps.t[0:tn, 0:256], lhsT=xin.t[:, kc, tt0:tt0 + tn], rhs=w.t[:, kc, :],
                                       start=(kc == 0), stop=(kc == KC - 1))
                    return ins
                k.op("pe", mm, reads=[w, xin], writes=[ps])
                o = vo.next()
                k.op("act", lambda e, o=o, ps=ps, tn=tn: e.activation(out=o.t[0:tn, :], in_=ps.t[0:tn, 0:256], func=AF.Copy),
                     reads=[ps], writes=[o])
                k.dma("pool", v_dram.t[t0 + tt0:t0 + tt0 + tn, sl * 256:(sl + 1) * 256], o.t[0:tn, :], o, v_dram)
    k.end_stage()


def stage_attn(k, cfg, qk_dram, v_dram, lamv, subg, maskf, identf, oT, lam_init):
    T, NSH, H = cfg.T, cfg.NSH, cfg.H
    NT = (T + 127) // 128
    NTF = T // 128
    QB = 256
    k.begin_stage()
    mk32 = k.sbuf("mk32", [128, 128], F32)
    id32 = k.sbuf("id32", [128, 128], F32)
    mask = k.sbuf("mask", [128, 128], BF16)
    ident = k.sbuf("ident", [128, 128], BF16)
    k.dma("sp", mk32.t[:, :], maskf.t[:, :], maskf, mk32)
    k.dma("sp", id32.t[:, :], identf.t[:, :], identf, id32)
    k.op("dve", lambda e: e.tensor_copy(mask.t[:, :], mk32.t[:, :]), reads=[mk32], writes=[mask])
    k.op("dve", lambda e: e.tensor_copy(ident.t[:, :], id32.t[:, :]), reads=[id32], writes=[ident])
    sg = k.sbuf("sg", [128, 256], F32)
    k.dma("sp", sg.t[:, :], subg.t[:, :], subg, sg)
    lv = k.sbuf("lv", [128, 512], F32)
    k.dma("sp", lv.t[:, :], lamv.t[:, :], lamv, lv)
    lt = k.sbuf("lt", [128, 256], F32)
    l2 = k.sbuf("l2", [128, 2], F32)
    nlam = k.sbuf("nlam", [128, 1], F32)
    k.op("dve", lambda e: e.tensor_tensor(out=lt.t[:, 0:128], in0=lv.t[:, 0:128], in1=lv.t[:, 128:256], op=ALU.mult), reads=[lv], writes=[lt])
    k.op("dve", lambda e: e.tensor_tensor(out=lt.t[:, 128:256], in0=lv.t[:, 256:384], in1=lv.t[:, 384:512], op=ALU.mult), reads=[lv, lt], writes=[lt])
    k.op("dve", lambda e: e.tensor_reduce(out=l2.t[:, :], in_=lt.t[:, :].rearrange("p (a b) -> p a b", a=2), axis=mybir.AxisListType.X, op=ALU.add),
         reads=[lt], writes=[l2])
    k.op("act", lambda e: e.activation(out=l2.t[:, :], in_=l2.t[:, :], func=AF.Exp), reads=[l2], writes=[l2])
    k.op("dve", lambda e: e.scalar_tensor_tensor(out=nlam.t[:, :], in0=l2.t[:, 1:2], scalar=-float(lam_init), in1=l2.t[:, 0:1],
                                                  op0=ALU.add, op1=ALU.subtract), reads=[l2], writes=[nlam])
    qs_ = [k.sbuf("q%d" % c, [128, T], BF16) for c in range(2)]
    ks_ = [k.sbuf("k%d" % c, [128, T], BF16) for c in range(2)]
    vh = k.sbuf("vh", [128, NT, 257], BF16)
    k.op("dve", lambda e: e.memset(vh.t[:, :, 256:257], 1.0), writes=[vh])
    pss = Rot([k.psum("pss%d" % i, [128, 512]) for i in range(3)])
    accs = [[k.psum("acc%d%d" % (c, i), [128, 512]) for i in range(2)] for c in range(2)]
    ptr = k.psum("ptr", [128, 1024], BF16)
    pT = Rot([k.sbuf("pT%d" % i, [128, 2, QB], BF16) for i in range(6)])
    r0 = k.sbuf("r0", [128, 1], F32)
    r1 = k.sbuf("r1", [128, 1], F32)
    of = k.sbuf("of", [128, 256], F32)
    junk = k.sbuf("junk", [128, 256], F32)
    ss = k.sbuf("ss", [128, 1], F32)
    ob = k.sbuf("ob", [128, 256], BF16)
    oTs = Rot([k.sbuf("oTs%d" % i, [128, 2, 128], BF16) for i in range(2)])
    scale = 128.0 ** -0.5
    epsb_att = k.sbuf("epsb_att", [128, 1], F32)
    k.op("dve", lambda e: e.memset(epsb_att.t[:, :], 1e-6), writes=[epsb_att])
    for h in range(H):
        for c in range(2):
            k.dma("sp", qs_[c].t[:, :], qk_dram.t[2 * h + c], qk_dram, qs_[c])
            k.dma("sp", ks_[c].t[:, :], qk_dram.t[NSH + 2 * h + c], qk_dram, ks_[c])
        for i0 in range(0, NTF, 8):
            i1 = min(NTF, i0 + 8)
            k.dma("sp", vh.t[:, i0:i1, 0:256],
                  v_dram.t[i0 * 128:i1 * 128, h * 256:(h + 1) * 256].rearrange("(i p) e -> p i e", p=128), v_dram, vh)
        if NT > NTF:
            rem = T - NTF * 128
            k.dma("sp", vh.t[0:rem, NTF, 0:256], v_dram.t[NTF * 128:T, h * 256:(h + 1) * 256], v_dram, vh)
        for (q0, nq) in token_blocks(T, QB):
            tq0 = q0 // 128
            qtiles = token_blocks(nq, 128)
            last_kt = (q0 + nq - 1) // 128
            pend = []

            def emit_pv(p, kt, nk, qs):
                for qi, (qq0, nqt) in enumerate(qtiles):
                    tq = tq0 + qi
                    if tq < kt:
                        continue
                    off = 128 * tq - qs
                    for c in range(2):
                        acc = accs[c][qi]
                        k.op("pe", lambda e, acc=acc, c=c, off=off, nqt=nqt, tq=tq: e.matmul(
                            acc.t[0:nqt, 0:257], lhsT=p.t[0:nk, c, off:off + nqt], rhs=vh.t[0:nk, kt, :],
                            start=(kt == 0), stop=(kt == tq)),
                            reads=[p, vh], writes=[acc] if kt == 0 else [])
                        if kt == tq:
                            k._record((k.esem["pe"], k.ecnt["pe"]), [], [acc])
            for kt in range(last_kt + 1):
                nk = min(128, T - 128 * kt)
                qs = max(q0, 128 * kt)
                nqq = q0 + nq - qs
                ps = pss.next()

                def mm(e, ps=ps, kt=kt, nk=nk, qs=qs, nqq=nqq):
                    ins = None
                    for c in range(2):
                        ins = e.matmul(ps.t[0:nk, c * QB:c * QB + nqq], lhsT=ks_[c].t[:, 128 * kt:128 * kt + nk],
                                       rhs=qs_[c].t[:, qs:qs + nqq], start=True, stop=True)
                    return ins
                k.op("pe", mm, reads=[ks_[0], ks_[1], qs_[0], qs_[1]], writes=[ps])
                p = pT.next()
                k.op("act", lambda e, p=p, ps=ps, nk=nk, nqq=nqq: e.activation(
                    out=p.t[0:nk, :, 0:nqq], in_=ps.t[0:nk, :].rearrange("p (c q) -> p c q", c=2)[:, :, 0:nqq],
                    func=AF.Exp, scale=scale), reads=[ps], writes=[p])
                if 128 * kt >= q0:
                    w = min(128, nqq)
                    for c in range(2):
                        k.op("pool", lambda e, p=p, c=c, nk=nk, w=w: e.tensor_tensor(
                            out=p.t[0:nk, c, 0:w], in0=p.t[0:nk, c, 0:w], in1=mask.t[0:nk, 0:w], op=ALU.mult),
                            reads=[p, mask], writes=[p])
                pend.append((p, kt, nk, qs))
                if len(pend) > 2:
                    emit_pv(*pend.pop(0))
            while pend:
                emit_pv(*pend.pop(0))
            for qi, (qq0, nqt) in enumerate(qtiles):
                tq = tq0 + qi
                a0, a1 = accs[0][qi], accs[1][qi]
                k.op("dve", lambda e: e.reciprocal(out=r0.t[0:nqt, :], in_=a0.t[0:nqt, 256:257]), reads=[a0], writes=[r0])
                k.op("dve", lambda e: e.reciprocal(out=r1.t[0:nqt, :], in_=a1.t[0:nqt, 256:257]), reads=[a1], writes=[r1])
                k.op("dve", lambda e: e.tensor_tensor(out=r1.t[0:nqt, :], in0=r1.t[0:nqt, :], in1=nlam.t[0:nqt, :], op=ALU.mult),
                     reads=[r1, nlam], writes=[r1])
                k.op("dve", lambda e: e.tensor_scalar(out=of.t[0:nqt, :], in0=a0.t[0:nqt, 0:256], scalar1=r0.t[0:nqt, 0:1], scalar2=None,
                                                       op0=ALU.mult), reads=[a0, r0], writes=[of])
                k.op("dve", lambda e: e.scalar_tensor_tensor(out=of.t[0:nqt, :], in0=a1.t[0:nqt, 0:256], scalar=r1.t[0:nqt, 0:1],
                                                              in1=of.t[0:nqt, :], op0=ALU.mult, op1=ALU.add), reads=[a1, r1, of], writes=[of])
                k.op("act", lambda e: e.activation(out=junk.t[0:nqt, :], in_=of.t[0:nqt, :], func=AF.Square, accum_out=ss.t[0:nqt, :]),
                     reads=[of], writes=[junk, ss])
                k.op("act", lambda e: e.activation(out=ss.t[0:nqt, :], in_=ss.t[0:nqt, :], func=AF.Sqrt, bias=epsb_att.t[0:nqt, 0:1], scale=1.0 / 256),
                     reads=[ss, epsb_att], writes=[ss])
                k.op("dve", lambda e: e.reciprocal(out=ss.t[0:nqt, :], in_=ss.t[0:nqt, :]), reads=[ss], writes=[ss])
                k.op("dve", lambda e: e.tensor_scalar(out=ss.t[0:nqt, :], in0=ss.t[0:nqt, :], scalar1=float(1.0 - lam_init), scalar2=None, op0=ALU.mult),
                     reads=[ss], writes=[ss])
                k.op("dve", lambda e: e.scalar_tensor_tensor(out=ob.t[0:nqt, :], in0=of.t[0:nqt, :], scalar=ss.t[0:nqt, 0:1], in1=sg.t[0:nqt, :],
                                                              op0=ALU.mult, op1=ALU.mult), reads=[of, ss, sg], writes=[ob])

                def tr(e):
                    ins = None
                    for e2 in range(2):
                        ins = e.transpose(ptr.t[:, e2 * 128:e2 * 128 + nqt], ob.t[0:nqt, e2 * 128:(e2 + 1) * 128], ident.t[0:nqt, 0:nqt])
                    return ins
                k.op("pe", tr, reads=[ob, ident], writes=[ptr])
                o = oTs.next()
                k.op("act", lambda e, o=o: e.activation(out=o.t[:, :, 0:nqt], in_=ptr.t[:, 0:256].rearrange("p (a b) -> p a b", a=2)[:, :, 0:nqt],
                                                        func=AF.Copy), reads=[ptr], writes=[o])
                k.dma("pool", oT.t[h * 256:(h + 1) * 256, 128 * tq:128 * tq + nqt].rearrange("(a p) t -> p a t", p=128),
                      o.t[:, :, 0:nqt], o, oT)
    k.end_stage()


def stage_linear_ksplit(k, cfg, name, xparts, KCin, w, nout, resT, outT, TB):
    T = cfg.T
    NP = len(xparts)
    KCh = KCin // NP
    k.begin_stage()
    xin = k.sbuf("xin", [128, KCh, TB], BF16)
    wbufs = Rot([k.sbuf("w%d" % i, [128, KCh, 128], BF16) for i in range(3)])
    psums = Rot([k.psum("ps%d" % i, [128, 512]) for i in range(4)])
    part = k.sbuf("part", [128, nout, TB], F32)
    rbuf = Rot([k.sbuf("r%d" % i, [128, TB], F32) for i in range(2)])
    obuf = Rot([k.sbuf("o%d" % i, [128, TB], F32) for i in range(2)])
    for (t0, n) in token_blocks(T, TB):
        for h in range(NP):
            load_block_fm(k, xin, xparts[h], KCh, t0, n)

            def epi(gi, pss, t0=t0, n=n, h=h):
                if h == 0:
                    k.op("act", lambda e: e.activation(out=part.t[:, gi, 0:n], in_=pss[0].t[:, 0:n], func=AF.Copy), reads=[pss[0]], writes=[part])
                    return
                if h < NP - 1:
                    k.op("dve", lambda e: e.tensor_tensor(out=part.t[:, gi, 0:n], in0=pss[0].t[:, 0:n], in1=part.t[:, gi, 0:n], op=ALU.add),
                         reads=[pss[0], part], writes=[part])
                    return
                r = rbuf.next()
                k.dma("pool", r.t[:, 0:n], resT.t[gi * 128:(gi + 1) * 128, t0:t0 + n], resT, r)
                o = obuf.next()
                k.op("dve", lambda e: e.tensor_tensor(out=o.t[:, 0:n], in0=pss[0].t[:, 0:n], in1=part.t[:, gi, 0:n], op=ALU.add), reads=[pss[0], part], writes=[o])
                k.op("pool", lambda e: e.tensor_tensor(out=o.t[:, 0:n], in0=o.t[:, 0:n], in1=r.t[:, 0:n], op=ALU.add), reads=[o, r], writes=[o])
                k.dma("pool", outT.t[gi * 128:(gi + 1) * 128, t0:t0 + n], o.t[:, 0:n], o, outT)
            fm_linear(k, xin, KCh, n, w, [[c] for c in range(nout)], wbufs, psums, epi, wk0=h * KCh)
    k.end_stage()


def stage_linear_res(k, cfg, name, xT_bf, KCin, w, nout, resT, outT, TB, bias=None, glu=False):
    T = cfg.T
    k.begin_stage()
    xin = k.sbuf("xin", [128, KCin, TB], BF16)
    wbufs = Rot([k.sbuf("w%d" % i, [128, KCin, 128], BF16) for i in range(3)])
    psums = Rot([k.psum("ps%d" % i, [128, 512]) for i in range(4)])
    rbuf = Rot([k.sbuf("r%d" % i, [128, TB], F32) for i in range(2)])
    obuf = Rot([k.sbuf("o%d" % i, [128, TB], F32) for i in range(2)])
    sgb = k.sbuf("sgb", [128, TB], F32)
    if bias is not None:
        bcol = k.sbuf("bcol", [128, 2 * nout], F32)
        k.dma("sp", bcol.t[:, :], bias.t[:, :], bias, bcol)
    for (t0, n) in token_blocks(T, TB):
        load_block_fm(k, xin, xT_bf, KCin, t0, n)

        def epi(gi, pss, t0=t0, n=n):
            r = rbuf.next()
            k.dma("pool", r.t[:, 0:n], resT.t[gi * 128:(gi + 1) * 128, t0:t0 + n], resT, r)
            o = obuf.next()
            if glu:
                k.op("act", lambda e: e.activation(out=sgb.t[:, 0:n], in_=pss[1].t[:, 0:n], func=AF.Sigmoid,
                                                   bias=bcol.t[:, nout + gi:nout + gi + 1], scale=1.0), reads=[pss[1], bcol], writes=[sgb])
                k.op("dve", lambda e: e.scalar_tensor_tensor(out=o.t[:, 0:n], in0=pss[0].t[:, 0:n], scalar=bcol.t[:, gi:gi + 1],
                                                              in1=sgb.t[:, 0:n], op0=ALU.add, op1=ALU.mult), reads=[pss[0], bcol, sgb], writes=[o])
                k.op("dve", lambda e: e.tensor_tensor(out=o.t[:, 0:n], in0=o.t[:, 0:n], in1=r.t[:, 0:n], op=ALU.add), reads=[o, r], writes=[o])
            else:
                k.op("dve", lambda e: e.tensor_tensor(out=o.t[:, 0:n], in0=pss[0].t[:, 0:n], in1=r.t[:, 0:n], op=ALU.add), reads=[pss[0], r], writes=[o])
            k.dma("pool", outT.t[gi * 128:(gi + 1) * 128, t0:t0 + n], o.t[:, 0:n], o, outT)
        groups = [[c, nout + c] for c in range(nout)] if glu else [[c] for c in range(nout)]
        fm_linear(k, xin, KCin, n, w, groups, wbufs, psums, epi)
    k.end_stage()


def stage_up(k, cfg, hT, gmlp, layer, wup, uT):
    TB = 512
    KC, T = cfg.KC, cfg.T
    k.begin_stage()
    st = common_scratch(k, TB)
    gcol = k.sbuf("gcol", [128, KC], F32)
    k.dma("sp", gcol.t[:, :], gmlp.t[:, layer, :], gmlp, gcol)
    hblk = k.sbuf("hblk", [128, KC, TB], F32)
    xin = k.sbuf("xin", [128, KC, TB], BF16)
    wbufs = Rot([k.sbuf("w%d" % i, [128, KC, 128], BF16) for i in range(3)])
    psums = Rot([k.psum("ps%d" % i, [128, 512]) for i in range(4)])
    rl = Rot([k.sbuf("rl%d" % i, [128, TB], F32) for i in range(2)])
    ub = Rot([k.sbuf("ub%d" % i, [128, TB], BF16) for i in range(3)])
    for (t0, n) in token_blocks(T, TB):
        load_block_fm(k, hblk, hT, KC, t0, n)
        rmsnorm_block(k, st, hblk, xin, KC, n, gcol, cfg.EPS)

        def epi(gi, pss, t0=t0, n=n):
            r = rl.next()
            u = ub.next()
            k.op("act", lambda e: e.activation(out=r.t[:, 0:n], in_=pss[0].t[:, 0:n], func=AF.Relu), reads=[pss[0]], writes=[r])
            k.op("dve", lambda e: e.tensor_tensor(out=u.t[:, 0:n], in0=r.t[:, 0:n], in1=r.t[:, 0:n], op=ALU.mult), reads=[r], writes=[u])
            cpp = cfg.FC // len(uT)
            up_ = uT[gi // cpp]
            gl = gi % cpp
            k.dma("pool", up_.t[gl * 128:(gl + 1) * 128, t0:t0 + n], u.t[:, 0:n], u, up_)
        fm_linear(k, xin, KC, n, wup, [[c] for c in range(cfg.FC)], wbufs, psums, epi)
    k.end_stage()


def rope_tables(T):
    inv = (np.float32(10000.0) ** (-np.arange(0, 128, 2, dtype=np.float32) / np.float32(128))).astype(np.float32)
    ang = (np.arange(T, dtype=np.float32)[:, None] * inv[None, :]).astype(np.float32)
    c = np.cos(ang).astype(np.float32).T
    s = np.sin(ang).astype(np.float32).T
    return np.ascontiguousarray(np.concatenate([c, c], 0)), np.ascontiguousarray(np.concatenate([s, s], 0))


def arr_w(W, cw=128):
    K, N = W.shape
    return np.ascontiguousarray(W.reshape(K // 128, 128, N // cw, cw).transpose(2, 1, 0, 3))


def col_g(g):
    return np.ascontiguousarray(g.reshape(-1, 128).T)


def build_program(cfg, layers=2):
    k = KB()
    D, T, KC, NSH, FC = cfg.D, cfg.T, cfg.KC, cfg.NSH, cfg.FC
    ein = lambda name, shape: k.dram(name, shape, F32, kind="ExternalInput")
    hT0 = ein("hT0", [D, T])
    gmix = ein("gmix", [128, 2, KC])
    gmlp = ein("gmlp", [128, 2, KC])
    wqk_f = ein("wqk", [2 * NSH, 128, KC, 128])
    wv_f = ein("wv", [D // 256, 128, KC, 256])
    gqk = ein("gqk", [128, 4])
    ropeC = ein("ropeC", [128, T])
    ropeS = ein("ropeS", [128, T])
    lamv = ein("lamv", [128, 512])
    subg = ein("subg", [128, 256])
    maskf = ein("maskf", [128, 128])
    identf = ein("identf", [128, 128])
    wo_f = ein("wo", [KC, 128, KC, 128])
    wup_f = [ein("wup%d" % l, [FC, 128, KC, 128]) for l in range(layers)]
    wdn_f = [ein("wdn%d" % l, [KC, 128, FC, 128]) for l in range(layers)]
    outT = k.dram("outT", [D, T], F32, kind="ExternalOutput")
    wqk = k.dram("wqk_b", [2 * NSH, 128, KC, 128], BF16)
    wv = k.dram("wv_b", [D // 256, 128, KC, 256], BF16)
    wo = k.dram("wo_b", [KC, 128, KC, 128], BF16)
    wup = [k.dram("wup_b%d" % l, [FC, 128, KC, 128], BF16) for l in range(layers)]
    wdn = [k.dram("wdn_b%d" % l, [KC, 128, FC, 128], BF16) for l in range(layers)]
    cast_weights(k, wqk_f, wqk, 2 * NSH)
    cast_weights(k, wv_f, wv, D // 256)
    cast_weights(k, wo_f, wo, KC)
    for l in range(layers):
        cast_weights(k, wup_f[l], wup[l], FC)
        cast_weights(k, wdn_f[l], wdn[l], KC)
    qk_dram = k.dram("qk_s", [2 * NSH, 128, T], BF16)
    v_dram = k.dram("v_s", [T, D], BF16)
    oT = k.dram("oT_s", [D, T], BF16)
    h1T = k.dram("h1T", [D, T], F32)
    uT = [k.dram("uT%d" % i, [cfg.FF // 2, T], BF16) for i in range(2)]
    h2T = k.dram("h2T", [D, T], F32) if layers > 1 else outT
    lam0 = 0.8 - 0.6 * math.exp(-0.3 * 0)
    stage_qkv(k, cfg, hT0, gmix, wqk, wv, gqk, ropeC, ropeS, qk_dram, v_dram)
    stage_attn(k, cfg, qk_dram, v_dram, lamv, subg, maskf, identf, oT, lam0)
    stage_linear_res(k, cfg, "wo", oT, KC, wo, KC, hT0, h1T, 512)
    stage_up(k, cfg, h1T, gmlp, 0, wup[0], uT)
    stage_linear_ksplit(k, cfg, "dn0", uT, FC, wdn[0], KC, h1T, h2T, 512)
    if layers > 1:
        GP = cfg.GP
        prm = {}
        for nm, shp in (("ar", [128, GP]), ("ai", [128, GP]), ("ldt", [128, GP]), ("br", [128, GP, 16]), ("bi", [128, GP, 16]),
                        ("cr", [128, GP, 16]), ("ci", [128, GP, 16]), ("m01", [128, 2]), ("dcol", [128, KC]), ("bglu", [128, 2 * KC])):
            prm[nm] = ein(nm, shp)
        prm["identf"] = identf
        wglu_f = ein("wglu", [2 * KC, 128, KC, 128])
        wglu = k.dram("wglu_b", [2 * KC, 128, KC, 128], BF16)
        cast_weights(k, wglu_f, wglu, 2 * KC)
        prm["abar_s"] = k.dram("abar_s", [128, 2, GP], F32)
        prm["lc_s"] = k.dram("lc_s", [2, 128, GP, 2, 16], BF16)
        prm["lb_s"] = k.dram("lb_s", [2, 32, GP, 128], BF16)
        hnT = k.dram("hnT", [D, T], F32)
        hnTb = k.dram("hnTb", [D, T], BF16)
        yT = k.dram("yT", [D, T], F32)
        zT = k.dram("zT", [D, T], BF16)
        h3T = k.dram("h3T", [D, T], F32)
        stage_s5(k, cfg, h2T, gmix, prm, hnT, hnTb, yT, zT)
        stage_linear_res(k, cfg, "glu", zT, KC, wglu, KC, h2T, h3T, 512, bias=prm["bglu"], glu=True)
        stage_up(k, cfg, h3T, gmlp, 1, wup[1], uT)
        stage_linear_ksplit(k, cfg, "dn1", uT, FC, wdn[1], KC, h3T, outT, 512)
    return k.finish([outT]), k


def prep_inputs(cfg, x_b, meta_tokens, norm_mix_g, norm_mlp_g, da_w_qkv, da_q_norm_g, da_k_norm_g,
                da_lambda, da_subln_g, da_w_o, mlp_w_up, mlp_w_down, layers=2, shared=None, ssm=None):
    D, T, NSH = cfg.D, cfg.T, cfg.NSH
    f = np.float32
    m = {}
    m["hT0"] = np.ascontiguousarray(np.concatenate([meta_tokens, x_b], 0).T.astype(f))
    if shared is not None:
        m.update(shared)
        return m, shared
    sh = {}
    sh["gmix"] = np.ascontiguousarray(np.stack([col_g(norm_mix_g[min(l, len(norm_mix_g) - 1)]) for l in range(2)], 1))
    sh["gmlp"] = np.ascontiguousarray(np.stack([col_g(norm_mlp_g[min(l, len(norm_mlp_g) - 1)]) for l in range(2)], 1))
    Wqkv = da_w_qkv[0]
    cols = []
    for base in (0, D):
        for j in range(NSH // 2):
            s0, s1 = base + (2 * j) * 128, base + (2 * j + 1) * 128
            cols.append(np.concatenate([np.arange(s0, s0 + 64), np.arange(s1, s1 + 64)]))
            cols.append(np.concatenate([np.arange(s0 + 64, s0 + 128), np.arange(s1 + 64, s1 + 128)]))
    cols = np.concatenate(cols)
    sh["wqk"] = arr_w(Wqkv[:, cols])
    sh["wv"] = arr_w(Wqkv[:, 2 * D:3 * D], 256)
    gq, gk = da_q_norm_g[0], da_k_norm_g[0]
    lo = np.tile(np.arange(64), 2)
    sh["gqk"] = np.ascontiguousarray(np.stack([gq[lo], gq[64 + lo], gk[lo], gk[64 + lo]], 1).astype(f))
    c, s = rope_tables(T)
    sh["ropeC"], sh["ropeS"] = c, s
    sh["lamv"] = np.ascontiguousarray(np.broadcast_to(da_lambda[0].reshape(1, 512), (128, 512)).astype(f))
    sh["subg"] = np.ascontiguousarray(np.broadcast_to(da_subln_g[0].reshape(1, 256), (128, 256)).astype(f))
    sh["maskf"] = np.triu(np.ones((128, 128), f))
    sh["identf"] = np.eye(128, dtype=f)
    sh["wo"] = arr_w(da_w_o[0])
    for l in range(layers):
        sh["wup%d" % l] = arr_w(mlp_w_up[l])
        sh["wdn%d" % l] = arr_w(mlp_w_down[l])
    if ssm is not None:
        a_re, a_im, log_dt, b_re, b_im, c_re, c_im, d_skip, w_glu, b_glu = ssm
        G = a_re.shape[0]
        GP = G // 2
        sh["ar"] = np.ascontiguousarray(a_re.reshape(GP, 2, 64).transpose(1, 2, 0).reshape(128, GP))
        sh["ai"] = np.ascontiguousarray(a_im.reshape(GP, 2, 64).transpose(1, 2, 0).reshape(128, GP))
        sh["ldt"] = np.ascontiguousarray(np.broadcast_to(log_dt.reshape(GP, 2).T[:, None, :], (2, 64, GP)).reshape(128, GP).astype(f))
        sh["br"] = np.ascontiguousarray(b_re.reshape(GP, 2, 64, 16).transpose(1, 2, 0, 3).reshape(128, GP, 16))
        sh["bi"] = np.ascontiguousarray(b_im.reshape(GP, 2, 64, 16).transpose(1, 2, 0, 3).reshape(128, GP, 16))
        sh["cr"] = np.ascontiguousarray(c_re.reshape(GP, 2, 16, 64).transpose(1, 3, 0, 2).reshape(128, GP, 16))
        sh["ci"] = np.ascontiguousarray(c_im.reshape(GP, 2, 16, 64).transpose(1, 3, 0, 2).reshape(128, GP, 16))
        m01 = np.zeros((128, 2), f)
        m01[:64, 0] = 1.0
        m01[64:, 1] = 1.0
        sh["m01"] = m01
        sh["dcol"] = col_g(d_skip)
        sh["bglu"] = col_g(b_glu)
        sh["wglu"] = arr_w(w_glu)
    m.update(sh)
    return m, sh


def kernel(x, meta_tokens, norm_mix_g, norm_mlp_g, da_w_qkv, da_q_norm_g, da_k_norm_g, da_lambda, da_subln_g, da_w_o,
           ssm_a_re, ssm_a_im, ssm_log_dt, ssm_b_re, ssm_b_im, ssm_c_re, ssm_c_im, ssm_d, ssm_w_glu, ssm_b_glu,
           mlp_w_up, mlp_w_down):
    a = lambda v: np.asarray(v, dtype=np.float32)
    x = a(x)
    B, SEQ, D = x.shape
    cfg = Cfg(D=D, SEQ=SEQ, H=D // 256, NMETA=meta_tokens.shape[0])
    nc, _ = build_program(cfg, layers=2)
    ssm = tuple(a(v)[0] for v in (ssm_a_re, ssm_a_im, ssm_log_dt, ssm_b_re, ssm_b_im, ssm_c_re, ssm_c_im, ssm_d, ssm_w_glu, ssm_b_glu))
    maps, sh = [], None
    for b in range(B):
        m, sh = prep_inputs(cfg, x[b], a(meta_tokens), a(norm_mix_g), a(norm_mlp_g), a(da_w_qkv), a(da_q_norm_g), a(da_k_norm_g),
                            a(da_lambda), a(da_subln_g), a(da_w_o), a(mlp_w_up), a(mlp_w_down), layers=2, shared=sh, ssm=ssm)
        maps.append(m)
    res = run_bass_kernel_spmd(nc, maps, core_ids=list(range(B)))
    nm = cfg.NMETA
    out = np.stack([np.ascontiguousarray(res.results[b]["outT"][:, nm:nm + SEQ].T) for b in range(B)])
    return out.astype(np.float32)


def stage_s5(k, cfg, hT, gmix, prm, hnT, hnTb, yT, zT):
    D, T, KC, GP = cfg.D, cfg.T, cfg.KC, cfg.GP
    TWO_PI = 2.0 * math.pi
    TB = 512
    k.begin_stage()
    st = common_scratch(k, TB)
    gcol = k.sbuf("gcol", [128, KC], F32)
    k.dma("sp", gcol.t[:, :], gmix.t[:, 1, :], gmix, gcol)
    hblk = k.sbuf("hblk", [128, KC, TB], F32)
    xin = k.sbuf("xin", [128, KC, TB], BF16)
    hn = k.sbuf("hn", [128, KC, TB], F32)
    for (t0, n) in token_blocks(T, TB):
        load_block_fm(k, hblk, hT, KC, t0, n)
        rmsnorm_block(k, st, hblk, xin, KC, n, gcol, cfg.EPS)
        for c in range(KC):
            k.op("dve", lambda e, c=c: e.scalar_tensor_tensor(out=hn.t[:, c, 0:n], in0=hblk.t[:, c, 0:n], scalar=gcol.t[:, c:c + 1],
                                                               in1=st["rstd"].t[:, 0:n], op0=ALU.mult, op1=ALU.mult),
                 reads=[hblk, gcol, st["rstd"]], writes=[hn])
        for c0 in range(0, KC, 8):
            c1 = min(KC, c0 + 8)
            k.dma("pool", hnT.t[c0 * 128:c1 * 128, t0:t0 + n].rearrange("(kc p) t -> p kc t", p=128), hn.t[:, c0:c1, 0:n], hn, hnT)
            k.dma("pool", hnTb.t[c0 * 128:c1 * 128, t0:t0 + n].rearrange("(kc p) t -> p kc t", p=128), xin.t[:, c0:c1, 0:n], xin, hnTb)
    k.end_stage()
    k.begin_stage()
    ld = lambda name, shape: (lambda b: (k.dma("sp", b.t[tuple(slice(None) for _ in shape)], prm[name].t[tuple(slice(None) for _ in shape)], prm[name], b), b)[1])(k.sbuf(name, shape, F32))
    ar, ai, ldt = ld("ar", [128, GP]), ld("ai", [128, GP]), ld("ldt", [128, GP])
    br, bi = ld("br", [128, GP, 16]), ld("bi", [128, GP, 16])
    cr, ci = ld("cr", [128, GP, 16]), ld("ci", [128, GP, 16])
    m01 = ld("m01", [128, 2])
    idf = ld("identf", [128, 128])
    ident = k.sbuf("identb", [128, 128], BF16)
    k.op("dve", lambda e: e.tensor_copy(ident.t[:, :], idf.t[:, :]), reads=[idf], writes=[ident])
    S = lambda name: k.sbuf(name, [128, GP], F32)
    dt, mag, ang, kk, tmp, sn_, cs_, abr, abi, den, fr, fi, t1, t2 = [S("s5_%d" % i) for i in range(14)]
    tt = lambda out, a, b, op: k.op("dve", lambda e: e.tensor_tensor(out=out.t[:, :], in0=a.t[:, :], in1=b.t[:, :], op=op), reads=[a, b], writes=[out])
    k.op("act", lambda e: e.activation(out=dt.t[:, :], in_=ldt.t[:, :], func=AF.Exp), reads=[ldt], writes=[dt])
    tt(mag, dt, ar, ALU.mult)
    k.op("act", lambda e: e.activation(out=mag.t[:, :], in_=mag.t[:, :], func=AF.Exp), reads=[mag], writes=[mag])
    tt(ang, dt, ai, ALU.mult)

    def sin_of(out, src, shift):
        k.op("dve", lambda e: e.tensor_scalar(out=tmp.t[:, :], in0=src.t[:, :], scalar1=float(shift), scalar2=None, op0=ALU.add), reads=[src], writes=[tmp])
        k.op("dve", lambda e: e.memset(kk.t[:, :], 0.0), writes=[kk])
        for m_ in range(1, 9):
            k.op("dve", lambda e, m_=m_: e.tensor_scalar(out=t1.t[:, :], in0=tmp.t[:, :], scalar1=float((2 * m_ - 1) * math.pi), scalar2=TWO_PI,
                                                          op0=ALU.is_ge, op1=ALU.mult), reads=[tmp], writes=[t1])
            tt(kk, kk, t1, ALU.add)
        tt(tmp, tmp, kk, ALU.subtract)
        k.op("dve", lambda e: e.tensor_scalar(out=tmp.t[:, :], in0=tmp.t[:, :], scalar1=-math.pi, scalar2=math.pi, op0=ALU.max, op1=ALU.min), reads=[tmp], writes=[tmp])
        k.op("act", lambda e: e.activation(out=out.t[:, :], in_=tmp.t[:, :], func=AF.Sin), reads=[tmp], writes=[out])
    sin_of(sn_, ang, 0.0)
    sin_of(cs_, ang, math.pi / 2)
    tt(abr, mag, cs_, ALU.mult)
    tt(abi, mag, sn_, ALU.mult)
    tt(t1, ar, ar, ALU.mult)
    tt(t2, ai, ai, ALU.mult)
    tt(den, t1, t2, ALU.add)
    k.op("dve", lambda e: e.reciprocal(out=den.t[:, :], in_=den.t[:, :]), reads=[den], writes=[den])
    k.op("dve", lambda e: e.tensor_scalar(out=tmp.t[:, :], in0=abr.t[:, :], scalar1=-1.0, scalar2=None, op0=ALU.add), reads=[abr], writes=[tmp])
    tt(t1, tmp, ar, ALU.mult)
    tt(t2, abi, ai, ALU.mult)
    tt(fr, t1, t2, ALU.add)
    tt(fr, fr, den, ALU.mult)
    tt(t1, abi, ar, ALU.mult)
    tt(t2, tmp, ai, ALU.mult)
    tt(fi, t1, t2, ALU.subtract)
    tt(fi, fi, den, ALU.mult)
    k.dma("sp", prm["abar_s"].t[:, 0, :], abr.t[:, :], abr, prm["abar_s"])
    k.dma("sp", prm["abar_s"].t[:, 1, :], abi.t[:, :], abi, prm["abar_s"])
    bbr = k.sbuf("bbr", [128, GP, 16], F32)
    bbi = k.sbuf("bbi", [128, GP, 16], F32)
    u1 = k.sbuf("u1", [128, GP], F32)
    u2 = k.sbuf("u2", [128, GP], F32)
    for i in range(16):
        for (dst, a, fa, b_, fb, op) in ((bbr, br, fr, bi, fi, ALU.subtract), (bbi, bi, fr, br, fi, ALU.add)):
            k.op("dve", lambda e, a=a, fa=fa, i=i: e.tensor_tensor(out=u1.t[:, :], in0=a.t[:, :, i], in1=fa.t[:, :], op=ALU.mult), reads=[a, fa], writes=[u1])
            k.op("dve", lambda e, b_=b_, fb=fb, i=i: e.tensor_tensor(out=u2.t[:, :], in0=b_.t[:, :, i], in1=fb.t[:, :], op=ALU.mult), reads=[b_, fb], writes=[u2])
            k.op("dve", lambda e, dst=dst, i=i, op=op: e.tensor_tensor(out=dst.t[:, :, i], in0=u1.t[:, :], in1=u2.t[:, :], op=op), reads=[u1, u2], writes=[dst])
    MB = [k.sbuf("MB%d" % r, [128, GP, 2, 16], BF16) for r in range(2)]
    LCs = [k.sbuf("LC%d" % r, [128, GP, 2, 16], BF16) for r in range(2)]
    nm01 = k.sbuf("nm01", [128, 2], F32)
    k.op("dve", lambda e: e.tensor_scalar(out=nm01.t[:, :], in0=m01.t[:, :], scalar1=-1.0, scalar2=None, op0=ALU.mult), reads=[m01], writes=[nm01])
    for r, (bsrc, csrc, msk) in enumerate(((bbr, cr, m01), (bbi, ci, nm01))):
        for g2 in range(2):
            k.op("dve", lambda e, r=r, g2=g2, bsrc=bsrc: e.tensor_scalar(out=MB[r].t[:, :, g2, :], in0=bsrc.t[:, :, :], scalar1=m01.t[:, g2:g2 + 1], scalar2=None, op0=ALU.mult),
                 reads=[bsrc, m01], writes=[MB[r]])
            k.op("dve", lambda e, r=r, g2=g2, csrc=csrc, msk=msk: e.tensor_scalar(out=LCs[r].t[:, :, g2, :], in0=csrc.t[:, :, :], scalar1=msk.t[:, g2:g2 + 1], scalar2=None, op0=ALU.mult),
                 reads=[csrc, msk], writes=[LCs[r]])
        k.dma("sp", prm["lc_s"].t[r], LCs[r].t[:, :, :, :], LCs[r], prm["lc_s"])
    ptr = k.psum("ptr", [128, 1024], BF16)
    lbo = Rot([k.sbuf("lbo%d" % i, [32, 8, 128], BF16) for i in range(2)])
    for r in range(2):
        for g0 in range(0, GP, 8):
            ng = min(8, GP - g0)

            def tr(e, r=r, g0=g0, ng=ng):
                ins = None
                for j in range(ng):
                    ins = e.transpose(ptr.t[0:32, j * 128:(j + 1) * 128], MB[r].t[:, g0 + j, :, :].rearrange("p a b -> p (a b)"), ident.t[:, :])
                return ins
            k.op("pe", tr, reads=[MB[r], ident], writes=[ptr])
            o = lbo.next()
            k.op("act", lambda e, o=o, ng=ng: e.activation(out=o.t[:, 0:ng, :], in_=ptr.t[0:32, 0:ng * 128].rearrange("p (a b) -> p a b", b=128), func=AF.Copy),
                 reads=[ptr], writes=[o])
            k.dma("sp", prm["lb_s"].t[r, :, g0:g0 + ng, :], o.t[:, 0:ng, :], o, prm["lb_s"])
    k.end_stage()
    TBs = 16
    k.begin_stage()
    LB = [k.sbuf("LB%d" % r, [32, GP, 128], BF16) for r in range(2)]
    LC = [k.sbuf("LCm%d" % r, [128, GP, 32], BF16) for r in range(2)]
    for r in range(2):
        k.dma("sp", LB[r].t[:, :, :], prm["lb_s"].t[r], prm["lb_s"], LB[r])
        k.dma("sp", LC[r].t[:, :, :], prm["lc_s"].t[r].rearrange("p g a b -> p g (a b)"), prm["lc_s"], LC[r])
    AS = k.sbuf("AS", [128, 2, GP], F32)
    AC = k.sbuf("AC", [128, 2, GP], F32)
    k.dma("sp", AS.t[:, 0, :], prm["abar_s"].t[:, 0, :], prm["abar_s"], AS)
    k.dma("sp", AS.t[:, 1, :], prm["abar_s"].t[:, 0, :], prm["abar_s"], AS)
    k.dma("sp", AC.t[:, 1, :], prm["abar_s"].t[:, 1, :], prm["abar_s"], AC)
    k.dma("sp", AC.t[:, 0, :], prm["abar_s"].t[:, 1, :], prm["abar_s"], AC)
    k.op("dve", lambda e: e.tensor_scalar(out=AC.t[:, 0, :], in0=AC.t[:, 0, :], scalar1=-1.0, scalar2=None, op0=ALU.mult), reads=[AC], writes=[AC])
    gpool = GP // 4 if GP >= 8 else 0
    gd = (GP - gpool) // 2
    bounds = [(0, gd), (gd, GP - gpool)] + ([(GP - gpool, GP)] if gpool else [])
    pcon = k.psum("pcon", [128, 512])
    ptmp = k.psum("ptmp", [128, 512])
    ASp = pcon.t[:, 0:2 * GP].rearrange("p (a g) -> p a g", a=2)
    ACp = pcon.t[:, 2 * GP:4 * GP].rearrange("p (a g) -> p a g", a=2)
    k.op("dve", lambda e: e.tensor_copy(ASp, AS.t[:, :, :]), reads=[AS], writes=[pcon])
    k.op("dve", lambda e: e.tensor_copy(ACp, AC.t[:, :, :]), reads=[AC, pcon], writes=[pcon])
    chains = []
    for ci, (g0, g1) in enumerate(bounds):
        ng_ = g1 - g0
        ch = dict(g0=g0, g1=g1, eng="dve" if ci < 2 else "pool")
        ch["X"] = k.sbuf("X%d" % ci, [128, 2, ng_, TBs + 1], F32)
        ch["XB"] = k.sbuf("XB%d" % ci, [128, 2, ng_, TBs], BF16)
        ch["Q"] = k.sbuf("Q%d" % ci, [128, 2, ng_], F32)
        if ci < 2:
            ch["P"] = Buf(k, "Pp%d" % ci, None)
            ch["R"] = Buf(k, "Rp%d" % ci, None)
            ch["P"].ap = ptmp.t[:, (2 * ci) * 2 * gd:(2 * ci) * 2 * gd + 2 * ng_].rearrange("p (a g) -> p a g", a=2)
            ch["R"].ap = ptmp.t[:, (2 * ci + 1) * 2 * gd:(2 * ci + 1) * 2 * gd + 2 * ng_].rearrange("p (a g) -> p a g", a=2)
            ch["AS"], ch["AC"], ch["cb"] = ASp, ACp, pcon
        else:
            ch["P"] = k.sbuf("Pq", [128, 2, ng_], F32)
            ch["R"] = k.sbuf("Rq", [128, 2, ng_], F32)
            ch["P"].ap = ch["P"].t[:, :, :]
            ch["R"].ap = ch["R"].t[:, :, :]
            ch["AS"], ch["AC"], ch["cb"] = AS.t, AC.t, AS
        k.op(ch["eng"], lambda e, ch=ch: e.memset(ch["X"].t[:, :, :, :], 0.0), writes=[ch["X"]])
        chains.append(ch)
    if gpool:
        k._record((k.esem["dve"], k.ecnt["dve"]), [AC], [])

    def xb_of(gp):
        for ch in chains:
            if ch["g0"] <= gp < ch["g1"]:
                return ch["XB"], gp - ch["g0"]
    BU = k.sbuf("BU", [128, 2, GP, TBs], F32)
    x32 = Rot([k.sbuf("x32_%d" % i, [32, GP, TBs], BF16) for i in range(2)])
    y32 = Rot([k.sbuf("y32_%d" % i, [32, GP, TBs], F32) for i in range(2)])
    psb = Rot([k.psum("psb%d" % i, [128, 512]) for i in range(4)])
    psy = Rot([k.psum("psy%d" % i, [128, 512]) for i in range(2)])
    per = 512 // TBs
    for (t0, n) in token_blocks(T, TBs):
        xi = x32.next()
        k.dma("sp", xi.t[:, :, 0:n], hnTb.t[:, t0:t0 + n].rearrange("(g r) t -> r g t", r=32), hnTb, xi)
        for r in range(2):
            for g0 in range(0, GP, per):
                ng = min(per, GP - g0)
                ps = psb.next()

                def mm(e, r=r, g0=g0, ng=ng, ps=ps):
                    ins = None
                    for j in range(ng):
                        ins = e.matmul(ps.t[:, j * TBs:j * TBs + n], lhsT=LB[r].t[:, g0 + j, :], rhs=xi.t[:, g0 + j, 0:n], start=True, stop=True)
                    return ins
                k.op("pe", mm, reads=[LB[r], xi], writes=[ps])
                k.op("act", lambda e, r=r, g0=g0, ng=ng, ps=ps: e.activation(
                    out=BU.t[:, r, g0:g0 + ng, 0:n], in_=ps.t[:, 0:ng * TBs].rearrange("p (g t) -> p g t", t=TBs)[:, :, 0:n], func=AF.Copy),
                    reads=[ps], writes=[BU])
        dch = [c_ for c_ in chains if c_["eng"] == "dve"]
        pch = [c_ for c_ in chains if c_["eng"] == "pool"]
        for s in range(n):
            for grp in (dch, pch):
                for ch in grp:
                    g = slice(ch["g0"], ch["g1"])
                    k.op(ch["eng"], lambda e, s=s, g=g, ch=ch: e.tensor_tensor(out=ch["P"].ap, in0=ch["X"].t[:, :, :, s], in1=ch["AS"][:, :, g], op=ALU.mult),
                         reads=[ch["X"], ch["cb"]], writes=[ch["P"]])
                    for r in range(2):
                        k.op(ch["eng"], lambda e, s=s, r=r, g=g, ch=ch: e.tensor_tensor(out=ch["Q"].t[:, r, :], in0=ch["X"].t[:, 1 - r, :, s], in1=ch["AC"][:, r, g], op=ALU.mult),
                             reads=[ch["X"], ch["cb"], AC], writes=[ch["Q"]])
                for ch in grp:
                    k.op(ch["eng"], lambda e, ch=ch: e.tensor_tensor(out=ch["R"].ap, in0=ch["P"].ap, in1=ch["Q"].t[:, :, :], op=ALU.add),
                         reads=[ch["P"], ch["Q"]], writes=[ch["R"]])
                for ch in grp:
                    g = slice(ch["g0"], ch["g1"])
                    k.op(ch["eng"], lambda e, s=s, g=g, ch=ch: e.tensor_tensor(out=ch["X"].t[:, :, :, s + 1], in0=ch["R"].ap, in1=BU.t[:, :, g, s], op=ALU.add),
                         reads=[ch["R"], BU], writes=[ch["X"]])
        for ch in chains:
            k.op(ch["eng"], lambda e, ch=ch: e.tensor_copy(ch["XB"].t[:, :, :, 0:n], ch["X"].t[:, :, :, 1:n + 1]), reads=[ch["X"]], writes=[ch["XB"]])
            k.op(ch["eng"], lambda e, ch=ch: e.tensor_copy(ch["X"].t[:, :, :, 0], ch["X"].t[:, :, :, n]), reads=[ch["X"]], writes=[ch["X"]])
        yo = y32.next()
        for g0 in range(0, GP, per):
            ng = min(per, GP - g0)
            ps = psy.next()

            def mm(e, g0=g0, ng=ng, ps=ps):
                ins = None
                for j in range(ng):
                    xb, lj = xb_of(g0 + j)
                    e.matmul(ps.t[0:32, j * TBs:j * TBs + n], lhsT=LC[0].t[:, g0 + j, :], rhs=xb.t[:, 0, lj, 0:n], start=True, stop=False)
                    ins = e.matmul(ps.t[0:32, j * TBs:j * TBs + n], lhsT=LC[1].t[:, g0 + j, :], rhs=xb.t[:, 1, lj, 0:n], start=False, stop=True)
                return ins
            k.op("pe", mm, reads=[LC[0], LC[1]] + [c_["XB"] for c_ in chains], writes=[ps])
            k.op("act", lambda e, g0=g0, ng=ng, ps=ps: e.activation(
                out=yo.t[:, g0:g0 + ng, 0:n], in_=ps.t[0:32, 0:ng * TBs].rearrange("p (g t) -> p g t", t=TBs)[:, :, 0:n], func=AF.Copy),
                reads=[ps], writes=[yo])
        k.dma("act", yT.t[:, t0:t0 + n].rearrange("(g r) t -> r g t", r=32), yo.t[:, :, 0:n], yo, yT)
    k.end_stage()
    k.begin_stage()
    dcol = k.sbuf("dcol", [128, KC], F32)
    k.dma("sp", dcol.t[:, :], prm["dcol"].t[:, :], prm["dcol"], dcol)
    yb = Rot([k.sbuf("yb%d" % i, [128, TB], F32) for i in range(2)])
    hb = Rot([k.sbuf("hb%d" % i, [128, TB], F32) for i in range(2)])
    v1 = k.sbuf("v1", [128, TB], F32)
    v2 = k.sbuf("v2", [128, TB], F32)
    zb = Rot([k.sbuf("zb%d" % i, [128, TB], BF16) for i in range(2)])
    C0 = math.sqrt(2.0 / math.pi)
    for (t0, n) in token_blocks(T, TB):
        for c in range(KC):
            y_, h_ = yb.next(), hb.next()
            k.dma("sp", y_.t[:, 0:n], yT.t[c * 128:(c + 1) * 128, t0:t0 + n], yT, y_)
            k.dma("sp", h_.t[:, 0:n], hnT.t[c * 128:(c + 1) * 128, t0:t0 + n], hnT, h_)
            k.op("dve", lambda e: e.scalar_tensor_tensor(out=v1.t[:, 0:n], in0=h_.t[:, 0:n], scalar=dcol.t[:, c:c + 1], in1=y_.t[:, 0:n],
                                                          op0=ALU.mult, op1=ALU.add), reads=[h_, dcol, y_], writes=[v1])
            k.op("act", lambda e: e.activation(out=v2.t[:, 0:n], in_=v1.t[:, 0:n], func=AF.Square), reads=[v1], writes=[v2])
            k.op("dve", lambda e: e.tensor_scalar(out=v2.t[:, 0:n], in0=v2.t[:, 0:n], scalar1=0.044715, scalar2=1.0, op0=ALU.mult, op1=ALU.add), reads=[v2], writes=[v2])
            k.op("dve", lambda e: e.tensor_tensor(out=v2.t[:, 0:n], in0=v2.t[:, 0:n], in1=v1.t[:, 0:n], op=ALU.mult), reads=[v2, v1], writes=[v2])
            k.op("act", lambda e: e.activation(out=v2.t[:, 0:n], in_=v2.t[:, 0:n], func=AF.Sigmoid, scale=2.0 * C0), reads=[v2], writes=[v2])
            z_ = zb.next()
            k.op("dve", lambda e: e.tensor_tensor(out=z_.t[:, 0:n], in0=v2.t[:, 0:n], in1=v1.t[:, 0:n], op=ALU.mult), reads=[v2, v1], writes=[z_])
            k.dma("pool", zT.t[c * 128:(c + 1) * 128, t0:t0 + n], z_.t[:, 0:n], z_, zT)
    k.end_stage()
```
